# Optimizing a Trainium2 kernel written in Bass

```python
import math
import jax
import jax.numpy as jnp
from jax import lax
import numpy as np

D_MODEL = 1024
BATCH = 4
SEQ = 4096
DEPTH = 2
DEC_BATCH = 32
DEC_SEQ = 1
PAST_LEN = 8192
PAGE_SIZE = 128

W_A = D_MODEL // 2
G_A = 4
CG_A = W_A // G_A
CHUNK = 128
N_HEADS_B = 4
HEAD_DIM_B = D_MODEL // 16
W_B = N_HEADS_B * 2 * HEAD_DIM_B
Q_BLOCK = 128
W_C = D_MODEL // 2
POOL_WINDOWS = (2, 4, 8, 16)
G_C = len(POOL_WINDOWS)
CG_C = W_C // G_C
MAX_WIN = max(POOL_WINDOWS)
N_BRANCH = 3
N_IN = 2 * W_A + 3 * W_B + W_C + N_BRANCH * D_MODEL
SPLITS = (2 * W_A, 2 * W_A + W_B, 2 * W_A + 2 * W_B, 2 * W_A + 3 * W_B, 2 * W_A + 3 * W_B + W_C)
D_FF = ((8 * D_MODEL // 3 + 127) // 128) * 128
CONV_W = 3
EPS = 1e-6
NEG_INF = -1e30

kernel_name = 'hybrid_gmlp_diffattn_pool_convffn_step'


def rmsnorm(x, g):
    xf = x.astype(jnp.float32)
    xf = xf * lax.rsqrt(jnp.mean(xf * xf, axis=-1, keepdims=True) + EPS)
    return xf.astype(x.dtype) * g


def chunk_spatial_gating(u, vn, ws, bs):
    b, L, _ = u.shape
    nc = -(-L // CHUNK)
    pad = nc * CHUNK - L
    vp = jnp.pad(vn, ((0, 0), (0, pad), (0, 0))).reshape(b, nc, CHUNK, G_A, CG_A)
    causal = jnp.tril(jnp.ones((CHUNK, CHUNK), dtype=bool))
    wm = jnp.where(causal[None], ws, jnp.zeros((), ws.dtype))
    s = jnp.einsum('gts,bnsgc->bntgc', wm, vp) + bs.T[None, None, :, :, None]
    s = s.reshape(b, nc * CHUNK, W_A)[:, :L]
    return u * s


def diff_attention(q, k, v, q_offset, lam):
    b, Lq = q.shape[0], q.shape[1]
    Lk = k.shape[1]
    blk = min(Q_BLOCK, Lq)
    nb = -(-Lq // blk)
    pad = nb * blk - Lq
    qp = jnp.pad(q, ((0, 0), (0, pad), (0, 0), (0, 0), (0, 0)))
    qb = jnp.moveaxis(qp.reshape(b, nb, blk, N_HEADS_B, 2, HEAD_DIM_B), 1, 0)
    k_pos = jnp.arange(Lk)
    scale = HEAD_DIM_B ** -0.5

    def one_block(args):
        qi, bi = args
        s = jnp.einsum('bqhcd,bkhcd->bhcqk', qi, k).astype(jnp.float32) * scale
        q_pos = q_offset + bi * blk + jnp.arange(blk)
        mask = k_pos[None, :] <= q_pos[:, None]
        s = jnp.where(mask, s, NEG_INF)
        pr = jax.nn.softmax(s, axis=-1)
        w = pr[:, :, 0] - lam * pr[:, :, 1]
        return jnp.einsum('bhqk,bkhe->bqhe', w.astype(v.dtype), v)

    out = lax.map(one_block, (qb, jnp.arange(nb)))
    out = jnp.moveaxis(out, 0, 1).reshape(b, nb * blk, N_HEADS_B, 2 * HEAD_DIM_B)
    return out[:, :Lq]


def multiscale_pool(c_ext, start_pos):
    P = MAX_WIN - 1
    L = c_ext.shape[1] - P
    cs = jnp.cumsum(c_ext.astype(jnp.float32), axis=1)
    cs = jnp.pad(cs, ((0, 0), (1, 0), (0, 0)))
    pos = start_pos + jnp.arange(L)
    means = []
    for g, win in enumerate(POOL_WINDOWS):
        sl = slice(g * CG_C, (g + 1) * CG_C)
        wsum = cs[:, P + 1:P + 1 + L, sl] - cs[:, P + 1 - win:P + 1 - win + L, sl]
        cnt = jnp.minimum(pos + 1, win).astype(jnp.float32)
        means.append(wsum / cnt[None, :, None])
    mean = jnp.concatenate(means, axis=-1).astype(c_ext.dtype)
    return mean - c_ext[:, P:]


def conv_ffn(h, prefix, f_up, f_conv_w, f_conv_b, f_down):
    up = h @ f_up
    L = up.shape[1]
    ext = jnp.concatenate([prefix.astype(up.dtype), up], axis=1)
    conv = f_conv_b + f_conv_w[0] * ext[:, 0:L]
    for j in range(1, CONV_W):
        conv = conv + f_conv_w[j] * ext[:, j:j + L]
    gate, val = jnp.split(conv, 2, axis=-1)
    return (jax.nn.silu(gate) * val) @ f_down, ext[:, L:]


def hybrid_layer(x, p, lam_init, k_past, v_past, pool_prefix, conv_prefix, start_pos):
    b, L, _ = x.shape
    h = rmsnorm(x, p['norm1_g'])
    z = h @ p['w_in']
    za, zq, zk, zv, zc, zg = jnp.split(z, SPLITS, axis=-1)
    u, va = jnp.split(jax.nn.gelu(za), 2, axis=-1)
    va = rmsnorm(va, p['a_vnorm_g'])
    out_a = chunk_spatial_gating(u, va, p['a_ws'], p['a_bs'])
    q = rmsnorm(zq.reshape(b, L, N_HEADS_B, 2, HEAD_DIM_B), p['b_qnorm_g'])
    k = rmsnorm(zk.reshape(b, L, N_HEADS_B, 2, HEAD_DIM_B), p['b_knorm_g']).reshape(b, L, N_HEADS_B, 2 * HEAD_DIM_B)
    v = zv.reshape(b, L, N_HEADS_B, 2 * HEAD_DIM_B)
    if k_past is None:
        k_all, v_all = k, v
    else:
        k_all = jnp.concatenate([k_past.astype(k.dtype), k], axis=1)
        v_all = jnp.concatenate([v_past.astype(v.dtype), v], axis=1)
    f32 = jnp.float32
    lam = (jnp.exp(jnp.sum(p['b_lq1'].astype(f32) * p['b_lk1'].astype(f32)))
           - jnp.exp(jnp.sum(p['b_lq2'].astype(f32) * p['b_lk2'].astype(f32))) + lam_init)
    o = diff_attention(q, k_all.reshape(b, -1, N_HEADS_B, 2, HEAD_DIM_B), v_all, start_pos, lam)
    out_b = (rmsnorm(o, p['b_subln_g']) * (1.0 - lam_init)).reshape(b, L, W_B)
    c_ext = jnp.concatenate([pool_prefix.astype(zc.dtype), zc], axis=1)
    pooled = multiscale_pool(c_ext, start_pos)
    out_c = jnp.einsum('blgc,gcd->blgd', pooled.reshape(b, L, G_C, CG_C), p['c_w']).reshape(b, L, W_C) * p['c_scale']
    g_a, g_b, g_c = jnp.split(jax.nn.sigmoid(zg), N_BRANCH, axis=-1)
    merged = g_a * (out_a @ p['p_a']) + g_b * (out_b @ p['p_b']) + g_c * (out_c @ p['p_c'])
    x = x + merged @ p['w_o']
    f, conv_tail = conv_ffn(rmsnorm(x, p['norm2_g']), conv_prefix, p['f_up'], p['f_conv_w'], p['f_conv_b'], p['f_down'])
    x = x + f
    return x, k, v, va, c_ext[:, L:], conv_tail


def setup_inputs(seed: int = 0) -> dict:
    key = jax.random.key(seed)
    ks = jax.random.split(key, 32)
    f32 = jnp.float32
    n_pages = PAST_LEN // PAGE_SIZE
    n_used = DEC_BATCH * n_pages
    n_pool = n_used + max(1, n_used // 4)

    def nrm(k, shape, scale=1.0):
        return jax.random.normal(k, shape, f32) * scale

    def gain(k, shape):
        return 1.0 + 0.02 * jax.random.normal(k, shape, f32)

    page_table = jax.random.permutation(ks[0], n_pool)[:n_used].reshape(DEC_BATCH, n_pages).astype(jnp.int32)
    kv_shape = (DEPTH, n_pool, PAGE_SIZE, N_HEADS_B, 2 * HEAD_DIM_B)
    return {
        'x_prompt': nrm(ks[1], (BATCH, SEQ, D_MODEL)),
        'x_sample': nrm(ks[2], (DEC_BATCH, DEC_SEQ, D_MODEL)),
        'cache_k': nrm(ks[3], kv_shape),
        'cache_v': nrm(ks[4], kv_shape),
        'page_table': page_table,
        'state_pool': nrm(ks[5], (DEPTH, DEC_BATCH, MAX_WIN - 1, W_C)),
        'state_conv': nrm(ks[6], (DEPTH, DEC_BATCH, CONV_W - 1, 2 * D_FF)),
        'norm1_g': gain(ks[7], (DEPTH, D_MODEL)),
        'w_in': nrm(ks[8], (DEPTH, D_MODEL, N_IN), D_MODEL ** -0.5),
        'a_vnorm_g': gain(ks[9], (DEPTH, W_A)),
        'a_ws': nrm(ks[10], (DEPTH, G_A, CHUNK, CHUNK), CHUNK ** -0.5),
        'a_bs': 1.0 + nrm(ks[11], (DEPTH, G_A, CHUNK), 0.1),
        'b_qnorm_g': gain(ks[12], (DEPTH, HEAD_DIM_B)),
        'b_knorm_g': gain(ks[13], (DEPTH, HEAD_DIM_B)),
        'b_lq1': nrm(ks[14], (DEPTH, HEAD_DIM_B), 0.1),
        'b_lk1': nrm(ks[15], (DEPTH, HEAD_DIM_B), 0.1),
        'b_lq2': nrm(ks[16], (DEPTH, HEAD_DIM_B), 0.1),
        'b_lk2': nrm(ks[17], (DEPTH, HEAD_DIM_B), 0.1),
        'b_subln_g': gain(ks[18], (DEPTH, 2 * HEAD_DIM_B)),
        'c_w': nrm(ks[19], (DEPTH, G_C, CG_C, CG_C), CG_C ** -0.5),
        'c_scale': gain(ks[20], (DEPTH, W_C)),
        'p_a': nrm(ks[21], (DEPTH, W_A, D_MODEL), W_A ** -0.5),
        'p_b': nrm(ks[22], (DEPTH, W_B, D_MODEL), W_B ** -0.5),
        'p_c': nrm(ks[23], (DEPTH, W_C, D_MODEL), W_C ** -0.5),
        'w_o': nrm(ks[24], (DEPTH, D_MODEL, D_MODEL), D_MODEL ** -0.5),
        'norm2_g': gain(ks[25], (DEPTH, D_MODEL)),
        'f_up': nrm(ks[26], (DEPTH, D_MODEL, 2 * D_FF), D_MODEL ** -0.5),
        'f_conv_w': nrm(ks[27], (DEPTH, CONV_W, 2 * D_FF), CONV_W ** -0.5),
        'f_conv_b': nrm(ks[28], (DEPTH, 2 * D_FF), 0.01),
        'f_down': nrm(ks[29], (DEPTH, D_FF, D_MODEL), D_FF ** -0.5),
    }


def reference(x_prompt, x_sample, cache_k, cache_v, page_table, state_pool, state_conv,
              norm1_g, w_in, a_vnorm_g, a_ws, a_bs, b_qnorm_g, b_knorm_g, b_lq1, b_lk1, b_lq2, b_lk2,
              b_subln_g, c_w, c_scale, p_a, p_b, p_c, w_o, norm2_g, f_up, f_conv_w, f_conv_b, f_down):
    n_dec, n_pages = page_table.shape
    past_len = n_pages * cache_k.shape[2]
    b_p = x_prompt.shape[0]
    pool_zero = jnp.zeros((b_p, MAX_WIN - 1, W_C), x_prompt.dtype)
    conv_zero = jnp.zeros((b_p, CONV_W - 1, 2 * D_FF), x_prompt.dtype)
    y_p, y_s = x_prompt, x_sample
    kp_l, vp_l, ks_l, vs_l, cvs_l, pp_l, ps_l, cp_l, cs_l = [], [], [], [], [], [], [], [], []
    for l in range(DEPTH):
        p = {
            'norm1_g': norm1_g[l], 'w_in': w_in[l], 'a_vnorm_g': a_vnorm_g[l], 'a_ws': a_ws[l], 'a_bs': a_bs[l],
            'b_qnorm_g': b_qnorm_g[l], 'b_knorm_g': b_knorm_g[l], 'b_lq1': b_lq1[l], 'b_lk1': b_lk1[l],
            'b_lq2': b_lq2[l], 'b_lk2': b_lk2[l], 'b_subln_g': b_subln_g[l], 'c_w': c_w[l], 'c_scale': c_scale[l],
            'p_a': p_a[l], 'p_b': p_b[l], 'p_c': p_c[l], 'w_o': w_o[l], 'norm2_g': norm2_g[l],
            'f_up': f_up[l], 'f_conv_w': f_conv_w[l], 'f_conv_b': f_conv_b[l], 'f_down': f_down[l],
        }
        lam_init = 0.8 - 0.6 * math.exp(-0.3 * l)
        y_p, k_p, v_p, _, pool_p, conv_p = hybrid_layer(y_p, p, lam_init, None, None, pool_zero, conv_zero, 0)
        k_past = cache_k[l][page_table].reshape(n_dec, past_len, N_HEADS_B, 2 * HEAD_DIM_B)
        v_past = cache_v[l][page_table].reshape(n_dec, past_len, N_HEADS_B, 2 * HEAD_DIM_B)
        y_s, k_s, v_s, cv_s, pool_s, conv_s = hybrid_layer(y_s, p, lam_init, k_past, v_past,
                                                            state_pool[l], state_conv[l], past_len)
        kp_l.append(k_p); vp_l.append(v_p); ks_l.append(k_s); vs_l.append(v_s); cvs_l.append(cv_s)
        pp_l.append(pool_p); ps_l.append(pool_s); cp_l.append(conv_p); cs_l.append(conv_s)
    return (y_p, y_s, jnp.stack(kp_l), jnp.stack(vp_l), jnp.stack(ks_l), jnp.stack(vs_l), jnp.stack(cvs_l),
            jnp.stack(pp_l), jnp.stack(ps_l), jnp.stack(cp_l), jnp.stack(cs_l))
```

```python
import math
from contextlib import ExitStack

import numpy as np
import concourse.bass as bass
import concourse.mybir as mybir
from concourse.bass_utils import run_bass_kernel_spmd

F32 = mybir.dt.float32
BF16 = mybir.dt.bfloat16
I32 = mybir.dt.int32
ALU = mybir.AluOpType
AF = mybir.ActivationFunctionType
AX = mybir.AxisListType

D = 1024
WA = 512
HD = 64
NH = 4
WB = 512
WC = 512
NIN = 6144
DFF = 2816
DEPTH = 2
EPS = 1e-6
PAGE = 128
C_U, C_VA, C_Q, C_K, C_V, C_C, C_G = 0, 512, 1024, 1536, 2048, 2560, 3072

ENGS = ("pe", "act", "dve", "pool", "sp")
import os
KSTOP = int(os.environ.get("KSTOP", "100000000"))


class Prog:
    def __init__(self, nc):
        self.nc = nc
        self.ops = {e: [] for e in ENGS}
        self.cnt = {e: 0 for e in ENGS}
        self.last_w = {}
        self.readers = {}
        self.seen = {e: {} for e in ENGS}
        self.dma_cnt = {}

    def op(self, eng, fn, reads=(), writes=(), dma=None, inc=True):
        self.nrec = getattr(self, "nrec", 0) + 1
        if self.nrec > KSTOP:
            return None
        waits = {}

        def need(tok):
            if tok is None:
                return
            k, v = tok
            if k == "pe" and eng == "pe" and dma is None:
                return
            if v > waits.get(k, 0):
                waits[k] = v

        for r in reads:
            need(self.last_w.get(r))
        for w in writes:
            need(self.last_w.get(w))
            for t in self.readers.get(w, ()):
                need(t)
        wl = []
        for k, v in waits.items():
            if self.seen[eng].get(k, 0) >= v:
                continue
            self.seen[eng][k] = v
            wl.append((k, v))
        if dma is not None:
            self.dma_cnt[dma] = self.dma_cnt.get(dma, 0) + 16
            tok = (dma, self.dma_cnt[dma])
            do_inc = True
        elif inc:
            self.cnt[eng] += 1
            tok = (eng, self.cnt[eng])
            do_inc = True
        else:
            tok = (eng, self.cnt[eng] + 1)
            do_inc = False
        self.ops[eng].append((wl, fn, tok, do_inc, dma is not None))
        for r in reads:
            self.readers.setdefault(r, []).append(tok)
        for w in writes:
            self.last_w[w] = tok
            self.readers[w] = []
        return tok

    def emit(self):
        nc = self.nc
        keys = list(ENGS) + sorted(self.dma_cnt.keys())
        with ExitStack() as st:
            sems = {k: st.enter_context(nc.semaphore("s_" + k)) for k in keys}
            block = st.enter_context(nc.Block())
            names = {"pe": "tensor", "act": "scalar", "dve": "vector", "pool": "gpsimd", "sp": "sync"}
            for eng in ENGS:
                def run(e, eng=eng):
                    for wl, fn, tok, do_inc, is_dma in self.ops[eng]:
                        for k, v in wl:
                            e.wait_ge(sems[k], v)
                        ins = fn(e)
                        if do_inc:
                            ins.then_inc(sems[tok[0]], 16 if is_dma else 1)
                    if eng == "sp":
                        for k in keys:
                            tot = self.cnt[k] if k in self.cnt else self.dma_cnt[k]
                            if tot > 0:
                                e.wait_ge(sems[k], tot)
                getattr(block, names[eng])(run)


class Cfg:
    def __init__(self, NB, SEQ, NS, NPOOL, NPG, PGRP):
        self.NB, self.SEQ, self.NS, self.NPOOL, self.NPG, self.PGRP = NB, SEQ, NS, NPOOL, NPG, PGRP


def build(cfg):
    NB, SEQ, NS, NPOOL, NPG, PGRP = cfg.NB, cfg.SEQ, cfg.NS, cfg.NPOOL, cfg.NPG, cfg.PGRP
    T = 512
    NT = SEQ // T
    NG = SEQ // 128
    nc = bass.Bass("TRN2", target_bir_lowering=False)
    P = Prog(nc)

    def din(name, shape, dt=F32):
        return nc.dram_tensor(name, list(shape), dt, kind="ExternalInput").ap()

    def dout(name, shape):
        return nc.dram_tensor(name, list(shape), F32, kind="ExternalOutput").ap()

    x_prompt = din("x_prompt", [NB, SEQ, D])
    norm1_g = din("norm1_g", [DEPTH, D])
    w_in = din("w_in", [DEPTH, D, NIN])
    a_vnorm_g = din("a_vnorm_g", [DEPTH, WA])
    a_ws = din("a_ws", [DEPTH, 4, 128, 128])
    a_bs = din("a_bs", [DEPTH, 4, 128])
    b_qnorm_g = din("b_qnorm_g", [DEPTH, HD])
    b_knorm_g = din("b_knorm_g", [DEPTH, HD])
    b_lq1 = din("b_lq1", [DEPTH, HD])
    b_lk1 = din("b_lk1", [DEPTH, HD])
    b_lq2 = din("b_lq2", [DEPTH, HD])
    b_lk2 = din("b_lk2", [DEPTH, HD])
    b_subln_g = din("b_subln_g", [DEPTH, 128])
    c_w = din("c_w", [DEPTH, 4, 128, 128])
    c_scale = din("c_scale", [DEPTH, WC])
    p_abc = [din("p_a", [DEPTH, 512, D]), din("p_b", [DEPTH, 512, D]), din("p_c", [DEPTH, 512, D])]
    w_o = din("w_o", [DEPTH, D, D])
    norm2_g = din("norm2_g", [DEPTH, D])
    f_up = din("f_up", [DEPTH, D, 2 * DFF])
    f_conv_w = din("f_conv_w", [DEPTH, 3, 2 * DFF])
    f_conv_b = din("f_conv_b", [DEPTH, 2 * DFF])
    f_down = din("f_down", [DEPTH, DFF, D])
    consts = din("consts", [128, 4, 128])
    if NS:
        x_sample = din("x_sample", [NS, D])
        cache_kv = [din(f"cache_kv{i}", [NPOOL * 64, 2048]) for i in range(DEPTH)]
        page_table = din("page_table", [NS, NPG], I32)
        state_pool = din("state_pool", [DEPTH, NS, 15, WC])
        state_conv = din("state_conv", [DEPTH, NS, 2, 2 * DFF])

    y_prompt = dout("y_prompt", [NB, SEQ, D])
    nk_p = dout("nk_p", [DEPTH, NB, SEQ, 512])
    nv_p = dout("nv_p", [DEPTH, NB, SEQ, 512])
    npool_p = dout("npool_p", [DEPTH, NB, 15, WC])
    nconv_p = dout("nconv_p", [DEPTH, NB, 2, 2 * DFF])
    if NS:
        y_sample = dout("y_sample", [NS, D])
        nk_s = dout("nk_s", [DEPTH, NS, 512])
        nv_s = dout("nv_s", [DEPTH, NS, 512])
        ncv_s = dout("ncv_s", [DEPTH, NS, WA])
        npool_s = dout("npool_s", [DEPTH, NS, 15, WC])
        nconv_s = dout("nconv_s", [DEPTH, NS, 2, 2 * DFF])
    xmid = nc.dram_tensor("xmid", [NB, SEQ, D], F32, kind="Internal").ap()
    DBG = bool(os.environ.get("KDBG"))
    if DBG:
        dbg = dout("dbg", [4, 128, 4 * 512])
    if NS:
        xmid_s = nc.dram_tensor("xmid_s", [NS, D], F32, kind="Internal").ap()
        qscr = nc.dram_tensor("qscr", [NS, 512], BF16, kind="Internal").ap()
        osc_t = nc.dram_tensor("osc", [NS, 8 * 512], F32, kind="Internal")
        osc = osc_t.ap()
        lsc = nc.dram_tensor("lsc", [NS, 8], F32, kind="Internal").ap()

    class W2D:
        def __init__(self, name, l, f32ap, bfap):
            self.name, self.l, self.f32, self.bf = name, l, f32ap, bfap
            self.res = f"cv_{name}{l}"

    class WT:
        def __init__(self, name, ap):
            self.name, self.ap = name, ap
            self.bf = nc.dram_tensor("bf_" + name, list(ap.shape), BF16, kind="Internal").ap()
        def __getitem__(self, l):
            return W2D(self.name, l, self.ap[l], self.bf[l])

    w_in = WT("w_in", w_in)
    p_abc = [WT(n, a) for n, a in zip(("p_a", "p_b", "p_c"), p_abc)]
    w_o = WT("w_o", w_o)
    f_up = WT("f_up", f_up)
    f_down = WT("f_down", f_down)

    def convert_layer(l):
        def cv(w, c0, c1):
            w2 = w[l]
            P.op("pool", lambda e: e.dma_start(out=w2.bf[:, c0:c1], in_=w2.f32[:, c0:c1]), writes=[w2.res], dma=w2.res)
        for c0, c1 in ((0, 1024), (1024, 2560), (2560, 4096), (4096, 6144)):
            cv(w_in, c0, c1)
        for w in p_abc:
            cv(w, 0, 1024)
        cv(w_o, 0, 1024)
        for c0, c1 in ((0, 2048), (2048, 4096), (4096, 5632)):
            cv(f_up, c0, c1)
        cv(f_down, 0, 1024)

    st = ExitStack()

    def sb(name, shape, dt=F32):
        return st.enter_context(nc.sbuf_tensor(name, list(shape), dt))

    def ps(name, shape, dt=F32):
        return st.enter_context(nc.psum_tensor(name, list(shape), dt))

    xres = sb("xres", [128, 4, D])
    hbf = sb("hbf", [128, D], BF16)
    junk = hbf
    hT = sb("hT", [128, 8, T], BF16)
    kT = sb("kT", [128, 4, SEQ], BF16)
    vE = sb("vE", [128, NG, 4, 130], BF16)
    NW = 3
    wrall = sb("wrall", [128, NW, 4096], BF16)
    wr = [wrall[:, i, :] for i in range(NW)]
    mergedT = sb("mergedT", [128, 8, T], BF16)
    X = [sb(f"X{i}", [128, 4, T], BF16) for i in range(3)]
    Fb = [sb(f"F{i}", [128, 528]) for i in range(8)]
    zcT = sb("zcT", [128, 4, 15 + T])
    NPT = 8
    PT = [sb(f"PT{i}", [128, T], BF16) for i in range(NPT)]
    actT = sb("actT", [128, 22, T], BF16)
    halo = sb("halo", [128, 44, 2])
    small = sb("small", [128, 64])
    epsb = sb("epsb", [128, 1])
    pref = sb("pref", [128, 60])
    cst = sb("cst", [128, 4, 128])
    identb = sb("identb", [128, 128], BF16)
    maskb = sb("maskb", [128, 128], BF16)
    g1b = sb("g1T", [128, 8])
    g2b = sb("g2T", [128, 8])
    avgb = sb("avgb", [128, WA])
    gqk = sb("gqk", [128, 2, HD])
    lqk = sb("lqk", [128, 4, HD])
    gsub = sb("gsub", [128, 128])
    lam = sb("lam", [128, 4])
    wsraw = Fb[1][:, 0:512].rearrange("p (m s) -> p m s", m=4)
    wmT = sb("wmT", [128, 4, 128], BF16)
    bsT = sb("bsT", [128, 4, 128])
    cwb = sb("cwb", [128, 4, 128], BF16)
    csc = sb("csc", [128, 4])
    cvw = sb("cvw", [128, 4, 44])
    ob = sb("ob", [128, 128], BF16)
    ofin = sb("ofin", [128, 2, 128])
    if NS:
        NIDX = NS * NPG // 2
        idx = sb("idx", [128, NIDX], I32)
        assert NIDX <= 512 and NPG % 8 == 0
        idxf = Fb[0][:, 0:NIDX]
        qb2 = [PT[1], PT[2]]
        sm2 = sb("sm2", [128, 64])
        ones32 = sb("ones32", [128, 1])
        stg = Fb[0][:, 0:512].rearrange("p (a b) -> p a b", a=2)
        upst = Fb[2][:, 0:512].rearrange("p (a b) -> p a b", a=2)

    pm = [ps(f"pm{i}", [128, 512]) for i in range(7)]
    ptr = ps("ptr", [128, 1024], BF16)

    state = {"w": 0, "pm": 0, "gel": 0, "mrg": 0, "ring": [0, 1, 2, 3, 4, 5, 6]}

    def wload(src, kc, ncols):
        src_ap, cres = src
        i = state["w"] % NW
        state["w"] += 1
        buf = wr[i]
        dst = buf[:, 0:kc * ncols].rearrange("p (k n) -> p k n", k=kc)
        P.op("sp", lambda e: e.dma_start(out=dst, in_=src_ap), reads=[cres], writes=[f"wr{i}"], dma=f"wr{i}")
        return dst, f"wr{i}"

    def wview(w2d, c0, ncols, k0=0, kc=None):
        v = w2d.bf.rearrange("(k p) n -> p k n", p=128)
        if kc is None:
            kc = v.shape[1] - k0
        return (v[:, k0:k0 + kc, c0:c0 + ncols], w2d.res), kc

    def mm_group(out_ap, pairs, reads, wres):
        n = len(pairs)
        for i, (l, r) in enumerate(pairs):
            P.op("pe", lambda e, l=l, r=r, i=i: e.matmul(out_ap, l, r, start=(i == 0), stop=(i == n - 1)),
                 reads=reads, writes=[wres], inc=(i == n - 1))

    def norm_to_hT(l, gb, gname, Tn, R):
        G = (Tn + 127) // 128
        for g in range(G):
            P.op("act", lambda e, g=g: e.activation(junk[0:R, :], xres[0:R, g, :], AF.Square, accum_out=small[0:R, 0:1]),
                 reads=["xres"], writes=["hbf", "small0"])
            P.op("act", lambda e: e.activation(small[0:R, 1:2], small[0:R, 0:1], AF.Sqrt, bias=epsb[0:R, 0:1], scale=1.0 / D),
                 reads=["small0"], writes=["small1"])
            P.op("dve", lambda e: e.reciprocal(small[0:R, 2:3], small[0:R, 1:2]),
                 reads=["small1"], writes=["small2"])
            P.op("dve", lambda e, g=g: e.tensor_scalar(hbf[0:R, :], xres[0:R, g, :], small[0:R, 2:3], None, ALU.mult),
                 reads=["xres", "small2"], writes=["hbf"])
            for kc in range(8):
                P.op("pe", lambda e, kc=kc: e.transpose(ptr[:, kc * 128:kc * 128 + R], hbf[0:R, kc * 128:(kc + 1) * 128], identb[0:R, 0:R]),
                     reads=["hbf", "identb"], writes=["ptr"], inc=(kc == 7))
            P.op("dve", lambda e, g=g: e.tensor_tensor(hT[:, :, g * 128:g * 128 + R], ptr[:, :].rearrange("p (k t) -> p k t", k=8)[:, :, 0:R],
                                                   gb[:, :].unsqueeze(2).to_broadcast([128, 8, R]), ALU.mult),
                 reads=["ptr", gname], writes=["hT"])

    def nextpm():
        ring = state["ring"]
        i = ring[state["pm"] % len(ring)]
        state["pm"] += 1
        return pm[i], f"pm{i}"

    def layer_setup(l):
        li = 0.8 - 0.6 * math.exp(-0.3 * l)
        q = "sp"
        def ld(dst, src, res):
            P.op(q, lambda e: e.dma_start(out=dst, in_=src, allow_slow_non_contiguous=True), writes=[res], dma="ld_" + res)
        ld(g1b[:, :], norm1_g[l].rearrange("(k p) -> p k", p=128), "g1b")
        ld(g2b[:, :], norm2_g[l].rearrange("(k p) -> p k", p=128), "g2b")
        ld(avgb[:, :], a_vnorm_g[l].partition_broadcast(128), "avgb")
        ld(gqk[:, 0, :], b_qnorm_g[l].partition_broadcast(128), "gqk")
        ld(gqk[:, 1, :], b_knorm_g[l].partition_broadcast(128), "gqk")
        ld(lqk[:, 0, :], b_lq1[l].partition_broadcast(128), "lqk")
        ld(lqk[:, 1, :], b_lk1[l].partition_broadcast(128), "lqk")
        ld(lqk[:, 2, :], b_lq2[l].partition_broadcast(128), "lqk")
        ld(lqk[:, 3, :], b_lk2[l].partition_broadcast(128), "lqk")
        ld(gsub[:, :], b_subln_g[l].partition_broadcast(128), "gsub")
        P.op("dve", lambda e: e.tensor_scalar(gsub[:, :], gsub[:, :], (1.0 - li), None, ALU.mult), reads=["gsub"], writes=["gsub"])
        ld(wsraw, a_ws[l].rearrange("m t s -> t m s"), "F1")
        ld(bsT[:, :, :].rearrange("p m t -> p (m t)"), a_bs[l].rearrange("m t -> (m t)").partition_broadcast(128), "bsT")
        ld(Fb[0][:, 0:512].rearrange("p (m d) -> p m d", m=4), c_w[l].rearrange("m c d -> c m d"), "F0")
        ld(csc[:, :], c_scale[l].rearrange("(m c) -> c m", c=128), "csc")
        if NS:
            ld(small[:, 16:20], a_ws[l][:, 0, 0].partition_broadcast(128), "small16")
            ld(small[:, 20:24], a_bs[l][:, 0].partition_broadcast(128), "small16")
        for j in range(3):
            ld(cvw[:, j, :], f_conv_w[l, j].rearrange("(k c) -> c k", c=128), "cvw")
        ld(cvw[:, 3, :], f_conv_b[l].rearrange("(k c) -> c k", c=128), "cvw")
        P.op("dve", lambda e: e.tensor_copy(cwb[:, :, :].rearrange("p m d -> p (m d)"), Fb[0][:, 0:512]), reads=["F0"], writes=["cwb"])
        P.op("dve", lambda e: e.tensor_tensor(lqk[:, 0, :], lqk[:, 0, :], lqk[:, 1, :], ALU.mult), reads=["lqk"], writes=["lqk"])
        P.op("dve", lambda e: e.tensor_tensor(lqk[:, 2, :], lqk[:, 2, :], lqk[:, 3, :], ALU.mult), reads=["lqk"], writes=["lqk"])
        P.op("dve", lambda e: e.tensor_reduce(small[:, 8:9], lqk[:, 0, :], AX.X, ALU.add), reads=["lqk"], writes=["small8"])
        P.op("dve", lambda e: e.tensor_reduce(small[:, 9:10], lqk[:, 2, :], AX.X, ALU.add), reads=["lqk"], writes=["small9"])
        P.op("act", lambda e: e.activation(small[:, 10:12], small[:, 8:10], AF.Exp), reads=["small8", "small9"], writes=["small10"])
        P.op("dve", lambda e: e.tensor_tensor(small[:, 12:13], small[:, 11:12], small[:, 10:11], ALU.subtract), reads=["small10"], writes=["small12"])
        P.op("dve", lambda e: e.tensor_scalar(lam[:, 0:1], small[:, 12:13], -li, None, ALU.add), reads=["small12"], writes=["lam"])
        for m in range(4):
            P.op("dve", lambda e, m=m: e.tensor_tensor(hbf[:, m * 128:(m + 1) * 128], wsraw[:, m, :], cst[:, 1, :], ALU.mult),
                 reads=["F1", "cst"], writes=["hbf"])
        for m in range(4):
            P.op("pe", lambda e, m=m: e.transpose(ptr[:, m * 128:(m + 1) * 128], hbf[:, m * 128:(m + 1) * 128], identb[:, :]),
                 reads=["hbf", "identb"], writes=["ptr"], inc=(m == 3))
        P.op("act", lambda e: e.activation(wmT[:, :, :].rearrange("p m t -> p (m t)"), ptr[:, 0:512], AF.Copy), reads=["ptr"], writes=["wmT"])
        return li

    P.op("sp", lambda e: e.dma_start(out=cst[:, :, :], in_=consts), writes=["cst"], dma="ld_cst")
    P.op("dve", lambda e: e.tensor_copy(identb[:, :], cst[:, 0, :]), reads=["cst"], writes=["identb"])
    P.op("dve", lambda e: e.tensor_copy(maskb[:, :], cst[:, 2, :]), reads=["cst"], writes=["maskb"])
    P.op("dve", lambda e: e.memset(vE[:, :, :, 128:130], 1.0), writes=["vE"])
    P.op("dve", lambda e: e.memset(epsb[:, :], EPS), writes=["epsb"])
    if NS:
        P.op("dve", lambda e: e.memset(ones32[:, :], 1.0), writes=["ones32"])
        ptv = page_table.rearrange("a (j two) -> two (a j)", two=2)
        P.op("sp", lambda e: e.dma_start(out=idx[0:64, :], in_=ptv[0].partition_broadcast(64), allow_slow_non_contiguous=True), writes=["idx"], dma="ld_idx")
        P.op("sp", lambda e: e.dma_start(out=idx[64:128, :], in_=ptv[1].partition_broadcast(64), allow_slow_non_contiguous=True), writes=["idx"], dma="ld_idx")
        P.op("dve", lambda e: e.tensor_copy(idxf[:, :], idx[:, :]), reads=["idx"], writes=["F0"])
        P.op("dve", lambda e: e.tensor_scalar(idxf[:, :], idxf[:, :], 64.0, cst[:, 3, 64:65], ALU.mult, ALU.add), reads=["F0", "cst"], writes=["F0"])
        P.op("dve", lambda e: e.tensor_copy(idx[:, :], idxf[:, :]), reads=["F0"], writes=["idx"])

    def tile_layer(l, li, b, ti, sample=False):
        last_layer = (l == DEPTH - 1)
        if sample:
            Tn, R, G = NS, NS, 1
        else:
            Tn, R, G = T, 128, 4
        t0 = ti * T
        if sample:
            src = (x_sample if l == 0 else xmid_s)
            P.op("sp", lambda e, src=src: e.dma_start(out=xres[0:R, 0, :], in_=src), reads=["xmid_s"], writes=["xres"], dma="ld_x")
        else:
            src = (x_prompt if l == 0 else xmid)[b, t0:t0 + T, :].rearrange("(g p) d -> p g d", p=128)
            P.op("sp", lambda e, src=src: e.dma_start(out=xres[:, :, :], in_=src), reads=[f"xmid{b}_{ti}"], writes=["xres"], dma="ld_x")
        norm_to_hT(l, g1b, "g1b", Tn, R)

        def fm_mm(wsrc2d, c0, ncols, kin_res, rhs_of, post):
            src, kc = wview(wsrc2d, c0, ncols)
            wt, wres = wload(src, kc, ncols)
            for m in range(ncols // 128):
                pt, pres = nextpm()
                mm_group(pt[:, 0:Tn], [(wt[:, k, m * 128:(m + 1) * 128], rhs_of(k)) for k in range(kc)], [wres] + kin_res, pres)
                post(m, pt, pres)

        def tm_mm(wsrc2d, c0, ncols, lhs_of, kin_res, post, k0=0, kcn=None):
            src, kc = wview(wsrc2d, c0, ncols, k0, kcn)
            wt, wres = wload(src, kc, ncols)
            for g in range(G):
                pt, pres = nextpm()
                mm_group(pt[0:R, 0:ncols], [(lhs_of(k, g), wt[:, k, :]) for k in range(kc)], [wres] + kin_res, pres)
                post(g, pt, pres)

        hT_rhs = lambda k: hT[:, k, 0:Tn]
        hT_lhs = lambda k, g: hT[:, k, g * 128:g * 128 + R]
        uT, va, outT = X[0], X[1], X[2]

        def gelu_post(dst_of):
            def post(m, pt, pres):
                rows = pt.shape[0] if False else None
                ia, ib = ((2, 3), (6, 7))[state["gel"] % 2]
                state["gel"] += 1
                a = Fb[ia]; bq = Fb[ib]
                ra, rb = f"F{ia}", f"F{ib}"
                n = Tn if dst_of[0] == "fm" else 512
                rr = 128 if dst_of[0] == "fm" else R
                P.op("act", lambda e: e.activation(a[0:rr, 0:n], pt[0:rr, 0:n], AF.Square), reads=[pres], writes=[ra])
                P.op("pool", lambda e: e.tensor_scalar(a[0:rr, 0:n], a[0:rr, 0:n], 0.044715, 1.0, ALU.mult, ALU.add), reads=[ra], writes=[ra])
                P.op("dve", lambda e: e.tensor_tensor(a[0:rr, 0:n], a[0:rr, 0:n], pt[0:rr, 0:n], ALU.mult), reads=[ra, pres], writes=[ra])
                P.op("act", lambda e: e.activation(bq[0:rr, 0:n], a[0:rr, 0:n], AF.Sigmoid, scale=1.5957691216), reads=[ra], writes=[rb])
                dst, dres = dst_of[1](m)
                P.op("dve", lambda e: e.tensor_tensor(dst, bq[0:rr, 0:n], pt[0:rr, 0:n], ALU.mult), reads=[rb, pres], writes=[dres])
            return post
        fm_mm(w_in[l], C_U, 512, ["hT"], hT_rhs, gelu_post(("fm", lambda m: (uT[:, m, 0:Tn], "X0"))))

        def va_post(g, pt, pres):
            gelu_post(("tm", lambda m: (Fb[0][0:R, 0:512], "F0")))(g, pt, pres)
            P.op("act", lambda e: e.activation(junk[0:R, 0:512], Fb[0][0:R, 0:512], AF.Square, accum_out=small[0:R, 0:1]),
                 reads=["F0"], writes=["hbf", "small0"])
            P.op("act", lambda e: e.activation(small[0:R, 1:2], small[0:R, 0:1], AF.Sqrt, bias=epsb[0:R, 0:1], scale=1.0 / WA), reads=["small0"], writes=["small1"])
            P.op("dve", lambda e: e.reciprocal(small[0:R, 2:3], small[0:R, 1:2]), reads=["small1"], writes=["small2"])
            if sample:
                P.op("dve", lambda e: e.scalar_tensor_tensor(Fb[1][0:R, 0:512], Fb[0][0:R, 0:512], small[0:R, 2:3], avgb[0:R, :], ALU.mult, ALU.mult),
                     reads=["F0", "small2", "avgb"], writes=["F1"])
                P.op("sp", lambda e: e.dma_start(out=ncv_s[l], in_=Fb[1][0:R, 0:512]), reads=["F1"], dma="st_ncv")
                P.op("dve", lambda e: e.tensor_copy(va[0:R, 0, :], Fb[1][0:R, 0:512]), reads=["F1"], writes=["X1"])
            else:
                P.op("dve", lambda e: e.scalar_tensor_tensor(va[0:R, g, :], Fb[0][0:R, 0:512], small[0:R, 2:3], avgb[0:R, :], ALU.mult, ALU.mult),
                     reads=["F0", "small2", "avgb"], writes=["X1"])
        tm_mm(w_in[l], C_VA, 512, hT_lhs, ["hT"], va_post)

        for g in range(G):
            pt, pres = nextpm()
            for m in range(4):
                if sample:
                    P.op("pe", lambda e, m=m, pt=pt: e.matmul(pt[:, m * 128:m * 128 + R], va[0:R, 0, m * 128:(m + 1) * 128], identb[0:R, 0:R], start=True, stop=True),
                         reads=["X1", "identb"], writes=[pres], inc=(m == 3))
                else:
                    P.op("pe", lambda e, m=m, g=g, pt=pt: e.matmul(pt[:, m * 128:(m + 1) * 128], va[:, g, m * 128:(m + 1) * 128], wmT[:, m, :], start=True, stop=True),
                         reads=["X1", "wmT"], writes=[pres], inc=(m == 3))
            if sample:
                for m in range(4):
                    P.op("dve", lambda e, m=m, pt=pt: e.tensor_scalar(Fb[1][:, m * 128:m * 128 + R], pt[:, m * 128:m * 128 + R], small[:, 16 + m:17 + m], small[:, 20 + m:21 + m], ALU.mult, ALU.add),
                         reads=[pres, "small16"], writes=["F1"])
                    P.op("dve", lambda e, m=m: e.tensor_tensor(outT[:, m, 0:R], Fb[1][:, m * 128:m * 128 + R], uT[:, m, 0:R], ALU.mult),
                         reads=["F1", "X0"], writes=["X2"])
            else:
                P.op("dve", lambda e, pt=pt: e.tensor_tensor(Fb[1][:, 0:512], pt[:, 0:512], bsT[:, :, :].rearrange("p m t -> p (m t)"), ALU.add),
                     reads=[pres, "bsT"], writes=["F1"])
                P.op("dve", lambda e, g=g: e.tensor_tensor(outT[:, :, g * 128:(g + 1) * 128], Fb[1][:, 0:512].rearrange("p (m t) -> p m t", m=4), uT[:, :, g * 128:(g + 1) * 128], ALU.mult),
                     reads=["F1", "X0"], writes=["X2"])

        def merge_branch(i, first):
            for half in range(2):
                srcp, kcp = wview(p_abc[i][l], half * 512, 512)
                wp, wpres = wload(srcp, kcp, 512)
                srcg, kcg = wview(w_in[l], C_G + i * 1024 + half * 512, 512)
                wg, wgres = wload(srcg, kcg, 512)
                for mm in range(4):
                    m = half * 4 + mm
                    pp, ppres = nextpm()
                    mm_group(pp[:, 0:Tn], [(wp[:, k, mm * 128:(mm + 1) * 128], outT[:, k, 0:Tn]) for k in range(4)], [wpres, "X2"], ppres)
                    pg, pgres = nextpm()
                    mm_group(pg[:, 0:Tn], [(wg[:, k, mm * 128:(mm + 1) * 128], hT[:, k, 0:Tn]) for k in range(8)], [wgres, "hT"], pgres)
                    isg, itm = ((5, 4), (7, 6))[state["mrg"] % 2]
                    state["mrg"] += 1
                    sg_, tm_ = Fb[isg], Fb[itm]
                    rsg, rtm = f"F{isg}", f"F{itm}"
                    P.op("act", lambda e, pg=pg, sg_=sg_: e.activation(sg_[:, 0:Tn], pg[:, 0:Tn], AF.Sigmoid), reads=[pgres], writes=[rsg])
                    if first:
                        P.op("dve", lambda e, pp=pp, m=m, sg_=sg_: e.tensor_tensor(mergedT[:, m, 0:Tn], sg_[:, 0:Tn], pp[:, 0:Tn], ALU.mult),
                             reads=[rsg, ppres], writes=["mergedT"])
                    else:
                        P.op("dve", lambda e, pp=pp, sg_=sg_, tm_=tm_: e.tensor_tensor(tm_[:, 0:Tn], sg_[:, 0:Tn], pp[:, 0:Tn], ALU.mult),
                             reads=[rsg, ppres], writes=[rtm])
                        P.op("pool", lambda e, m=m, tm_=tm_: e.tensor_tensor(mergedT[:, m, 0:Tn], mergedT[:, m, 0:Tn], tm_[:, 0:Tn], ALU.add),
                             reads=[rtm, "mergedT"], writes=["mergedT"])
        def dump(i, srcT):
            if DBG and l == 0 and ti == 0 and sample == bool(os.environ.get("KDBGS")):
                for m in range(4):
                    P.op("dve", lambda e, m=m: e.tensor_copy(Fb[4][:, 0:512], srcT[:, m, :]), reads=["X2", "mergedT"], writes=["F4"])
                    P.op("sp", lambda e, m=m: e.dma_start(out=dbg[i, :, m * 512:(m + 1) * 512], in_=Fb[4][:, 0:512]), reads=["F4"], dma="st_dbg")
        dump(0, outT)
        merge_branch(0, True)

        qT = X[0]
        gq = gqk[:, 0, :]
        gk = gqk[:, 1, :]

        def qk_norm(srcp, pres, g, gain, dst, dres):
            sq = Fb[2]
            P.op("act", lambda e: e.activation(sq[0:R, 0:512], srcp[0:R, 0:512], AF.Square), reads=[pres], writes=["F2"])
            P.op("dve", lambda e: e.tensor_reduce(small[0:R, 24:32], sq[0:R, 0:512].rearrange("p (a d) -> p a d", d=HD), AX.X, ALU.add), reads=["F2"], writes=["small24"])
            P.op("act", lambda e: e.activation(small[0:R, 40:48], small[0:R, 24:32], AF.Sqrt, bias=epsb[0:R, 0:1], scale=1.0 / HD), reads=["small24"], writes=["small40"])
            P.op("dve", lambda e: e.reciprocal(small[0:R, 24:32], small[0:R, 40:48]), reads=["small40"], writes=["small24"])
            P.op("dve", lambda e: e.tensor_tensor(dst[0:R, 0:512].rearrange("p (a d) -> p a d", d=HD), srcp[0:R, 0:512].rearrange("p (a d) -> p a d", d=HD),
                                                   small[0:R, 24:32].unsqueeze(2).to_broadcast([R, 8, HD]), ALU.mult), reads=[pres, "small24"], writes=[dres])
            P.op("dve", lambda e: e.tensor_tensor(dst[0:R, 0:512].rearrange("p (a d) -> p a d", d=HD), dst[0:R, 0:512].rearrange("p (a d) -> p a d", d=HD),
                                                   gain[0:R, :].unsqueeze(1).to_broadcast([R, 8, HD]), ALU.mult), reads=[dres, "gqk"], writes=[dres])

        def to_bf_T(src, sres, dst_of, dres, nblk=4):
            P.op("act", lambda e: e.activation(hbf[0:R, 0:512], src[0:R, 0:512], AF.Copy), reads=[sres], writes=["hbf"])
            for j in range(nblk):
                P.op("pe", lambda e, j=j: e.transpose(ptr[:, j * 128:j * 128 + R], hbf[0:R, j * 128:(j + 1) * 128], identb[0:R, 0:R]),
                     reads=["hbf", "identb"], writes=["ptr"], inc=(j == nblk - 1))
            P.op("act", lambda e: e.activation(dst_of, ptr[:, 0:512].rearrange("p (j t) -> p j t", j=4)[:, :, 0:R], AF.Copy), reads=["ptr"], writes=[dres])

        kst = Fb[0]
        vst = Fb[1]
        if sample:
            ksT = X[1]
        def q_post(g, pt, pres):
            qk_norm(pt, pres, g, gq, Fb[3], "F3")
            to_bf_T(Fb[3], "F3", qT[:, :, g * 128:g * 128 + R], "X0")
        tm_mm(w_in[l], C_Q, 512, hT_lhs, ["hT"], q_post)

        def k_post(g, pt, pres):
            qk_norm(pt, pres, g, gk, kst, "F0")
            if sample:
                P.op("sp", lambda e: e.dma_start(out=nk_s[l], in_=kst[0:R, 0:512]), reads=["F0"], dma="st_k")
            else:
                P.op("sp", lambda e: e.dma_start(out=nk_p[l, b, t0 + g * 128:t0 + (g + 1) * 128, :], in_=kst[:, 0:512]), reads=["F0"], dma="st_k")
                to_bf_T(kst, "F0", kT[:, :, t0 + g * 128:t0 + (g + 1) * 128], "kT")
        tm_mm(w_in[l], C_K, 512, hT_lhs, ["hT"], k_post)

        def v_post(g, pt, pres):
            P.op("act", lambda e: e.activation(vst[0:R, 0:512], pt[0:R, 0:512], AF.Copy), reads=[pres], writes=["F1"])
            if sample:
                P.op("sp", lambda e: e.dma_start(out=nv_s[l], in_=vst[0:R, 0:512]), reads=["F1"], dma="st_v")
            else:
                P.op("sp", lambda e: e.dma_start(out=nv_p[l, b, t0 + g * 128:t0 + (g + 1) * 128, :], in_=vst[:, 0:512]), reads=["F1"], dma="st_v")
                P.op("dve", lambda e: e.tensor_copy(vE[:, ti * 4 + g, :, 0:128], vst[:, 0:512].rearrange("p (h e) -> p h e", h=4)), reads=["F1"], writes=["vE"])
        tm_mm(w_in[l], C_V, 512, hT_lhs, ["hT"], v_post)

        scale = HD ** -0.5
        if not sample:
            b0 = ti * 4
            ptc = [0]
            scnt = [0]
            nkb = b0 + 4
            accs = [pm[4], pm[5], pm[6]]

            def acc_of(g, c):
                idx = g * 2 + c
                return accs[idx // 3][:, (idx % 3) * 129:(idx % 3) * 129 + 129], f"pm{4 + idx // 3}"

            def qk_exp(h, j):
                g_lo = max(0, j - b0)
                q0 = g_lo * 128
                pts = []
                for c in range(2):
                    bi = scnt[0] % 4
                    scnt[0] += 1
                    sp_, spres = pm[bi], f"pm{bi}"
                    P.op("pe", lambda e, sp_=sp_, c=c, j=j, q0=q0, h=h: e.matmul(sp_[:, q0:T], kT[c * 64:(c + 1) * 64, h, j * 128:(j + 1) * 128],
                                                                              qT[c * 64:(c + 1) * 64, h, q0:T], start=True, stop=True),
                         reads=["kT", "X0"], writes=[spres])
                    pi = ptc[0] % NPT
                    ptc[0] += 1
                    P.op("act", lambda e, sp_=sp_, pi=pi, q0=q0: e.activation(PT[pi][:, q0:T], sp_[:, q0:T], AF.Exp, scale=scale),
                         reads=[spres], writes=[f"PT{pi}"])
                    if j >= b0:
                        P.op("pool", lambda e, pi=pi, q0=q0: e.tensor_tensor(PT[pi][:, q0:q0 + 128], PT[pi][:, q0:q0 + 128], maskb[:, :], ALU.mult),
                             reads=[f"PT{pi}", "maskb"], writes=[f"PT{pi}"])
                    pts.append(pi)
                return (h, j, g_lo, pts)

            def pv(h, j, g_lo, pts):
                for g in range(g_lo, 4):
                    for c in range(2):
                        a_ap, ares = acc_of(g, c)
                        pi = pts[c]
                        last = (j == b0 + g)
                        P.op("pe", lambda e, a_ap=a_ap, pi=pi, g=g, j=j, h=h, last=last, c=c: e.matmul(a_ap, PT[pi][:, g * 128:(g + 1) * 128], vE[:, j, h, 0:129],
                                                                                          start=(j == 0 and (g * 2 + c) % 3 == 0), stop=last, skip_group_check=True),
                             reads=[f"PT{pi}", "vE"], writes=[ares], inc=(c == 1))

            def finalize(h):
                of = Fb[4]
                sq = Fb[5]
                of3 = of[:, 0:512].rearrange("p (g e) -> p g e", g=4)
                sq3 = sq[:, 0:512].rearrange("p (g e) -> p g e", g=4)
                ob4 = hbf[:, 0:512].rearrange("p (g e) -> p g e", g=4)
                for idx_ in range(8):
                    a_, r_ = acc_of(idx_ // 2, idx_ % 2)
                    P.op("dve", lambda e, a_=a_, idx_=idx_: e.reciprocal(small[:, 32 + idx_:33 + idx_], a_[:, 128:129]), reads=[r_], writes=["small32"])
                rd = small[:, 32:40].rearrange("p (g c) -> p g c", c=2)
                P.op("dve", lambda e: e.tensor_scalar(rd[:, :, 1], rd[:, :, 1], lam[:, 0:1], None, ALU.mult), reads=["small32", "lam"], writes=["small32"])
                for g in range(4):
                    a0, r0 = acc_of(g, 0)
                    a1, r1 = acc_of(g, 1)
                    P.op("dve", lambda e, a0=a0, g=g: e.tensor_scalar(of3[:, g, :], a0[:, 0:128], small[:, 32 + 2 * g:33 + 2 * g], None, ALU.mult), reads=[r0, "small32"], writes=["F4"])
                    P.op("dve", lambda e, a1=a1, g=g: e.scalar_tensor_tensor(of3[:, g, :], a1[:, 0:128], small[:, 33 + 2 * g:34 + 2 * g], of3[:, g, :], ALU.mult, ALU.add),
                         reads=[r1, "small32", "F4"], writes=["F4"])
                P.op("pool", lambda e: e.tensor_tensor(sq[:, 0:512], of[:, 0:512], of[:, 0:512], ALU.mult), reads=["F4"], writes=["F5"])
                P.op("dve", lambda e: e.tensor_reduce(small[:, 40:44], sq3, AX.X, ALU.add), reads=["F5"], writes=["small40"])
                P.op("act", lambda e: e.activation(small[:, 44:48], small[:, 40:44], AF.Sqrt, bias=epsb[:, 0:1], scale=1.0 / 128), reads=["small40"], writes=["small44"])
                P.op("dve", lambda e: e.reciprocal(small[:, 48:52], small[:, 44:48]), reads=["small44"], writes=["small48"])
                P.op("dve", lambda e: e.tensor_tensor(sq3, of3, small[:, 48:52].unsqueeze(2).to_broadcast([128, 4, 128]), ALU.mult), reads=["F4", "small48"], writes=["F5"])
                P.op("dve", lambda e: e.tensor_tensor(ob4, sq3, gsub[:, :].unsqueeze(1).to_broadcast([128, 4, 128]), ALU.mult), reads=["F5", "gsub"], writes=["hbf"])
                for g in range(4):
                    P.op("pe", lambda e, g=g: e.transpose(ptr[:, g * 128:(g + 1) * 128], hbf[:, g * 128:(g + 1) * 128], identb[:, :]), reads=["hbf", "identb"], writes=["ptr"], inc=(g == 3))
                P.op("act", lambda e, h=h: e.activation(outT[:, h, 0:512], ptr[:, 0:512], AF.Copy), reads=["ptr"], writes=["X2"])

            pend = None
            for h in range(NH):
                for j in range(nkb):
                    cur = qk_exp(h, j)
                    if os.environ.get("KNOSKEW"):
                        pv(*cur)
                        if j == nkb - 1:
                            finalize(h)
                        continue
                    if pend is not None:
                        pv(*pend)
                        if pend[1] == nkb - 1:
                            finalize(pend[0])
                    pend = cur
            if pend is not None:
                pv(*pend)
                finalize(pend[0])
        else:
            sample_attention(l, li, qT, kst, vst, outT)
        dump(1, outT)
        merge_branch(1, False)

        pooled = X[0]
        if not sample:
            if ti == 0:
                P.op("dve", lambda e: e.memset(zcT[:, :, 0:15], 0.0), writes=["zcT"])
            else:
                P.op("dve", lambda e: e.tensor_copy(pref[:, :].rearrange("p (m t) -> p m t", m=4), zcT[:, :, T:T + 15]), reads=["zcT"], writes=["pref"])
                P.op("dve", lambda e: e.tensor_copy(zcT[:, :, 0:15], pref[:, :].rearrange("p (m t) -> p m t", m=4)), reads=["pref"], writes=["zcT"])
            def c_post(m, pt, pres):
                P.op("act", lambda e: e.activation(zcT[:, m, 15:15 + T], pt[:, 0:T], AF.Copy), reads=[pres], writes=["zcT"])
            fm_mm(w_in[l], C_C, 512, ["hT"], hT_rhs, c_post)
            if ti == NT - 1:
                for m in range(4):
                    P.op("sp", lambda e, m=m: e.dma_start(out=npool_p[l, b][:, m * 128:(m + 1) * 128].rearrange("t c -> c t"), in_=zcT[:, m, T:T + 15], allow_slow_non_contiguous=True), reads=["zcT"], dma="st_misc")
            L = 15 + T
            for m in range(4):
                cur = zcT[:, m, :]
                cres = "zcT"
                sh = 1
                for step in range(m + 1):
                    dstb = Fb[step % 2]
                    dres = f"F{step % 2}"
                    P.op("dve" if step % 2 == 0 else "pool", lambda e, dstb=dstb, cur=cur, sh=sh: e.tensor_tensor(dstb[:, sh:L], cur[:, sh:L], cur[:, 0:L - sh], ALU.add),
                         reads=[cres], writes=[dres])
                    cur = dstb
                    cres = dres
                    sh *= 2
                win = 2 ** (m + 1)
                P.op("dve", lambda e, cur=cur, m=m, win=win: e.scalar_tensor_tensor(pooled[:, m, 0:T], cur[:, 15:L], 1.0 / win, zcT[:, m, 15:L], ALU.mult, ALU.subtract),
                     reads=[cres, "zcT"], writes=["X0"])
                if ti == 0:
                    P.op("dve", lambda e, cur=cur, m=m: e.tensor_tensor(Fb[2][:, 0:16], cur[:, 15:31], cst[:, 3, m * 16:(m + 1) * 16], ALU.mult), reads=[cres, "cst"], writes=["F2"])
                    P.op("dve", lambda e, m=m: e.tensor_tensor(pooled[:, m, 0:16], Fb[2][:, 0:16], zcT[:, m, 15:31], ALU.subtract), reads=["F2", "zcT"], writes=["X0"])
        else:
            sample_pool(l, pooled)
        for m in range(4):
            pt, pres = nextpm()
            P.op("pe", lambda e, pt=pt, m=m: e.matmul(pt[:, 0:Tn], cwb[:, m, :], pooled[:, m, 0:Tn], start=True, stop=True), reads=["cwb", "X0"], writes=[pres])
            P.op("act", lambda e, pt=pt, m=m: e.activation(outT[:, m, 0:Tn], pt[:, 0:Tn], AF.Copy, scale=csc[:, m:m + 1]), reads=[pres, "csc"], writes=["X2"])
        dump(2, outT)
        merge_branch(2, False)
        dump(3, mergedT)

        def wo_post_of(half):
            def post(g, pt, pres):
                P.op("dve", lambda e: e.tensor_tensor(xres[0:R, g, half * 512:(half + 1) * 512], xres[0:R, g, half * 512:(half + 1) * 512], pt[0:R, 0:512], ALU.add),
                     reads=[pres, "xres"], writes=["xres"])
            return post
        for half in range(2):
            tm_mm(w_o[l], half * 512, 512, lambda k, g: mergedT[:, k, g * 128:g * 128 + R], ["mergedT"], wo_post_of(half))

        norm_to_hT(l, g2b, "g2b", Tn, R)
        if sample:
            sample_ffn_up(l)
        else:
            if ti == 0:
                P.op("dve", lambda e: e.memset(halo[:, :, :], 0.0), writes=["halo"])
            for jj in range(0, 22, 2):
                srcg, _ = wview(f_up[l], jj * 128, 256)
                srcv, _ = wview(f_up[l], DFF + jj * 128, 256)
                i = state["w"] % NW
                state["w"] += 1
                wbuf = wr[i]
                wres = f"wr{i}"
                dg = wbuf[:, 0:2048].rearrange("p (k n) -> p k n", k=8)
                dv = wbuf[:, 2048:4096].rearrange("p (k n) -> p k n", k=8)
                P.op("sp", lambda e, dg=dg, srcg=srcg: e.dma_start(out=dg, in_=srcg[0]), reads=[srcg[1]], writes=[wres], dma=wres)
                P.op("sp", lambda e, dv=dv, srcv=srcv: e.dma_start(out=dv, in_=srcv[0]), reads=[srcv[1]], writes=[wres], dma=wres)
                for jo in range(2):
                    j = jj + jo
                    cfin = []
                    for part, wt in ((0, dg), (1, dv)):
                        ch = j + 22 * part
                        pt, pres = nextpm()
                        mm_group(pt[:, 0:T], [(wt[:, k, jo * 128:(jo + 1) * 128], hT[:, k, 0:T]) for k in range(8)], [wres, "hT"], pres)
                        fbase = 4 * (j % 2)
                        ub = Fb[fbase + part * 2]
                        ures = f"F{fbase + part * 2}"
                        cb = Fb[fbase + part * 2 + 1]
                        cres = f"F{fbase + part * 2 + 1}"
                        P.op("pool", lambda e, ub=ub, ch=ch: e.tensor_copy(ub[:, 0:2], halo[:, ch, :]), reads=["halo"], writes=[ures])
                        P.op("act", lambda e, ub=ub, pt=pt: e.activation(ub[:, 2:2 + T], pt[:, 0:T], AF.Copy), reads=[pres], writes=[ures])
                        P.op("pool", lambda e, ub=ub, ch=ch: e.tensor_copy(halo[:, ch, :], ub[:, T:T + 2]), reads=[ures], writes=["halo"])
                        P.op("act", lambda e, ub=ub, cb=cb, ch=ch: e.activation(cb[:, 0:T], ub[:, 2:2 + T], AF.Identity, bias=cvw[:, 3, ch:ch + 1], scale=cvw[:, 2, ch:ch + 1]),
                             reads=[ures, "cvw"], writes=[cres])
                        P.op("dve", lambda e, ub=ub, cb=cb, ch=ch: e.scalar_tensor_tensor(cb[:, 0:T], ub[:, 1:1 + T], cvw[:, 1, ch:ch + 1], cb[:, 0:T], ALU.mult, ALU.add),
                             reads=[ures, cres, "cvw"], writes=[cres])
                        P.op("dve", lambda e, ub=ub, cb=cb, ch=ch: e.scalar_tensor_tensor(cb[:, 0:T], ub[:, 0:T], cvw[:, 0, ch:ch + 1], cb[:, 0:T], ALU.mult, ALU.add),
                             reads=[ures, cres, "cvw"], writes=[cres])
                        cfin.append((cb, cres))
                    P.op("act", lambda e, c0=cfin[0][0]: e.activation(c0[:, 0:T], c0[:, 0:T], AF.Silu), reads=[cfin[0][1]], writes=[cfin[0][1]])
                    P.op("pool", lambda e, c0=cfin[0][0], c1=cfin[1][0], j=j: e.tensor_tensor(actT[:, j, 0:T], c0[:, 0:T], c1[:, 0:T], ALU.mult), reads=[cfin[0][1], cfin[1][1]], writes=["actT"])
            if ti == NT - 1:
                for r in range(2):
                    P.op("sp", lambda e, r=r: e.dma_start(out=nconv_p[l, b, r].rearrange("(k c) -> c k", c=128), in_=halo[:, :, r], allow_slow_non_contiguous=True), reads=["halo"], dma="st_misc")
        for half in range(2):
            accs = [pm[3 + g] for g in range(G)]
            kparts = [(0, 8), (8, 8), (16, 6)]
            for pi_, (k0, kcn) in enumerate(kparts):
                src, kc = wview(f_down[l], half * 512, 512, k0, kcn)
                wt, wres = wload(src, kc, 512)
                for g in range(G):
                    for k in range(kc):
                        first = (pi_ == 0 and k == 0)
                        lastk = (pi_ == 2 and k == kc - 1)
                        P.op("pe", lambda e, g=g, k=k, k0=k0, wt=wt, first=first, lastk=lastk: e.matmul(accs[g][0:R, 0:512], actT[:, k0 + k, g * 128:g * 128 + R], wt[:, k, :], start=first, stop=lastk),
                             reads=[wres, "actT"], writes=[f"pm{3 + g}"], inc=(k == kc - 1))
            for g in range(G):
                P.op("dve", lambda e, g=g, half=half: e.tensor_tensor(xres[0:R, g, half * 512:(half + 1) * 512], xres[0:R, g, half * 512:(half + 1) * 512], accs[g][0:R, 0:512], ALU.add),
                     reads=[f"pm{3 + g}", "xres"], writes=["xres"])
        if sample:
            dst = y_sample if last_layer else xmid_s
            P.op("sp", lambda e: e.dma_start(out=dst, in_=xres[0:R, 0, :]), reads=["xres"], writes=["xmid_s"], dma="st_x")
        else:
            dst = (y_prompt if last_layer else xmid)[b, t0:t0 + T, :].rearrange("(g p) d -> p g d", p=128)
            P.op("sp", lambda e: e.dma_start(out=dst, in_=xres[:, :, :]), reads=["xres"], writes=[f"xmid{b}_{ti}"], dma="st_x")

    def subln(src, sres, R, li):
        P.op("act", lambda e: e.activation(junk[0:R, 0:128], src, AF.Square, accum_out=small[0:R, 34:35]), reads=[sres], writes=["hbf", "small34"])
        P.op("act", lambda e: e.activation(small[0:R, 35:36], small[0:R, 34:35], AF.Sqrt, bias=epsb[0:R, 0:1], scale=1.0 / 128), reads=["small34"], writes=["small35"])
        P.op("dve", lambda e: e.reciprocal(small[0:R, 36:37], small[0:R, 35:36]), reads=["small35"], writes=["small36"])
        P.op("dve", lambda e: e.scalar_tensor_tensor(ob[0:R, :], src, small[0:R, 36:37], gsub[0:R, :], ALU.mult, ALU.mult), reads=[sres, "small36", "gsub"], writes=["ob"])

    def sample_attention(l, li, qT, kst, vst, outT):
        R = NS
        scale = HD ** -0.5
        qf = Fb[3]
        P.op("sp", lambda e: e.dma_start(out=qscr, in_=hbf[0:R, 0:512]), reads=["hbf"], writes=["qscr"], dma="st_q")
        akv = actT[:, :, :].rearrange("p a b -> p (a b)")
        KV = [akv[:, 0:8192].rearrange("p (d s n) -> p d s n", d=4, s=2),
              wrall[:, 0:2, :].rearrange("p a b -> p (a b)").rearrange("p (d s n) -> p d s n", d=4, s=2)]
        KVres = [["KVa"], ["wr0", "wr1"]]
        KVsem = ["KVa", "KVb"]
        S = Fb[5]
        Pb = PT[0]
        ngrp = NPG // 8
        gcount = [0]
        first = [True]
        state["ring"] = [0, 1, 2]
        for i in range(NS):
            qb = qb2[i % 2]
            qres = f"PT{1 + i % 2}"
            P.op("sp", lambda e, qb=qb, i=i: e.dma_start(out=qb[:, :], in_=qscr[i].partition_broadcast(128)), reads=["qscr"], writes=[qres], dma="ld_" + qres)
            for gi in range(ngrp):
                bi = gcount[0] % 2
                gcount[0] += 1
                for dd in range(4):
                    col = i * (NPG // 2) + gi * 4 + dd
                    wk = KVres[bi] + (["actT"] if first[0] else [])
                    first[0] = False
                    P.op("pool", lambda e, bi=bi, dd=dd, col=col: e.indirect_dma_start(out=KV[bi][:, dd, :, :].rearrange("p s n -> p (s n)"), out_offset=None, in_=cache_kv[l],
                                                                                 in_offset=bass.IndirectOffsetOnAxis(ap=idx[:, col:col + 1], axis=0)),
                         reads=["idx"], writes=wk, dma=KVsem[bi])
                for hf in range(2):
                    tmp = X[hf]
                    tres = f"X{hf}"
                    P.op("dve", lambda e, tmp=tmp, bi=bi, hf=hf, qb=qb: e.tensor_tensor(tmp[:, :, :].rearrange("p (d s) n -> p d s n", s=2), KV[bi][:, hf * 2:(hf + 1) * 2, :, 0:512],
                                                                                   qb[:, :].unsqueeze(1).unsqueeze(1).to_broadcast([128, 2, 2, 512]), ALU.mult),
                         reads=KVres[bi] + [qres], writes=[tres])
                    pg0 = gi * 8 + hf * 4
                    P.op("dve", lambda e, tmp=tmp, pg0=pg0: e.tensor_reduce(S[:, pg0 * 8:(pg0 + 4) * 8], tmp[:, :, :].rearrange("p j (a d) -> p (j a) d", d=HD), AX.X, ALU.add),
                         reads=[tres], writes=["F5"])
                P.op("act", lambda e, gi=gi: e.activation(Pb[:, gi * 64:(gi + 1) * 64], S[:, gi * 64:(gi + 1) * 64], AF.Exp, scale=scale), reads=["F5"], writes=["PT0"])
                for jj in range(8):
                    j = gi * 8 + jj
                    P.op("pe", lambda e, bi=bi, jj=jj, j=j: e.matmul(pm[3][0:8, 0:512], Pb[:, j * 8:(j + 1) * 8], KV[bi][:, jj // 2, jj % 2, 512:1024], start=(j == 0), stop=(j == NPG - 1)),
                         reads=["PT0"] + KVres[bi], writes=["pm3"], inc=(jj == 7))
            P.op("dve", lambda e: e.tensor_reduce(sm2[:, 0:8], Pb[:, 0:NPG * 8].rearrange("p (j a) -> p a j", a=8), AX.X, ALU.add), reads=["PT0"], writes=["sm2a"])
            lp, lres = nextpm()
            P.op("pe", lambda e, lp=lp: e.matmul(lp[0:8, 0:1], sm2[:, 0:8], ones32[:, 0:1], start=True, stop=True), reads=["sm2a", "ones32"], writes=[lres])
            P.op("act", lambda e, lp=lp: e.activation(sm2[0:8, 8:9], lp[0:8, 0:1], AF.Copy), reads=[lres], writes=["sm2b"])
            P.op("sp", lambda e, i=i: e.dma_start(out=lsc[i].rearrange("(a b) -> a b", b=1), in_=sm2[0:8, 8:9], allow_slow_non_contiguous=True), reads=["sm2b"], writes=["lsc"], dma="st_l")
            ob_ = Fb[2] if i % 2 == 0 else Fb[4]
            ores = "F2" if i % 2 == 0 else "F4"
            P.op("act", lambda e, ob_=ob_: e.activation(ob_[0:8, 0:512], pm[3][0:8, 0:512], AF.Copy), reads=["pm3"], writes=[ores])
            P.op("sp", lambda e, ob_=ob_, i=i: e.dma_start(out=osc[i].rearrange("(a b) -> a b", b=512), in_=ob_[0:8, 0:512]), reads=[ores], writes=["osc"], dma="st_o")
        Od = [Fb[4], Fb[5]]
        for c in range(2):
            src = bass.AP(osc_t, c * 512, [[4096, NS], [1152, 4], [1, 128]])
            P.op("sp", lambda e, c=c, src=src: e.dma_start(out=Od[c][0:R, 0:512].rearrange("p (h e) -> p h e", h=4), in_=src), reads=["osc"], writes=[f"F{4 + c}"], dma=f"ld_od{c}")
        P.op("sp", lambda e: e.dma_start(out=sm2[0:R, 16:24], in_=lsc), reads=["lsc"], writes=["sm2c"], dma="ld_l")
        P.op("dve", lambda e: e.tensor_tensor(Fb[2][0:R, 0:512], qf[0:R, 0:512], kst[0:R, 0:512], ALU.mult), reads=["F3", "F0"], writes=["F2"])
        P.op("dve", lambda e: e.tensor_reduce(sm2[0:R, 24:32], Fb[2][0:R, 0:512].rearrange("p (a d) -> p a d", d=HD), AX.X, ALU.add), reads=["F2"], writes=["sm2d"])
        P.op("act", lambda e: e.activation(sm2[0:R, 32:40], sm2[0:R, 24:32], AF.Exp, scale=scale), reads=["sm2d"], writes=["sm2e"])
        P.op("dve", lambda e: e.tensor_tensor(sm2[0:R, 40:48], sm2[0:R, 16:24], sm2[0:R, 32:40], ALU.add), reads=["sm2c", "sm2e"], writes=["sm2f"])
        P.op("dve", lambda e: e.reciprocal(sm2[0:R, 48:56], sm2[0:R, 40:48]), reads=["sm2f"], writes=["sm2g"])
        rd = sm2[0:R, 48:56].rearrange("p (h c) -> p h c", c=2)
        ps_ = sm2[0:R, 32:40].rearrange("p (h c) -> p h c", c=2)
        P.op("dve", lambda e: e.tensor_scalar(rd[:, :, 1], rd[:, :, 1], lam[0:R, 0:1], None, ALU.mult), reads=["sm2g", "lam"], writes=["sm2g"])
        v3 = vst[0:R, 0:512].rearrange("p (h e) -> p h e", h=4)
        t3 = Fb[2][0:R, 0:512].rearrange("p (h e) -> p h e", h=4)
        for c in range(2):
            o3 = Od[c][0:R, 0:512].rearrange("p (h e) -> p h e", h=4)
            P.op("dve", lambda e, c=c: e.tensor_tensor(t3, v3, ps_[:, :, c].unsqueeze(2).to_broadcast([R, 4, 128]), ALU.mult), reads=["F1", "sm2e"], writes=["F2"])
            P.op("dve", lambda e, o3=o3: e.tensor_tensor(o3, o3, t3, ALU.add), reads=["F2", f"F{4 + c}"], writes=[f"F{4 + c}"])
            P.op("dve", lambda e, o3=o3, c=c: e.tensor_tensor(o3, o3, rd[:, :, c].unsqueeze(2).to_broadcast([R, 4, 128]), ALU.mult), reads=["sm2g", f"F{4 + c}"], writes=[f"F{4 + c}"])
        P.op("dve", lambda e: e.tensor_tensor(Fb[4][0:R, 0:512], Fb[4][0:R, 0:512], Fb[5][0:R, 0:512], ALU.add), reads=["F4", "F5"], writes=["F4"])
        for h in range(NH):
            subln(Fb[4][0:R, h * 128:(h + 1) * 128], "F4", R, li)
            P.op("pe", lambda e: e.transpose(ptr[:, 0:R], ob[0:R, :], identb[0:R, 0:R]), reads=["ob", "identb"], writes=["ptr"])
            P.op("act", lambda e, h=h: e.activation(outT[:, h, 0:R], ptr[:, 0:R], AF.Copy), reads=["ptr"], writes=["X2"])
        state["ring"] = [0, 1, 2, 3, 4, 5, 6]

    def sample_pool(l, pooled):
        R = NS
        zcs = zcT[:, :, 0:NS * 16].rearrange("p m (i t) -> p m i t", t=16)
        src, kc = wview(w_in[l], C_C, 512)
        wt, wres = wload(src, kc, 512)
        pt, pres = nextpm()
        mm_group(pt[0:R, 0:512], [(hT[:, k, 0:R], wt[:, k, :]) for k in range(8)], [wres, "hT"], pres)
        P.op("act", lambda e, pt=pt: e.activation(Fb[2][0:R, 0:512], pt[0:R, 0:512], AF.Copy), reads=[pres], writes=["F2"])
        P.op("sp", lambda e: e.dma_start(out=npool_s[l, :, 14, :], in_=Fb[2][0:R, 0:512]), reads=["F2"], dma="st_misc")
        P.op("sp", lambda e: e.dma_start(out=npool_s[l, :, 0:14, :], in_=state_pool[l, :, 1:15, :]), dma="st_misc")
        for m in range(4):
            pt, pres = nextpm()
            mm_group(pt[:, 0:R], [(wt[:, k, m * 128:(m + 1) * 128], hT[:, k, 0:R]) for k in range(8)], [wres, "hT"], pres)
            P.op("act", lambda e, pt=pt, m=m: e.activation(zcs[:, m, :, 15], pt[:, 0:R], AF.Copy), reads=[pres], writes=["zcT"])
        for i0 in range(0, NS, 8):
            n = min(8, NS - i0)
            rows = n * 15
            P.op("sp", lambda e, i0=i0, n=n, rows=rows: e.dma_start(out=Fb[0][0:rows, 0:512], in_=state_pool[l, i0:i0 + n].rearrange("i r f -> (i r) f")), writes=["F0"], dma="ld_sp")
            pt, pres = nextpm()
            for m in range(4):
                P.op("pe", lambda e, pt=pt, m=m, rows=rows: e.transpose(pt[:, m * 120:m * 120 + rows], Fb[0][0:rows, m * 128:(m + 1) * 128], cst[0:rows, 0, 0:rows]),
                     reads=["F0", "cst"], writes=[pres], inc=(m == 3))
            for m in range(4):
                P.op("act", lambda e, pt=pt, m=m, i0=i0, n=n, rows=rows: e.activation(zcs[:, m, i0:i0 + n, 0:15], pt[:, m * 120:m * 120 + rows].rearrange("p (i t) -> p i t", t=15), AF.Copy),
                     reads=[pres], writes=["zcT"])
        for m in range(4):
            cur = zcs[:, m, :, :]
            cres = "zcT"
            sh = 1
            for step in range(m + 1):
                dstb = Fb[step % 2][:, 0:NS * 16].rearrange("p (i t) -> p i t", t=16)
                dres = f"F{step % 2}"
                P.op("dve", lambda e, dstb=dstb, cur=cur, sh=sh: e.tensor_tensor(dstb[:, :, sh:16], cur[:, :, sh:16], cur[:, :, 0:16 - sh], ALU.add), reads=[cres], writes=[dres])
                cur = dstb
                cres = dres
                sh *= 2
            win = 2 ** (m + 1)
            P.op("dve", lambda e, cur=cur, m=m, win=win: e.scalar_tensor_tensor(pooled[:, m, 0:R], cur[:, :, 15], 1.0 / win, zcs[:, m, :, 15], ALU.mult, ALU.subtract),
                 reads=[cres, "zcT"], writes=["X0"])

    def sample_ffn_up(l):
        R = NS
        P.op("sp", lambda e: e.dma_start(out=nconv_s[l, :, 0, :], in_=state_conv[l, :, 1, :]), dma="st_misc")
        for jj in range(0, 22, 2):
            srcg, _ = wview(f_up[l], jj * 128, 256)
            srcv, _ = wview(f_up[l], DFF + jj * 128, 256)
            i = state["w"] % NW
            state["w"] += 1
            wbuf = wr[i]
            wres = f"wr{i}"
            dg = wbuf[:, 0:2048].rearrange("p (k n) -> p k n", k=8)
            dv = wbuf[:, 2048:4096].rearrange("p (k n) -> p k n", k=8)
            P.op("sp", lambda e, dg=dg, srcg=srcg: e.dma_start(out=dg, in_=srcg[0]), reads=[srcg[1]], writes=[wres], dma=wres)
            P.op("sp", lambda e, dv=dv, srcv=srcv: e.dma_start(out=dv, in_=srcv[0]), reads=[srcv[1]], writes=[wres], dma=wres)
            for part, wt in ((0, dg), (1, dv)):
                c0 = part * DFF + jj * 128
                P.op("sp", lambda e, part=part, c0=c0: e.dma_start(out=stg[0:2 * R, part, :], in_=state_conv[l, :, :, c0:c0 + 256].rearrange("i r f -> (i r) f")), writes=["F0"], dma="ld_stg")
                pt, pres = nextpm()
                mm_group(pt[0:R, 0:256], [(hT[:, k, 0:R], wt[:, k, :]) for k in range(8)], [wres, "hT"], pres)
                P.op("act", lambda e, pt=pt, part=part: e.activation(upst[0:R, part, :], pt[0:R, 0:256], AF.Copy), reads=[pres], writes=["F2"])
                P.op("sp", lambda e, part=part, c0=c0: e.dma_start(out=nconv_s[l, :, 1, c0:c0 + 256], in_=upst[0:R, part, :]), reads=["F2"], dma="st_up")
            for jo in range(2):
                j = jj + jo
                cfin = []
                for part, wt in ((0, dg), (1, dv)):
                    ch = j + 22 * part
                    pt, pres = nextpm()
                    mm_group(pt[:, 0:R], [(wt[:, k, jo * 128:(jo + 1) * 128], hT[:, k, 0:R]) for k in range(8)], [wres, "hT"], pres)
                    p2, p2res = nextpm()
                    P.op("pe", lambda e, p2=p2, part=part, jo=jo: e.transpose(p2[:, 0:2 * R], stg[0:2 * R, part, jo * 128:(jo + 1) * 128], cst[0:2 * R, 0, 0:2 * R]),
                         reads=["F0", "cst"], writes=[p2res])
                    cb = Fb[part * 2 + 1]
                    cres = f"F{part * 2 + 1}"
                    st2 = p2[:, 0:2 * R].rearrange("p (i r) -> p i r", r=2)
                    P.op("act", lambda e, cb=cb, pt=pt, ch=ch: e.activation(cb[:, 0:R], pt[:, 0:R], AF.Identity, bias=cvw[:, 3, ch:ch + 1], scale=cvw[:, 2, ch:ch + 1]),
                         reads=[pres, "cvw"], writes=[cres])
                    P.op("dve", lambda e, cb=cb, st2=st2, ch=ch: e.scalar_tensor_tensor(cb[:, 0:R], st2[:, :, 1], cvw[:, 1, ch:ch + 1], cb[:, 0:R], ALU.mult, ALU.add),
                         reads=[p2res, cres, "cvw"], writes=[cres])
                    P.op("dve", lambda e, cb=cb, st2=st2, ch=ch: e.scalar_tensor_tensor(cb[:, 0:R], st2[:, :, 0], cvw[:, 0, ch:ch + 1], cb[:, 0:R], ALU.mult, ALU.add),
                         reads=[p2res, cres, "cvw"], writes=[cres])
                    cfin.append((cb, cres))
                P.op("act", lambda e, c0_=cfin[0][0]: e.activation(Fb[4][:, 0:R], c0_[:, 0:R], AF.Silu), reads=[cfin[0][1]], writes=["F4"])
                P.op("dve", lambda e, c1=cfin[1][0], j=j: e.tensor_tensor(actT[:, j, 0:R], Fb[4][:, 0:R], c1[:, 0:R], ALU.mult), reads=["F4", cfin[1][1]], writes=["actT"])

    convert_layer(0)
    for l in range(DEPTH):
        li = layer_setup(l)
        for b in range(NB):
            for ti in range(NT):
                tile_layer(l, li, b, ti)
                if b == 0 and ti == 0 and l + 1 < DEPTH:
                    convert_layer(l + 1)
        if NS:
            tile_layer(l, li, 0, 0, sample=True)
    print("nrec", P.nrec, {e: len(P.ops[e]) for e in ENGS})
    P.emit()
    st.close()
    return nc


def make_consts():
    c = np.zeros((128, 4, 128), np.float32)
    c[:, 0, :] = np.eye(128, dtype=np.float32)
    s = np.arange(128)[:, None]
    t = np.arange(128)[None, :]
    c[:, 1, :] = (t <= s).astype(np.float32)
    c[:, 2, :] = (s <= t).astype(np.float32)
    for m in range(4):
        win = 2 ** (m + 1)
        for p in range(16):
            c[:, 3, m * 16 + p] = 1.0 / min(p + 1, win)
    c[:, 3, 64] = (np.arange(128) % 64).astype(np.float32)
    return c


_W_NAMES = ['norm1_g', 'w_in', 'a_vnorm_g', 'a_ws', 'a_bs', 'b_qnorm_g', 'b_knorm_g', 'b_lq1', 'b_lk1', 'b_lq2', 'b_lk2',
            'b_subln_g', 'c_w', 'c_scale', 'p_a', 'p_b', 'p_c', 'w_o', 'norm2_g', 'f_up', 'f_conv_w', 'f_conv_b', 'f_down']


def kernel(**inputs):
    f32 = np.float32
    x_prompt = np.ascontiguousarray(np.asarray(inputs['x_prompt'], dtype=f32))
    B, SEQ, _ = x_prompt.shape
    x_sample = np.asarray(inputs['x_sample'], dtype=f32)
    n_dec = x_sample.shape[0]
    page_table = np.asarray(inputs['page_table']).astype(np.int32)
    n_cores = B
    NS = n_dec // n_cores
    with_samples = not os.environ.get("KNOSAMPLE")
    cache_k = np.asarray(inputs['cache_k'], dtype=f32)
    cache_v = np.asarray(inputs['cache_v'], dtype=f32)
    n_pool = cache_k.shape[1]
    npg = page_table.shape[1]
    cfg = Cfg(NB=1, SEQ=SEQ, NS=(NS if with_samples else 0), NPOOL=n_pool, NPG=npg, PGRP=8)
    nc = build(cfg)
    consts = make_consts()
    weights = {k: np.ascontiguousarray(np.asarray(inputs[k], dtype=f32)) for k in _W_NAMES}
    state_pool = np.asarray(inputs['state_pool'], dtype=f32)
    state_conv = np.asarray(inputs['state_conv'], dtype=f32)
    kv_l = [np.concatenate([cache_k[l].reshape(n_pool * PAGE, 512), cache_v[l].reshape(n_pool * PAGE, 512)], axis=1).reshape(n_pool * 64, 2048)
            for l in range(DEPTH)] if with_samples else None
    in_maps = []
    for c in range(n_cores):
        m = dict(weights)
        m['x_prompt'] = x_prompt[c:c + 1]
        m['consts'] = consts
        if with_samples:
            sl = slice(c * NS, (c + 1) * NS)
            m['x_sample'] = np.ascontiguousarray(x_sample[sl, 0, :])
            m['page_table'] = np.ascontiguousarray(page_table[sl])
            m['state_pool'] = np.ascontiguousarray(state_pool[:, sl])
            m['state_conv'] = np.ascontiguousarray(state_conv[:, sl])
            for l in range(DEPTH):
                m[f'cache_kv{l}'] = kv_l[l]
        in_maps.append(m)
    res = run_bass_kernel_spmd(nc, in_maps, core_ids=list(range(n_cores))).results
    y_p = np.concatenate([r['y_prompt'].reshape(1, SEQ, D) for r in res], axis=0)
    nk = np.concatenate([r['nk_p'].reshape(DEPTH, 1, SEQ, NH, 128) for r in res], axis=1)
    nv = np.concatenate([r['nv_p'].reshape(DEPTH, 1, SEQ, NH, 128) for r in res], axis=1)
    npool = np.concatenate([r['npool_p'].reshape(DEPTH, 1, 15, WC) for r in res], axis=1)
    nconv = np.concatenate([r['nconv_p'].reshape(DEPTH, 1, 2, 2 * DFF) for r in res], axis=1)
    if with_samples:
        y_s = np.concatenate([r['y_sample'].reshape(NS, 1, D) for r in res], axis=0)
        nk_s = np.concatenate([r['nk_s'].reshape(DEPTH, NS, 1, NH, 128) for r in res], axis=1)
        nv_s = np.concatenate([r['nv_s'].reshape(DEPTH, NS, 1, NH, 128) for r in res], axis=1)
        ncv_s = np.concatenate([r['ncv_s'].reshape(DEPTH, NS, 1, WA) for r in res], axis=1)
        npool_s = np.concatenate([r['npool_s'].reshape(DEPTH, NS, 15, WC) for r in res], axis=1)
        nconv_s = np.concatenate([r['nconv_s'].reshape(DEPTH, NS, 2, 2 * DFF) for r in res], axis=1)
    else:
        y_s = np.zeros((n_dec, 1, D), f32)
        nk_s = np.zeros((DEPTH, n_dec, 1, NH, 128), f32)
        nv_s = np.zeros((DEPTH, n_dec, 1, NH, 128), f32)
        ncv_s = np.zeros((DEPTH, n_dec, 1, WA), f32)
        npool_s = np.zeros((DEPTH, n_dec, 15, WC), f32)
        nconv_s = np.zeros((DEPTH, n_dec, 2, 2 * DFF), f32)
    return (y_p.astype(f32), y_s.astype(f32), nk.astype(f32), nv.astype(f32), nk_s.astype(f32), nv_s.astype(f32), ncv_s.astype(f32),
            npool.astype(f32), npool_s.astype(f32), nconv.astype(f32), nconv_s.astype(f32))
```

```python
import math
from contextlib import ExitStack

import numpy as np
import concourse.bass as bass
import concourse.mybir as mybir
from concourse.bass_utils import run_bass_kernel_spmd

F32 = mybir.dt.float32
BF16 = mybir.dt.bfloat16
I32 = mybir.dt.int32
ALU = mybir.AluOpType
AF = mybir.ActivationFunctionType
AX = mybir.AxisListType

D = 1024
WA = 512
HD = 64
NH = 4
WB = 512
WC = 512
NIN = 6144
DFF = 2816
DEPTH = 2
EPS = 1e-6
PAGE = 128
C_U, C_VA, C_Q, C_K, C_V, C_C, C_G = 0, 512, 1024, 1536, 2048, 2560, 3072

ENGS = ("pe", "act", "dve", "pool", "sp")
import os
KSTOP = int(os.environ.get("KSTOP", "100000000"))


class Prog:
    def __init__(self, nc):
        self.nc = nc
        self.ops = {e: [] for e in ENGS}
        self.cnt = {e: 0 for e in ENGS}
        self.last_w = {}
        self.readers = {}
        self.seen = {e: {} for e in ENGS}
        self.dma_cnt = {}

    def op(self, eng, fn, reads=(), writes=(), dma=None, inc=True):
        self.nrec = getattr(self, "nrec", 0) + 1
        if self.nrec > KSTOP:
            return None
        waits = {}

        def need(tok):
            if tok is None:
                return
            k, v = tok
            if k == "pe" and eng == "pe" and dma is None:
                return
            if v > waits.get(k, 0):
                waits[k] = v

        for r in reads:
            need(self.last_w.get(r))
        for w in writes:
            need(self.last_w.get(w))
            for t in self.readers.get(w, ()):
                need(t)
        wl = []
        for k, v in waits.items():
            if self.seen[eng].get(k, 0) >= v:
                continue
            self.seen[eng][k] = v
            wl.append((k, v))
        if dma is not None:
            self.dma_cnt[dma] = self.dma_cnt.get(dma, 0) + 16
            tok = (dma, self.dma_cnt[dma])
            do_inc = True
        elif inc:
            self.cnt[eng] += 1
            tok = (eng, self.cnt[eng])
            do_inc = True
        else:
            tok = (eng, self.cnt[eng] + 1)
            do_inc = False
        self.ops[eng].append((wl, fn, tok, do_inc, dma is not None))
        for r in reads:
            self.readers.setdefault(r, []).append(tok)
        for w in writes:
            self.last_w[w] = tok
            self.readers[w] = []
        return tok

    def emit(self):
        nc = self.nc
        keys = list(ENGS) + sorted(self.dma_cnt.keys())
        with ExitStack() as st:
            sems = {k: st.enter_context(nc.semaphore("s_" + k)) for k in keys}
            block = st.enter_context(nc.Block())
            names = {"pe": "tensor", "act": "scalar", "dve": "vector", "pool": "gpsimd", "sp": "sync"}
            for eng in ENGS:
                def run(e, eng=eng):
                    for wl, fn, tok, do_inc, is_dma in self.ops[eng]:
                        for k, v in wl:
                            e.wait_ge(sems[k], v)
                        ins = fn(e)
                        if do_inc:
                            ins.then_inc(sems[tok[0]], 16 if is_dma else 1)
                    if eng == "sp":
                        for k in keys:
                            tot = self.cnt[k] if k in self.cnt else self.dma_cnt[k]
                            if tot > 0:
                                e.wait_ge(sems[k], tot)
                getattr(block, names[eng])(run)


class Cfg:
    def __init__(self, NB, SEQ, NS, NPOOL, NPG, PGRP):
        self.NB, self.SEQ, self.NS, self.NPOOL, self.NPG, self.PGRP = NB, SEQ, NS, NPOOL, NPG, PGRP


def build(cfg):
    NB, SEQ, NS, NPOOL, NPG, PGRP = cfg.NB, cfg.SEQ, cfg.NS, cfg.NPOOL, cfg.NPG, cfg.PGRP
    T = 512
    NT = SEQ // T
    NG = SEQ // 128
    nc = bass.Bass("TRN2", target_bir_lowering=False)
    P = Prog(nc)

    def din(name, shape, dt=F32):
        return nc.dram_tensor(name, list(shape), dt, kind="ExternalInput").ap()

    def dout(name, shape):
        return nc.dram_tensor(name, list(shape), F32, kind="ExternalOutput").ap()

    x_prompt = din("x_prompt", [NB, SEQ, D])
    norm1_g = din("norm1_g", [DEPTH, D])
    w_in = din("w_in", [DEPTH, D, NIN])
    a_vnorm_g = din("a_vnorm_g", [DEPTH, WA])
    a_ws = din("a_ws", [DEPTH, 4, 128, 128])
    a_bs = din("a_bs", [DEPTH, 4, 128])
    b_qnorm_g = din("b_qnorm_g", [DEPTH, HD])
    b_knorm_g = din("b_knorm_g", [DEPTH, HD])
    b_lq1 = din("b_lq1", [DEPTH, HD])
    b_lk1 = din("b_lk1", [DEPTH, HD])
    b_lq2 = din("b_lq2", [DEPTH, HD])
    b_lk2 = din("b_lk2", [DEPTH, HD])
    b_subln_g = din("b_subln_g", [DEPTH, 128])
    c_w = din("c_w", [DEPTH, 4, 128, 128])
    c_scale = din("c_scale", [DEPTH, WC])
    p_abc = [din("p_a", [DEPTH, 512, D]), din("p_b", [DEPTH, 512, D]), din("p_c", [DEPTH, 512, D])]
    w_o = din("w_o", [DEPTH, D, D])
    norm2_g = din("norm2_g", [DEPTH, D])
    f_up = din("f_up", [DEPTH, D, 2 * DFF])
    f_conv_w = din("f_conv_w", [DEPTH, 3, 2 * DFF])
    f_conv_b = din("f_conv_b", [DEPTH, 2 * DFF])
    f_down = din("f_down", [DEPTH, DFF, D])
    consts = din("consts", [128, 4, 128])
    if NS:
        x_sample = din("x_sample", [NS, D])
        cache_kv = [din(f"cache_kv{i}", [NPOOL * 64, 2048]) for i in range(DEPTH)]
        page_table = din("page_table", [NS, NPG], I32)
        state_pool = din("state_pool", [DEPTH, NS, 15, WC])
        state_conv = din("state_conv", [DEPTH, NS, 2, 2 * DFF])

    y_prompt = dout("y_prompt", [NB, SEQ, D])
    nk_p = dout("nk_p", [DEPTH, NB, SEQ, 512])
    nv_p = dout("nv_p", [DEPTH, NB, SEQ, 512])
    npool_p = dout("npool_p", [DEPTH, NB, 15, WC])
    nconv_p = dout("nconv_p", [DEPTH, NB, 2, 2 * DFF])
    if NS:
        y_sample = dout("y_sample", [NS, D])
        nk_s = dout("nk_s", [DEPTH, NS, 512])
        nv_s = dout("nv_s", [DEPTH, NS, 512])
        ncv_s = dout("ncv_s", [DEPTH, NS, WA])
        npool_s = dout("npool_s", [DEPTH, NS, 15, WC])
        nconv_s = dout("nconv_s", [DEPTH, NS, 2, 2 * DFF])
    xmid = nc.dram_tensor("xmid", [NB, SEQ, D], F32, kind="Internal").ap()
    DBG = bool(os.environ.get("KDBG"))
    if DBG:
        dbg = dout("dbg", [4, 128, 4 * 512])
    if NS:
        xmid_s = nc.dram_tensor("xmid_s", [NS, D], F32, kind="Internal").ap()
        qscr = nc.dram_tensor("qscr", [NS, 512], BF16, kind="Internal").ap()
        osc_t = nc.dram_tensor("osc", [NS, 8 * 512], F32, kind="Internal")
        osc = osc_t.ap()
        lsc = nc.dram_tensor("lsc", [NS, 8], F32, kind="Internal").ap()

    class W2D:
        def __init__(self, name, l, f32ap, bfap):
            self.name, self.l, self.f32, self.bf = name, l, f32ap, bfap
            self.res = f"cv_{name}{l}"

    class WT:
        def __init__(self, name, ap):
            self.name, self.ap = name, ap
            self.bf = nc.dram_tensor("bf_" + name, list(ap.shape), BF16, kind="Internal").ap()
        def __getitem__(self, l):
            return W2D(self.name, l, self.ap[l], self.bf[l])

    w_in = WT("w_in", w_in)
    p_abc = [WT(n, a) for n, a in zip(("p_a", "p_b", "p_c"), p_abc)]
    w_o = WT("w_o", w_o)
    f_up = WT("f_up", f_up)
    f_down = WT("f_down", f_down)

    def convert_layer(l):
        def cv(w, c0, c1):
            w2 = w[l]
            P.op("pool", lambda e: e.dma_start(out=w2.bf[:, c0:c1], in_=w2.f32[:, c0:c1]), writes=[w2.res], dma=w2.res)
        for c0, c1 in ((0, 1024), (1024, 2560), (2560, 4096), (4096, 6144)):
            cv(w_in, c0, c1)
        for w in p_abc:
            cv(w, 0, 1024)
        cv(w_o, 0, 1024)
        for c0, c1 in ((0, 2048), (2048, 4096), (4096, 5632)):
            cv(f_up, c0, c1)
        cv(f_down, 0, 1024)

    st = ExitStack()

    def sb(name, shape, dt=F32):
        return st.enter_context(nc.sbuf_tensor(name, list(shape), dt))

    def ps(name, shape, dt=F32):
        return st.enter_context(nc.psum_tensor(name, list(shape), dt))

    xres = sb("xres", [128, 4, D])
    hbf = sb("hbf", [128, D], BF16)
    junk = hbf
    hT = sb("hT", [128, 8, T], BF16)
    kT = sb("kT", [128, 4, SEQ], BF16)
    vE = sb("vE", [128, NG, 4, 130], BF16)
    NW = 3
    wrall = sb("wrall", [128, NW, 4096], BF16)
    wr = [wrall[:, i, :] for i in range(NW)]
    mergedT = sb("mergedT", [128, 8, T], BF16)
    X = [sb(f"X{i}", [128, 4, T], BF16) for i in range(3)]
    Fb = [sb(f"F{i}", [128, 528]) for i in range(8)]
    zcT = sb("zcT", [128, 4, 15 + T])
    NPT = 8
    PT = [sb(f"PT{i}", [128, T], BF16) for i in range(NPT)]
    actT = sb("actT", [128, 22, T], BF16)
    halo = sb("halo", [128, 44, 2])
    small = sb("small", [128, 64])
    epsb = sb("epsb", [128, 1])
    pref = sb("pref", [128, 60])
    smq = sb("smq", [128, 32])
    cst = sb("cst", [128, 4, 128])
    identb = sb("identb", [128, 128], BF16)
    maskb = sb("maskb", [128, 128], BF16)
    g1b = sb("g1T", [128, 8])
    g2b = sb("g2T", [128, 8])
    avgb = sb("avgb", [128, WA])
    gqk = sb("gqk", [128, 2, HD])
    lqk = sb("lqk", [128, 4, HD])
    gsub = sb("gsub", [128, 128])
    lam = sb("lam", [128, 4])
    wsraw = Fb[1][:, 0:512].rearrange("p (m s) -> p m s", m=4)
    wmT = sb("wmT", [128, 4, 128], BF16)
    bsT = sb("bsT", [128, 4, 128])
    cwb = sb("cwb", [128, 4, 128], BF16)
    csc = sb("csc", [128, 4])
    cvw = sb("cvw", [128, 4, 44])
    ob = sb("ob", [128, 128], BF16)
    ofin = sb("ofin", [128, 2, 128])
    if NS:
        NIDX = NS * NPG // 2
        idx = sb("idx", [128, NIDX], I32)
        assert NIDX <= 512 and NPG % 8 == 0
        idxf = Fb[0][:, 0:NIDX]
        qb2 = [PT[1], PT[2]]
        sm2 = sb("sm2", [128, 64])
        ones32 = sb("ones32", [128, 1])
        stg = Fb[0][:, 0:512].rearrange("p (a b) -> p a b", a=2)
        upst = Fb[2][:, 0:512].rearrange("p (a b) -> p a b", a=2)

    pm = [ps(f"pm{i}", [128, 512]) for i in range(7)]
    ptr = ps("ptr", [128, 1024], BF16)

    state = {"w": 0, "pm": 0, "gel": 0, "mrg": 0, "ring": [0, 1, 2, 3, 4, 5, 6]}

    def wload(src, kc, ncols):
        src_ap, cres = src
        i = state["w"] % NW
        state["w"] += 1
        buf = wr[i]
        dst = buf[:, 0:kc * ncols].rearrange("p (k n) -> p k n", k=kc)
        P.op("sp", lambda e: e.dma_start(out=dst, in_=src_ap), reads=[cres], writes=[f"wr{i}"], dma=f"wr{i}")
        return dst, f"wr{i}"

    def wview(w2d, c0, ncols, k0=0, kc=None):
        v = w2d.bf.rearrange("(k p) n -> p k n", p=128)
        if kc is None:
            kc = v.shape[1] - k0
        return (v[:, k0:k0 + kc, c0:c0 + ncols], w2d.res), kc

    def mm_group(out_ap, pairs, reads, wres):
        n = len(pairs)
        for i, (l, r) in enumerate(pairs):
            P.op("pe", lambda e, l=l, r=r, i=i: e.matmul(out_ap, l, r, start=(i == 0), stop=(i == n - 1)),
                 reads=reads, writes=[wres], inc=(i == n - 1))

    def norm_to_hT(l, gb, gname, Tn, R):
        G = (Tn + 127) // 128
        for g in range(G):
            P.op("act", lambda e, g=g: e.activation(junk[0:R, :], xres[0:R, g, :], AF.Square, accum_out=small[0:R, 0:1]),
                 reads=["xres"], writes=["hbfA", "hbfB", "small0"])
            P.op("act", lambda e: e.activation(small[0:R, 1:2], small[0:R, 0:1], AF.Sqrt, bias=epsb[0:R, 0:1], scale=1.0 / D),
                 reads=["small0"], writes=["small1"])
            P.op("dve", lambda e: e.reciprocal(small[0:R, 2:3], small[0:R, 1:2]),
                 reads=["small1"], writes=["small2"])
            P.op("dve", lambda e, g=g: e.tensor_scalar(hbf[0:R, :], xres[0:R, g, :], small[0:R, 2:3], None, ALU.mult),
                 reads=["xres", "small2"], writes=["hbfA", "hbfB"])
            for kc in range(8):
                P.op("pe", lambda e, kc=kc: e.transpose(ptr[:, kc * 128:kc * 128 + R], hbf[0:R, kc * 128:(kc + 1) * 128], identb[0:R, 0:R]),
                     reads=["hbfA", "hbfB", "identb"], writes=["ptrA", "ptrB"], inc=(kc == 7))
            P.op("dve", lambda e, g=g: e.tensor_tensor(hT[:, :, g * 128:g * 128 + R], ptr[:, :].rearrange("p (k t) -> p k t", k=8)[:, :, 0:R],
                                                   gb[:, :].unsqueeze(2).to_broadcast([128, 8, R]), ALU.mult),
                 reads=["ptrA", "ptrB", gname], writes=["hT"])

    def nextpm():
        ring = state["ring"]
        i = ring[state["pm"] % len(ring)]
        state["pm"] += 1
        return pm[i], f"pm{i}"

    def layer_setup(l):
        li = 0.8 - 0.6 * math.exp(-0.3 * l)
        q = "sp"
        def ld(dst, src, res):
            P.op(q, lambda e: e.dma_start(out=dst, in_=src, allow_slow_non_contiguous=True), writes=[res], dma="ld_" + res)
        ld(g1b[:, :], norm1_g[l].rearrange("(k p) -> p k", p=128), "g1b")
        ld(g2b[:, :], norm2_g[l].rearrange("(k p) -> p k", p=128), "g2b")
        ld(avgb[:, :], a_vnorm_g[l].partition_broadcast(128), "avgb")
        ld(gqk[:, 0, :], b_qnorm_g[l].partition_broadcast(128), "gqk")
        ld(gqk[:, 1, :], b_knorm_g[l].partition_broadcast(128), "gqk")
        ld(lqk[:, 0, :], b_lq1[l].partition_broadcast(128), "lqk")
        ld(lqk[:, 1, :], b_lk1[l].partition_broadcast(128), "lqk")
        ld(lqk[:, 2, :], b_lq2[l].partition_broadcast(128), "lqk")
        ld(lqk[:, 3, :], b_lk2[l].partition_broadcast(128), "lqk")
        ld(gsub[:, :], b_subln_g[l].partition_broadcast(128), "gsub")
        P.op("dve", lambda e: e.tensor_scalar(gsub[:, :], gsub[:, :], (1.0 - li), None, ALU.mult), reads=["gsub"], writes=["gsub"])
        ld(wsraw, a_ws[l].rearrange("m t s -> t m s"), "F1")
        ld(bsT[:, :, :].rearrange("p m t -> p (m t)"), a_bs[l].rearrange("m t -> (m t)").partition_broadcast(128), "bsT")
        ld(Fb[0][:, 0:512].rearrange("p (m d) -> p m d", m=4), c_w[l].rearrange("m c d -> c m d"), "F0")
        ld(csc[:, :], c_scale[l].rearrange("(m c) -> c m", c=128), "csc")
        if NS:
            ld(small[:, 16:20], a_ws[l][:, 0, 0].partition_broadcast(128), "small16")
            ld(small[:, 20:24], a_bs[l][:, 0].partition_broadcast(128), "small16")
        for j in range(3):
            ld(cvw[:, j, :], f_conv_w[l, j].rearrange("(k c) -> c k", c=128), "cvw")
        ld(cvw[:, 3, :], f_conv_b[l].rearrange("(k c) -> c k", c=128), "cvw")
        P.op("dve", lambda e: e.tensor_copy(cwb[:, :, :].rearrange("p m d -> p (m d)"), Fb[0][:, 0:512]), reads=["F0"], writes=["cwb"])
        P.op("dve", lambda e: e.tensor_tensor(lqk[:, 0, :], lqk[:, 0, :], lqk[:, 1, :], ALU.mult), reads=["lqk"], writes=["lqk"])
        P.op("dve", lambda e: e.tensor_tensor(lqk[:, 2, :], lqk[:, 2, :], lqk[:, 3, :], ALU.mult), reads=["lqk"], writes=["lqk"])
        P.op("dve", lambda e: e.tensor_reduce(small[:, 8:9], lqk[:, 0, :], AX.X, ALU.add), reads=["lqk"], writes=["small8"])
        P.op("dve", lambda e: e.tensor_reduce(small[:, 9:10], lqk[:, 2, :], AX.X, ALU.add), reads=["lqk"], writes=["small9"])
        P.op("act", lambda e: e.activation(small[:, 10:12], small[:, 8:10], AF.Exp), reads=["small8", "small9"], writes=["small10"])
        P.op("dve", lambda e: e.tensor_tensor(small[:, 12:13], small[:, 11:12], small[:, 10:11], ALU.subtract), reads=["small10"], writes=["small12"])
        P.op("dve", lambda e: e.tensor_scalar(lam[:, 0:1], small[:, 12:13], -li, None, ALU.add), reads=["small12"], writes=["lam"])
        for m in range(4):
            P.op("dve", lambda e, m=m: e.tensor_tensor(hbf[:, m * 128:(m + 1) * 128], wsraw[:, m, :], cst[:, 1, :], ALU.mult),
                 reads=["F1", "cst"], writes=["hbfA", "hbfB"])
        for m in range(4):
            P.op("pe", lambda e, m=m: e.transpose(ptr[:, m * 128:(m + 1) * 128], hbf[:, m * 128:(m + 1) * 128], identb[:, :]),
                 reads=["hbfA", "hbfB", "identb"], writes=["ptrA", "ptrB"], inc=(m == 3))
        P.op("act", lambda e: e.activation(wmT[:, :, :].rearrange("p m t -> p (m t)"), ptr[:, 0:512], AF.Copy), reads=["ptrA", "ptrB"], writes=["wmT"])
        return li

    P.op("sp", lambda e: e.dma_start(out=cst[:, :, :], in_=consts), writes=["cst"], dma="ld_cst")
    P.op("dve", lambda e: e.tensor_copy(identb[:, :], cst[:, 0, :]), reads=["cst"], writes=["identb"])
    P.op("dve", lambda e: e.tensor_copy(maskb[:, :], cst[:, 2, :]), reads=["cst"], writes=["maskb"])
    P.op("dve", lambda e: e.memset(vE[:, :, :, 128:130], 1.0), writes=["vE"])
    P.op("dve", lambda e: e.memset(epsb[:, :], EPS), writes=["epsb"])
    if NS:
        P.op("dve", lambda e: e.memset(ones32[:, :], 1.0), writes=["ones32"])
        ptv = page_table.rearrange("a (j two) -> two (a j)", two=2)
        P.op("sp", lambda e: e.dma_start(out=idx[0:64, :], in_=ptv[0].partition_broadcast(64), allow_slow_non_contiguous=True), writes=["idx"], dma="ld_idx")
        P.op("sp", lambda e: e.dma_start(out=idx[64:128, :], in_=ptv[1].partition_broadcast(64), allow_slow_non_contiguous=True), writes=["idx"], dma="ld_idx")
        P.op("dve", lambda e: e.tensor_copy(idxf[:, :], idx[:, :]), reads=["idx"], writes=["F0"])
        P.op("dve", lambda e: e.tensor_scalar(idxf[:, :], idxf[:, :], 64.0, cst[:, 3, 64:65], ALU.mult, ALU.add), reads=["F0", "cst"], writes=["F0"])
        P.op("dve", lambda e: e.tensor_copy(idx[:, :], idxf[:, :]), reads=["F0"], writes=["idx"])

    def tile_layer(l, li, b, ti, sample=False):
        last_layer = (l == DEPTH - 1)
        if sample:
            Tn, R, G = NS, NS, 1
        else:
            Tn, R, G = T, 128, 4
        t0 = ti * T
        if sample:
            src = (x_sample if l == 0 else xmid_s)
            P.op("sp", lambda e, src=src: e.dma_start(out=xres[0:R, 0, :], in_=src), reads=["xmid_s"], writes=["xres"], dma="ld_x")
        else:
            src = (x_prompt if l == 0 else xmid)[b, t0:t0 + T, :].rearrange("(g p) d -> p g d", p=128)
            P.op("sp", lambda e, src=src: e.dma_start(out=xres[:, :, :], in_=src), reads=[f"xmid{b}_{ti}"], writes=["xres"], dma="ld_x")
        norm_to_hT(l, g1b, "g1b", Tn, R)

        def fm_mm(wsrc2d, c0, ncols, kin_res, rhs_of, post):
            src, kc = wview(wsrc2d, c0, ncols)
            wt, wres = wload(src, kc, ncols)
            for m in range(ncols // 128):
                pt, pres = nextpm()
                mm_group(pt[:, 0:Tn], [(wt[:, k, m * 128:(m + 1) * 128], rhs_of(k)) for k in range(kc)], [wres] + kin_res, pres)
                post(m, pt, pres)

        def tm_mm(wsrc2d, c0, ncols, lhs_of, kin_res, post, k0=0, kcn=None):
            src, kc = wview(wsrc2d, c0, ncols, k0, kcn)
            wt, wres = wload(src, kc, ncols)
            for g in range(G):
                pt, pres = nextpm()
                mm_group(pt[0:R, 0:ncols], [(lhs_of(k, g), wt[:, k, :]) for k in range(kc)], [wres] + kin_res, pres)
                post(g, pt, pres)

        hT_rhs = lambda k: hT[:, k, 0:Tn]
        hT_lhs = lambda k, g: hT[:, k, g * 128:g * 128 + R]
        uT, va, outT = X[0], X[1], X[2]

        def gelu_post(dst_of):
            def post(m, pt, pres):
                rows = pt.shape[0] if False else None
                ia, ib = ((2, 3), (6, 7))[state["gel"] % 2]
                state["gel"] += 1
                a = Fb[ia]; bq = Fb[ib]
                ra, rb = f"F{ia}", f"F{ib}"
                n = Tn if dst_of[0] == "fm" else 512
                rr = 128 if dst_of[0] == "fm" else R
                P.op("act", lambda e: e.activation(a[0:rr, 0:n], pt[0:rr, 0:n], AF.Square), reads=[pres], writes=[ra])
                P.op("pool", lambda e: e.tensor_scalar(a[0:rr, 0:n], a[0:rr, 0:n], 0.044715, 1.0, ALU.mult, ALU.add), reads=[ra], writes=[ra])
                P.op("dve", lambda e: e.tensor_tensor(a[0:rr, 0:n], a[0:rr, 0:n], pt[0:rr, 0:n], ALU.mult), reads=[ra, pres], writes=[ra])
                P.op("act", lambda e: e.activation(bq[0:rr, 0:n], a[0:rr, 0:n], AF.Sigmoid, scale=1.5957691216), reads=[ra], writes=[rb])
                dst, dres = dst_of[1](m)
                P.op("dve", lambda e: e.tensor_tensor(dst, bq[0:rr, 0:n], pt[0:rr, 0:n], ALU.mult), reads=[rb, pres], writes=[dres])
            return post
        fm_mm(w_in[l], C_U, 512, ["hT"], hT_rhs, gelu_post(("fm", lambda m: (uT[:, m, 0:Tn], "X0"))))

        def va_post(g, pt, pres):
            gelu_post(("tm", lambda m: (Fb[0][0:R, 0:512], "F0")))(g, pt, pres)
            P.op("act", lambda e: e.activation(junk[0:R, 0:512], Fb[0][0:R, 0:512], AF.Square, accum_out=small[0:R, 0:1]),
                 reads=["F0"], writes=["hbfA", "hbfB", "small0"])
            P.op("act", lambda e: e.activation(small[0:R, 1:2], small[0:R, 0:1], AF.Sqrt, bias=epsb[0:R, 0:1], scale=1.0 / WA), reads=["small0"], writes=["small1"])
            P.op("dve", lambda e: e.reciprocal(small[0:R, 2:3], small[0:R, 1:2]), reads=["small1"], writes=["small2"])
            if sample:
                P.op("dve", lambda e: e.scalar_tensor_tensor(Fb[1][0:R, 0:512], Fb[0][0:R, 0:512], small[0:R, 2:3], avgb[0:R, :], ALU.mult, ALU.mult),
                     reads=["F0", "small2", "avgb"], writes=["F1"])
                P.op("sp", lambda e: e.dma_start(out=ncv_s[l], in_=Fb[1][0:R, 0:512]), reads=["F1"], dma="st_ncv")
                P.op("dve", lambda e: e.tensor_copy(va[0:R, 0, :], Fb[1][0:R, 0:512]), reads=["F1"], writes=["X1"])
            else:
                P.op("dve", lambda e: e.scalar_tensor_tensor(va[0:R, g, :], Fb[0][0:R, 0:512], small[0:R, 2:3], avgb[0:R, :], ALU.mult, ALU.mult),
                     reads=["F0", "small2", "avgb"], writes=["X1"])
        tm_mm(w_in[l], C_VA, 512, hT_lhs, ["hT"], va_post)

        for g in range(G):
            pt, pres = nextpm()
            for m in range(4):
                if sample:
                    P.op("pe", lambda e, m=m, pt=pt: e.matmul(pt[:, m * 128:m * 128 + R], va[0:R, 0, m * 128:(m + 1) * 128], identb[0:R, 0:R], start=True, stop=True),
                         reads=["X1", "identb"], writes=[pres], inc=(m == 3))
                else:
                    P.op("pe", lambda e, m=m, g=g, pt=pt: e.matmul(pt[:, m * 128:(m + 1) * 128], va[:, g, m * 128:(m + 1) * 128], wmT[:, m, :], start=True, stop=True),
                         reads=["X1", "wmT"], writes=[pres], inc=(m == 3))
            if sample:
                for m in range(4):
                    P.op("dve", lambda e, m=m, pt=pt: e.tensor_scalar(Fb[1][:, m * 128:m * 128 + R], pt[:, m * 128:m * 128 + R], small[:, 16 + m:17 + m], small[:, 20 + m:21 + m], ALU.mult, ALU.add),
                         reads=[pres, "small16"], writes=["F1"])
                    P.op("dve", lambda e, m=m: e.tensor_tensor(outT[:, m, 0:R], Fb[1][:, m * 128:m * 128 + R], uT[:, m, 0:R], ALU.mult),
                         reads=["F1", "X0"], writes=["X2"])
            else:
                P.op("dve", lambda e, pt=pt: e.tensor_tensor(Fb[1][:, 0:512], pt[:, 0:512], bsT[:, :, :].rearrange("p m t -> p (m t)"), ALU.add),
                     reads=[pres, "bsT"], writes=["F1"])
                P.op("dve", lambda e, g=g: e.tensor_tensor(outT[:, :, g * 128:(g + 1) * 128], Fb[1][:, 0:512].rearrange("p (m t) -> p m t", m=4), uT[:, :, g * 128:(g + 1) * 128], ALU.mult),
                     reads=["F1", "X0"], writes=["X2"])

        def merge_branch(i, first):
            for half in range(2):
                srcp, kcp = wview(p_abc[i][l], half * 512, 512)
                wp, wpres = wload(srcp, kcp, 512)
                srcg, kcg = wview(w_in[l], C_G + i * 1024 + half * 512, 512)
                wg, wgres = wload(srcg, kcg, 512)
                for mm in range(4):
                    m = half * 4 + mm
                    pp, ppres = nextpm()
                    mm_group(pp[:, 0:Tn], [(wp[:, k, mm * 128:(mm + 1) * 128], outT[:, k, 0:Tn]) for k in range(4)], [wpres, "X2"], ppres)
                    pg, pgres = nextpm()
                    mm_group(pg[:, 0:Tn], [(wg[:, k, mm * 128:(mm + 1) * 128], hT[:, k, 0:Tn]) for k in range(8)], [wgres, "hT"], pgres)
                    isg, itm = ((5, 4), (7, 6))[state["mrg"] % 2]
                    state["mrg"] += 1
                    sg_, tm_ = Fb[isg], Fb[itm]
                    rsg, rtm = f"F{isg}", f"F{itm}"
                    P.op("act", lambda e, pg=pg, sg_=sg_: e.activation(sg_[:, 0:Tn], pg[:, 0:Tn], AF.Sigmoid), reads=[pgres], writes=[rsg])
                    if first:
                        P.op("dve", lambda e, pp=pp, m=m, sg_=sg_: e.tensor_tensor(mergedT[:, m, 0:Tn], sg_[:, 0:Tn], pp[:, 0:Tn], ALU.mult),
                             reads=[rsg, ppres], writes=["mergedT"])
                    else:
                        P.op("dve", lambda e, pp=pp, sg_=sg_, tm_=tm_: e.tensor_tensor(tm_[:, 0:Tn], sg_[:, 0:Tn], pp[:, 0:Tn], ALU.mult),
                             reads=[rsg, ppres], writes=[rtm])
                        P.op("pool", lambda e, m=m, tm_=tm_: e.tensor_tensor(mergedT[:, m, 0:Tn], mergedT[:, m, 0:Tn], tm_[:, 0:Tn], ALU.add),
                             reads=[rtm, "mergedT"], writes=["mergedT"])
        def dump(i, srcT):
            if DBG and l == 0 and ti == 0 and sample == bool(os.environ.get("KDBGS")):
                for m in range(4):
                    P.op("dve", lambda e, m=m: e.tensor_copy(Fb[4][:, 0:512], srcT[:, m, :]), reads=["X2", "mergedT"], writes=["F4"])
                    P.op("sp", lambda e, m=m: e.dma_start(out=dbg[i, :, m * 512:(m + 1) * 512], in_=Fb[4][:, 0:512]), reads=["F4"], dma="st_dbg")
        dump(0, outT)
        merge_branch(0, True)

        qT = X[0]
        gq = gqk[:, 0, :]
        gk = gqk[:, 1, :]

        def qk_norm(srcp, pres, g, gain, dst, dres):
            r = g % 2
            sq = Fb[2] if r == 0 else Fb[6]
            sqres = "F2" if r == 0 else "F6"
            c0 = r * 16
            sa = smq[0:R, c0:c0 + 8]
            sb_ = smq[0:R, c0 + 8:c0 + 16]
            ra, rb = f"smq{r}a", f"smq{r}b"
            P.op("act", lambda e: e.activation(sq[0:R, 0:512], srcp[0:R, 0:512], AF.Square), reads=[pres], writes=[sqres])
            P.op("dve", lambda e: e.tensor_reduce(sa, sq[0:R, 0:512].rearrange("p (a d) -> p a d", d=HD), AX.X, ALU.add), reads=[sqres], writes=[ra])
            P.op("act", lambda e: e.activation(sb_, sa, AF.Sqrt, bias=epsb[0:R, 0:1], scale=1.0 / HD), reads=[ra], writes=[rb])
            P.op("dve", lambda e: e.reciprocal(sa, sb_), reads=[rb], writes=[ra])
            P.op("dve", lambda e: e.tensor_tensor(dst[0:R, 0:512].rearrange("p (a d) -> p a d", d=HD), srcp[0:R, 0:512].rearrange("p (a d) -> p a d", d=HD),
                                                   sa.unsqueeze(2).to_broadcast([R, 8, HD]), ALU.mult), reads=[pres, ra], writes=[dres])
            P.op("pool", lambda e: e.tensor_tensor(dst[0:R, 0:512].rearrange("p (a d) -> p a d", d=HD), dst[0:R, 0:512].rearrange("p (a d) -> p a d", d=HD),
                                                    gain[0:R, :].unsqueeze(1).to_broadcast([R, 8, HD]), ALU.mult), reads=[dres, "gqk"], writes=[dres])

        def to_bf_T(src, sres, dst_of, dres, g=0, nblk=4):
            r = g % 2
            hres = "hbfA" if r == 0 else "hbfB"
            pres_ = "ptrA" if r == 0 else "ptrB"
            hb = hbf[0:R, r * 512:(r + 1) * 512]
            pp = ptr[:, r * 512:(r + 1) * 512]
            P.op("act", lambda e: e.activation(hb, src[0:R, 0:512], AF.Copy), reads=[sres], writes=[hres])
            for j in range(nblk):
                P.op("pe", lambda e, j=j: e.transpose(pp[:, j * 128:j * 128 + R], hb[:, j * 128:(j + 1) * 128], identb[0:R, 0:R]),
                     reads=[hres, "identb"], writes=[pres_], inc=(j == nblk - 1))
            P.op("act", lambda e: e.activation(dst_of, pp.rearrange("p (j t) -> p j t", j=4)[:, :, 0:R], AF.Copy), reads=[pres_], writes=[dres])

        kst = Fb[0]
        vst = Fb[1]
        if sample:
            ksT = X[1]
        def q_post(g, pt, pres):
            qd, qr = (Fb[3], "F3") if g % 2 == 0 else (Fb[7], "F7")
            qk_norm(pt, pres, g, gq, qd, qr)
            to_bf_T(qd, qr, qT[:, :, g * 128:g * 128 + R], "X0", g)
        tm_mm(w_in[l], C_Q, 512, hT_lhs, ["hT"], q_post)

        def k_post(g, pt, pres):
            kd, kr = (Fb[0], "F0") if g % 2 == 0 else (Fb[4], "F4")
            qk_norm(pt, pres, g, gk, kd, kr)
            if sample:
                P.op("sp", lambda e: e.dma_start(out=nk_s[l], in_=kd[0:R, 0:512]), reads=[kr], dma="st_k")
            else:
                P.op("sp", lambda e: e.dma_start(out=nk_p[l, b, t0 + g * 128:t0 + (g + 1) * 128, :], in_=kd[:, 0:512]), reads=[kr], dma="st_k")
                to_bf_T(kd, kr, kT[:, :, t0 + g * 128:t0 + (g + 1) * 128], "kT", g)
        tm_mm(w_in[l], C_K, 512, hT_lhs, ["hT"], k_post)

        def v_post(g, pt, pres):
            vd, vr = (Fb[1], "F1") if g % 2 == 0 else (Fb[5], "F5")
            P.op("act", lambda e: e.activation(vd[0:R, 0:512], pt[0:R, 0:512], AF.Copy), reads=[pres], writes=[vr])
            if sample:
                P.op("sp", lambda e: e.dma_start(out=nv_s[l], in_=vd[0:R, 0:512]), reads=[vr], dma="st_v")
            else:
                P.op("sp", lambda e: e.dma_start(out=nv_p[l, b, t0 + g * 128:t0 + (g + 1) * 128, :], in_=vd[:, 0:512]), reads=[vr], dma="st_v")
                P.op("pool", lambda e: e.tensor_copy(vE[:, ti * 4 + g, :, 0:128], vd[:, 0:512].rearrange("p (h e) -> p h e", h=4)), reads=[vr], writes=["vE"])
        tm_mm(w_in[l], C_V, 512, hT_lhs, ["hT"], v_post)

        scale = HD ** -0.5
        if not sample:
            b0 = ti * 4
            ptc = [0]
            scnt = [0]
            nkb = b0 + 4
            accs = [pm[4], pm[5], pm[6]]

            def acc_of(g, c):
                idx = g * 2 + c
                return accs[idx // 3][:, (idx % 3) * 129:(idx % 3) * 129 + 129], f"pm{4 + idx // 3}"

            def qk_exp(h, j):
                g_lo = max(0, j - b0)
                q0 = g_lo * 128
                pts = []
                for c in range(2):
                    bi = scnt[0] % 4
                    scnt[0] += 1
                    sp_, spres = pm[bi], f"pm{bi}"
                    P.op("pe", lambda e, sp_=sp_, c=c, j=j, q0=q0, h=h: e.matmul(sp_[:, q0:T], kT[c * 64:(c + 1) * 64, h, j * 128:(j + 1) * 128],
                                                                              qT[c * 64:(c + 1) * 64, h, q0:T], start=True, stop=True),
                         reads=["kT", "X0"], writes=[spres])
                    pi = ptc[0] % NPT
                    ptc[0] += 1
                    P.op("act", lambda e, sp_=sp_, pi=pi, q0=q0: e.activation(PT[pi][:, q0:T], sp_[:, q0:T], AF.Exp, scale=scale),
                         reads=[spres], writes=[f"PT{pi}"])
                    if j >= b0:
                        P.op("pool", lambda e, pi=pi, q0=q0: e.tensor_tensor(PT[pi][:, q0:q0 + 128], PT[pi][:, q0:q0 + 128], maskb[:, :], ALU.mult),
                             reads=[f"PT{pi}", "maskb"], writes=[f"PT{pi}"])
                    pts.append(pi)
                return (h, j, g_lo, pts)

            def pv(h, j, g_lo, pts):
                for g in range(g_lo, 4):
                    for c in range(2):
                        a_ap, ares = acc_of(g, c)
                        pi = pts[c]
                        last = (j == b0 + g)
                        P.op("pe", lambda e, a_ap=a_ap, pi=pi, g=g, j=j, h=h, last=last, c=c: e.matmul(a_ap, PT[pi][:, g * 128:(g + 1) * 128], vE[:, j, h, 0:129],
                                                                                          start=(j == 0 and (g * 2 + c) % 3 == 0), stop=last, skip_group_check=True),
                             reads=[f"PT{pi}", "vE"], writes=[ares], inc=(c == 1))

            def finalize(h):
                of = Fb[4]
                sq = Fb[5]
                of3 = of[:, 0:512].rearrange("p (g e) -> p g e", g=4)
                sq3 = sq[:, 0:512].rearrange("p (g e) -> p g e", g=4)
                ob4 = hbf[:, 0:512].rearrange("p (g e) -> p g e", g=4)
                for idx_ in range(8):
                    a_, r_ = acc_of(idx_ // 2, idx_ % 2)
                    P.op("dve", lambda e, a_=a_, idx_=idx_: e.reciprocal(small[:, 32 + idx_:33 + idx_], a_[:, 128:129]), reads=[r_], writes=["small32"])
                rd = small[:, 32:40].rearrange("p (g c) -> p g c", c=2)
                P.op("dve", lambda e: e.tensor_scalar(rd[:, :, 1], rd[:, :, 1], lam[:, 0:1], None, ALU.mult), reads=["small32", "lam"], writes=["small32"])
                for g in range(4):
                    a0, r0 = acc_of(g, 0)
                    a1, r1 = acc_of(g, 1)
                    P.op("dve", lambda e, a0=a0, g=g: e.tensor_scalar(of3[:, g, :], a0[:, 0:128], small[:, 32 + 2 * g:33 + 2 * g], None, ALU.mult), reads=[r0, "small32"], writes=["F4"])
                    P.op("dve", lambda e, a1=a1, g=g: e.scalar_tensor_tensor(of3[:, g, :], a1[:, 0:128], small[:, 33 + 2 * g:34 + 2 * g], of3[:, g, :], ALU.mult, ALU.add),
                         reads=[r1, "small32", "F4"], writes=["F4"])
                P.op("pool", lambda e: e.tensor_tensor(sq[:, 0:512], of[:, 0:512], of[:, 0:512], ALU.mult), reads=["F4"], writes=["F5"])
                P.op("dve", lambda e: e.tensor_reduce(small[:, 40:44], sq3, AX.X, ALU.add), reads=["F5"], writes=["small40"])
                P.op("act", lambda e: e.activation(small[:, 44:48], small[:, 40:44], AF.Sqrt, bias=epsb[:, 0:1], scale=1.0 / 128), reads=["small40"], writes=["small44"])
                P.op("dve", lambda e: e.reciprocal(small[:, 48:52], small[:, 44:48]), reads=["small44"], writes=["small48"])
                P.op("dve", lambda e: e.tensor_tensor(sq3, of3, small[:, 48:52].unsqueeze(2).to_broadcast([128, 4, 128]), ALU.mult), reads=["F4", "small48"], writes=["F5"])
                P.op("dve", lambda e: e.tensor_tensor(ob4, sq3, gsub[:, :].unsqueeze(1).to_broadcast([128, 4, 128]), ALU.mult), reads=["F5", "gsub"], writes=["hbfA", "hbfB"])
                for g in range(4):
                    P.op("pe", lambda e, g=g: e.transpose(ptr[:, g * 128:(g + 1) * 128], hbf[:, g * 128:(g + 1) * 128], identb[:, :]), reads=["hbfA", "hbfB", "identb"], writes=["ptrA", "ptrB"], inc=(g == 3))
                P.op("act", lambda e, h=h: e.activation(outT[:, h, 0:512], ptr[:, 0:512], AF.Copy), reads=["ptrA", "ptrB"], writes=["X2"])

            pend = None
            for h in range(NH):
                for j in range(nkb):
                    cur = qk_exp(h, j)
                    if os.environ.get("KNOSKEW"):
                        pv(*cur)
                        if j == nkb - 1:
                            finalize(h)
                        continue
                    if pend is not None:
                        pv(*pend)
                        if pend[1] == nkb - 1:
                            finalize(pend[0])
                    pend = cur
            if pend is not None:
                pv(*pend)
                finalize(pend[0])
        else:
            sample_attention(l, li, qT, kst, vst, outT)
        dump(1, outT)
        merge_branch(1, False)

        pooled = X[0]
        if not sample:
            if ti == 0:
                P.op("dve", lambda e: e.memset(zcT[:, :, 0:15], 0.0), writes=["zcT"])
            else:
                P.op("dve", lambda e: e.tensor_copy(pref[:, :].rearrange("p (m t) -> p m t", m=4), zcT[:, :, T:T + 15]), reads=["zcT"], writes=["pref"])
                P.op("dve", lambda e: e.tensor_copy(zcT[:, :, 0:15], pref[:, :].rearrange("p (m t) -> p m t", m=4)), reads=["pref"], writes=["zcT"])
            def c_post(m, pt, pres):
                P.op("act", lambda e: e.activation(zcT[:, m, 15:15 + T], pt[:, 0:T], AF.Copy), reads=[pres], writes=["zcT"])
            fm_mm(w_in[l], C_C, 512, ["hT"], hT_rhs, c_post)
            if ti == NT - 1:
                for m in range(4):
                    P.op("sp", lambda e, m=m: e.dma_start(out=npool_p[l, b][:, m * 128:(m + 1) * 128].rearrange("t c -> c t"), in_=zcT[:, m, T:T + 15], allow_slow_non_contiguous=True), reads=["zcT"], dma="st_misc")
            L = 15 + T
            for m in range(4):
                cur = zcT[:, m, :]
                cres = "zcT"
                sh = 1
                for step in range(m + 1):
                    dstb = Fb[step % 2]
                    dres = f"F{step % 2}"
                    P.op("dve" if step % 2 == 0 else "pool", lambda e, dstb=dstb, cur=cur, sh=sh: e.tensor_tensor(dstb[:, sh:L], cur[:, sh:L], cur[:, 0:L - sh], ALU.add),
                         reads=[cres], writes=[dres])
                    cur = dstb
                    cres = dres
                    sh *= 2
                win = 2 ** (m + 1)
                P.op("dve", lambda e, cur=cur, m=m, win=win: e.scalar_tensor_tensor(pooled[:, m, 0:T], cur[:, 15:L], 1.0 / win, zcT[:, m, 15:L], ALU.mult, ALU.subtract),
                     reads=[cres, "zcT"], writes=["X0"])
                if ti == 0:
                    P.op("dve", lambda e, cur=cur, m=m: e.tensor_tensor(Fb[2][:, 0:16], cur[:, 15:31], cst[:, 3, m * 16:(m + 1) * 16], ALU.mult), reads=[cres, "cst"], writes=["F2"])
                    P.op("dve", lambda e, m=m: e.tensor_tensor(pooled[:, m, 0:16], Fb[2][:, 0:16], zcT[:, m, 15:31], ALU.subtract), reads=["F2", "zcT"], writes=["X0"])
        else:
            sample_pool(l, pooled)
        for m in range(4):
            pt, pres = nextpm()
            P.op("pe", lambda e, pt=pt, m=m: e.matmul(pt[:, 0:Tn], cwb[:, m, :], pooled[:, m, 0:Tn], start=True, stop=True), reads=["cwb", "X0"], writes=[pres])
            P.op("act", lambda e, pt=pt, m=m: e.activation(outT[:, m, 0:Tn], pt[:, 0:Tn], AF.Copy, scale=csc[:, m:m + 1]), reads=[pres, "csc"], writes=["X2"])
        dump(2, outT)
        merge_branch(2, False)
        dump(3, mergedT)

        def wo_post_of(half):
            def post(g, pt, pres):
                P.op("dve", lambda e: e.tensor_tensor(xres[0:R, g, half * 512:(half + 1) * 512], xres[0:R, g, half * 512:(half + 1) * 512], pt[0:R, 0:512], ALU.add),
                     reads=[pres, "xres"], writes=["xres"])
            return post
        for half in range(2):
            tm_mm(w_o[l], half * 512, 512, lambda k, g: mergedT[:, k, g * 128:g * 128 + R], ["mergedT"], wo_post_of(half))

        norm_to_hT(l, g2b, "g2b", Tn, R)
        if sample:
            sample_ffn_up(l)
        else:
            if ti == 0:
                P.op("dve", lambda e: e.memset(halo[:, :, :], 0.0), writes=["halo"])
            ffn_pend = [None]

            def ffn_finish(j, cfin):
                P.op("act", lambda e, c0=cfin[0][0]: e.activation(c0[:, 0:T], c0[:, 0:T], AF.Silu), reads=[cfin[0][1]], writes=[cfin[0][1]])
                P.op("pool", lambda e, c0=cfin[0][0], c1=cfin[1][0], j=j: e.tensor_tensor(actT[:, j, 0:T], c0[:, 0:T], c1[:, 0:T], ALU.mult), reads=[cfin[0][1], cfin[1][1]], writes=["actT"])

            for jj in range(0, 22, 2):
                srcg, _ = wview(f_up[l], jj * 128, 256)
                srcv, _ = wview(f_up[l], DFF + jj * 128, 256)
                i = state["w"] % NW
                state["w"] += 1
                wbuf = wr[i]
                wres = f"wr{i}"
                dg = wbuf[:, 0:2048].rearrange("p (k n) -> p k n", k=8)
                dv = wbuf[:, 2048:4096].rearrange("p (k n) -> p k n", k=8)
                P.op("sp", lambda e, dg=dg, srcg=srcg: e.dma_start(out=dg, in_=srcg[0]), reads=[srcg[1]], writes=[wres], dma=wres)
                P.op("sp", lambda e, dv=dv, srcv=srcv: e.dma_start(out=dv, in_=srcv[0]), reads=[srcv[1]], writes=[wres], dma=wres)
                for jo in range(2):
                    j = jj + jo
                    cfin = []
                    for part, wt in ((0, dg), (1, dv)):
                        ch = j + 22 * part
                        pt, pres = nextpm()
                        mm_group(pt[:, 0:T], [(wt[:, k, jo * 128:(jo + 1) * 128], hT[:, k, 0:T]) for k in range(8)], [wres, "hT"], pres)
                        fbase = 4 * (j % 2)
                        ub = Fb[fbase + part * 2]
                        ures = f"F{fbase + part * 2}"
                        cb = Fb[fbase + part * 2 + 1]
                        cres = f"F{fbase + part * 2 + 1}"
                        P.op("pool", lambda e, ub=ub, ch=ch: e.tensor_copy(ub[:, 0:2], halo[:, ch, :]), reads=["halo"], writes=[ures])
                        P.op("act", lambda e, ub=ub, pt=pt: e.activation(ub[:, 2:2 + T], pt[:, 0:T], AF.Copy), reads=[pres], writes=[ures])
                        P.op("pool", lambda e, ub=ub, ch=ch: e.tensor_copy(halo[:, ch, :], ub[:, T:T + 2]), reads=[ures], writes=["halo"])
                        P.op("act", lambda e, ub=ub, cb=cb, ch=ch: e.activation(cb[:, 0:T], ub[:, 2:2 + T], AF.Identity, bias=cvw[:, 3, ch:ch + 1], scale=cvw[:, 2, ch:ch + 1]),
                             reads=[ures, "cvw"], writes=[cres])
                        P.op("dve", lambda e, ub=ub, cb=cb, ch=ch: e.scalar_tensor_tensor(cb[:, 0:T], ub[:, 1:1 + T], cvw[:, 1, ch:ch + 1], cb[:, 0:T], ALU.mult, ALU.add),
                             reads=[ures, cres, "cvw"], writes=[cres])
                        P.op("dve", lambda e, ub=ub, cb=cb, ch=ch: e.scalar_tensor_tensor(cb[:, 0:T], ub[:, 0:T], cvw[:, 0, ch:ch + 1], cb[:, 0:T], ALU.mult, ALU.add),
                             reads=[ures, cres, "cvw"], writes=[cres])
                        cfin.append((cb, cres))
                    if ffn_pend[0] is not None:
                        ffn_finish(*ffn_pend[0])
                    ffn_pend[0] = (j, cfin)
            ffn_finish(*ffn_pend[0])
            if ti == NT - 1:
                for r in range(2):
                    P.op("sp", lambda e, r=r: e.dma_start(out=nconv_p[l, b, r].rearrange("(k c) -> c k", c=128), in_=halo[:, :, r], allow_slow_non_contiguous=True), reads=["halo"], dma="st_misc")
        for half in range(2):
            fbanks = ([0, 1, 2, 3] if half == 0 else [4, 5, 6, 0])[:G]
            accs = [pm[i_] for i_ in fbanks]
            kparts = [(0, 8), (8, 8), (16, 6)]
            for pi_, (k0, kcn) in enumerate(kparts):
                src, kc = wview(f_down[l], half * 512, 512, k0, kcn)
                wt, wres = wload(src, kc, 512)
                for g in range(G):
                    for k in range(kc):
                        first = (pi_ == 0 and k == 0)
                        lastk = (pi_ == 2 and k == kc - 1)
                        P.op("pe", lambda e, g=g, k=k, k0=k0, wt=wt, first=first, lastk=lastk: e.matmul(accs[g][0:R, 0:512], actT[:, k0 + k, g * 128:g * 128 + R], wt[:, k, :], start=first, stop=lastk),
                             reads=[wres, "actT"], writes=[f"pm{fbanks[g]}"], inc=(k == kc - 1))
            for g in range(G):
                P.op("dve", lambda e, g=g, half=half: e.tensor_tensor(xres[0:R, g, half * 512:(half + 1) * 512], xres[0:R, g, half * 512:(half + 1) * 512], accs[g][0:R, 0:512], ALU.add),
                     reads=[f"pm{fbanks[g]}", "xres"], writes=["xres"])
        if sample:
            dst = y_sample if last_layer else xmid_s
            P.op("sp", lambda e: e.dma_start(out=dst, in_=xres[0:R, 0, :]), reads=["xres"], writes=["xmid_s"], dma="st_x")
        else:
            dst = (y_prompt if last_layer else xmid)[b, t0:t0 + T, :].rearrange("(g p) d -> p g d", p=128)
            P.op("sp", lambda e: e.dma_start(out=dst, in_=xres[:, :, :]), reads=["xres"], writes=[f"xmid{b}_{ti}"], dma="st_x")

    def subln(src, sres, R, li):
        P.op("act", lambda e: e.activation(junk[0:R, 0:128], src, AF.Square, accum_out=small[0:R, 34:35]), reads=[sres], writes=["hbfA", "hbfB", "small34"])
        P.op("act", lambda e: e.activation(small[0:R, 35:36], small[0:R, 34:35], AF.Sqrt, bias=epsb[0:R, 0:1], scale=1.0 / 128), reads=["small34"], writes=["small35"])
        P.op("dve", lambda e: e.reciprocal(small[0:R, 36:37], small[0:R, 35:36]), reads=["small35"], writes=["small36"])
        P.op("dve", lambda e: e.scalar_tensor_tensor(ob[0:R, :], src, small[0:R, 36:37], gsub[0:R, :], ALU.mult, ALU.mult), reads=[sres, "small36", "gsub"], writes=["ob"])

    def sample_attention(l, li, qT, kst, vst, outT):
        R = NS
        scale = HD ** -0.5
        qf = Fb[3]
        P.op("sp", lambda e: e.dma_start(out=qscr, in_=hbf[0:R, 0:512]), reads=["hbfA", "hbfB"], writes=["qscr"], dma="st_q")
        akv = actT[:, :, :].rearrange("p a b -> p (a b)")
        KV = [akv[:, 0:8192].rearrange("p (d s n) -> p d s n", d=4, s=2),
              wrall[:, 0:2, :].rearrange("p a b -> p (a b)").rearrange("p (d s n) -> p d s n", d=4, s=2)]
        KVres = [["KVa"], ["wr0", "wr1"]]
        KVsem = ["KVa", "KVb"]
        S = Fb[5]
        Pb = PT[0]
        ngrp = NPG // 8
        gcount = [0]
        first = [True]
        state["ring"] = [0, 1, 2]
        for i in range(NS):
            qb = qb2[i % 2]
            qres = f"PT{1 + i % 2}"
            P.op("sp", lambda e, qb=qb, i=i: e.dma_start(out=qb[:, :], in_=qscr[i].partition_broadcast(128)), reads=["qscr"], writes=[qres], dma="ld_" + qres)
            for gi in range(ngrp):
                bi = gcount[0] % 2
                gcount[0] += 1
                for dd in range(4):
                    col = i * (NPG // 2) + gi * 4 + dd
                    wk = KVres[bi] + (["actT"] if first[0] else [])
                    first[0] = False
                    P.op("pool", lambda e, bi=bi, dd=dd, col=col: e.indirect_dma_start(out=KV[bi][:, dd, :, :].rearrange("p s n -> p (s n)"), out_offset=None, in_=cache_kv[l],
                                                                                 in_offset=bass.IndirectOffsetOnAxis(ap=idx[:, col:col + 1], axis=0)),
                         reads=["idx"], writes=wk, dma=KVsem[bi])
                for hf in range(2):
                    tmp = X[hf]
                    tres = f"X{hf}"
                    P.op("dve", lambda e, tmp=tmp, bi=bi, hf=hf, qb=qb: e.tensor_tensor(tmp[:, :, :].rearrange("p (d s) n -> p d s n", s=2), KV[bi][:, hf * 2:(hf + 1) * 2, :, 0:512],
                                                                                   qb[:, :].unsqueeze(1).unsqueeze(1).to_broadcast([128, 2, 2, 512]), ALU.mult),
                         reads=KVres[bi] + [qres], writes=[tres])
                    pg0 = gi * 8 + hf * 4
                    P.op("dve", lambda e, tmp=tmp, pg0=pg0: e.tensor_reduce(S[:, pg0 * 8:(pg0 + 4) * 8], tmp[:, :, :].rearrange("p j (a d) -> p (j a) d", d=HD), AX.X, ALU.add),
                         reads=[tres], writes=["F5"])
                P.op("act", lambda e, gi=gi: e.activation(Pb[:, gi * 64:(gi + 1) * 64], S[:, gi * 64:(gi + 1) * 64], AF.Exp, scale=scale), reads=["F5"], writes=["PT0"])
                for jj in range(8):
                    j = gi * 8 + jj
                    P.op("pe", lambda e, bi=bi, jj=jj, j=j: e.matmul(pm[3][0:8, 0:512], Pb[:, j * 8:(j + 1) * 8], KV[bi][:, jj // 2, jj % 2, 512:1024], start=(j == 0), stop=(j == NPG - 1)),
                         reads=["PT0"] + KVres[bi], writes=["pm3"], inc=(jj == 7))
            P.op("dve", lambda e: e.tensor_reduce(sm2[:, 0:8], Pb[:, 0:NPG * 8].rearrange("p (j a) -> p a j", a=8), AX.X, ALU.add), reads=["PT0"], writes=["sm2a"])
            lp, lres = nextpm()
            P.op("pe", lambda e, lp=lp: e.matmul(lp[0:8, 0:1], sm2[:, 0:8], ones32[:, 0:1], start=True, stop=True), reads=["sm2a", "ones32"], writes=[lres])
            P.op("act", lambda e, lp=lp: e.activation(sm2[0:8, 8:9], lp[0:8, 0:1], AF.Copy), reads=[lres], writes=["sm2b"])
            P.op("sp", lambda e, i=i: e.dma_start(out=lsc[i].rearrange("(a b) -> a b", b=1), in_=sm2[0:8, 8:9], allow_slow_non_contiguous=True), reads=["sm2b"], writes=["lsc"], dma="st_l")
            ob_ = Fb[2] if i % 2 == 0 else Fb[4]
            ores = "F2" if i % 2 == 0 else "F4"
            P.op("act", lambda e, ob_=ob_: e.activation(ob_[0:8, 0:512], pm[3][0:8, 0:512], AF.Copy), reads=["pm3"], writes=[ores])
            P.op("sp", lambda e, ob_=ob_, i=i: e.dma_start(out=osc[i].rearrange("(a b) -> a b", b=512), in_=ob_[0:8, 0:512]), reads=[ores], writes=["osc"], dma="st_o")
        Od = [Fb[4], Fb[5]]
        for c in range(2):
            src = bass.AP(osc_t, c * 512, [[4096, NS], [1152, 4], [1, 128]])
            P.op("sp", lambda e, c=c, src=src: e.dma_start(out=Od[c][0:R, 0:512].rearrange("p (h e) -> p h e", h=4), in_=src), reads=["osc"], writes=[f"F{4 + c}"], dma=f"ld_od{c}")
        P.op("sp", lambda e: e.dma_start(out=sm2[0:R, 16:24], in_=lsc), reads=["lsc"], writes=["sm2c"], dma="ld_l")
        P.op("dve", lambda e: e.tensor_tensor(Fb[2][0:R, 0:512], qf[0:R, 0:512], kst[0:R, 0:512], ALU.mult), reads=["F3", "F0"], writes=["F2"])
        P.op("dve", lambda e: e.tensor_reduce(sm2[0:R, 24:32], Fb[2][0:R, 0:512].rearrange("p (a d) -> p a d", d=HD), AX.X, ALU.add), reads=["F2"], writes=["sm2d"])
        P.op("act", lambda e: e.activation(sm2[0:R, 32:40], sm2[0:R, 24:32], AF.Exp, scale=scale), reads=["sm2d"], writes=["sm2e"])
        P.op("dve", lambda e: e.tensor_tensor(sm2[0:R, 40:48], sm2[0:R, 16:24], sm2[0:R, 32:40], ALU.add), reads=["sm2c", "sm2e"], writes=["sm2f"])
        P.op("dve", lambda e: e.reciprocal(sm2[0:R, 48:56], sm2[0:R, 40:48]), reads=["sm2f"], writes=["sm2g"])
        rd = sm2[0:R, 48:56].rearrange("p (h c) -> p h c", c=2)
        ps_ = sm2[0:R, 32:40].rearrange("p (h c) -> p h c", c=2)
        P.op("dve", lambda e: e.tensor_scalar(rd[:, :, 1], rd[:, :, 1], lam[0:R, 0:1], None, ALU.mult), reads=["sm2g", "lam"], writes=["sm2g"])
        v3 = vst[0:R, 0:512].rearrange("p (h e) -> p h e", h=4)
        t3 = Fb[2][0:R, 0:512].rearrange("p (h e) -> p h e", h=4)
        for c in range(2):
            o3 = Od[c][0:R, 0:512].rearrange("p (h e) -> p h e", h=4)
            P.op("dve", lambda e, c=c: e.tensor_tensor(t3, v3, ps_[:, :, c].unsqueeze(2).to_broadcast([R, 4, 128]), ALU.mult), reads=["F1", "sm2e"], writes=["F2"])
            P.op("dve", lambda e, o3=o3: e.tensor_tensor(o3, o3, t3, ALU.add), reads=["F2", f"F{4 + c}"], writes=[f"F{4 + c}"])
            P.op("dve", lambda e, o3=o3, c=c: e.tensor_tensor(o3, o3, rd[:, :, c].unsqueeze(2).to_broadcast([R, 4, 128]), ALU.mult), reads=["sm2g", f"F{4 + c}"], writes=[f"F{4 + c}"])
        P.op("dve", lambda e: e.tensor_tensor(Fb[4][0:R, 0:512], Fb[4][0:R, 0:512], Fb[5][0:R, 0:512], ALU.add), reads=["F4", "F5"], writes=["F4"])
        for h in range(NH):
            subln(Fb[4][0:R, h * 128:(h + 1) * 128], "F4", R, li)
            P.op("pe", lambda e: e.transpose(ptr[:, 0:R], ob[0:R, :], identb[0:R, 0:R]), reads=["ob", "identb"], writes=["ptrA", "ptrB"])
            P.op("act", lambda e, h=h: e.activation(outT[:, h, 0:R], ptr[:, 0:R], AF.Copy), reads=["ptrA", "ptrB"], writes=["X2"])
        state["ring"] = [0, 1, 2, 3, 4, 5, 6]

    def sample_pool(l, pooled):
        R = NS
        zcs = zcT[:, :, 0:NS * 16].rearrange("p m (i t) -> p m i t", t=16)
        src, kc = wview(w_in[l], C_C, 512)
        wt, wres = wload(src, kc, 512)
        pt, pres = nextpm()
        mm_group(pt[0:R, 0:512], [(hT[:, k, 0:R], wt[:, k, :]) for k in range(8)], [wres, "hT"], pres)
        P.op("act", lambda e, pt=pt: e.activation(Fb[2][0:R, 0:512], pt[0:R, 0:512], AF.Copy), reads=[pres], writes=["F2"])
        P.op("sp", lambda e: e.dma_start(out=npool_s[l, :, 14, :], in_=Fb[2][0:R, 0:512]), reads=["F2"], dma="st_misc")
        P.op("sp", lambda e: e.dma_start(out=npool_s[l, :, 0:14, :], in_=state_pool[l, :, 1:15, :]), dma="st_misc")
        for m in range(4):
            pt, pres = nextpm()
            mm_group(pt[:, 0:R], [(wt[:, k, m * 128:(m + 1) * 128], hT[:, k, 0:R]) for k in range(8)], [wres, "hT"], pres)
            P.op("act", lambda e, pt=pt, m=m: e.activation(zcs[:, m, :, 15], pt[:, 0:R], AF.Copy), reads=[pres], writes=["zcT"])
        for i0 in range(0, NS, 8):
            n = min(8, NS - i0)
            rows = n * 15
            P.op("sp", lambda e, i0=i0, n=n, rows=rows: e.dma_start(out=Fb[0][0:rows, 0:512], in_=state_pool[l, i0:i0 + n].rearrange("i r f -> (i r) f")), writes=["F0"], dma="ld_sp")
            pt, pres = nextpm()
            for m in range(4):
                P.op("pe", lambda e, pt=pt, m=m, rows=rows: e.transpose(pt[:, m * 120:m * 120 + rows], Fb[0][0:rows, m * 128:(m + 1) * 128], cst[0:rows, 0, 0:rows]),
                     reads=["F0", "cst"], writes=[pres], inc=(m == 3))
            for m in range(4):
                P.op("act", lambda e, pt=pt, m=m, i0=i0, n=n, rows=rows: e.activation(zcs[:, m, i0:i0 + n, 0:15], pt[:, m * 120:m * 120 + rows].rearrange("p (i t) -> p i t", t=15), AF.Copy),
                     reads=[pres], writes=["zcT"])
        for m in range(4):
            cur = zcs[:, m, :, :]
            cres = "zcT"
            sh = 1
            for step in range(m + 1):
                dstb = Fb[step % 2][:, 0:NS * 16].rearrange("p (i t) -> p i t", t=16)
                dres = f"F{step % 2}"
                P.op("dve", lambda e, dstb=dstb, cur=cur, sh=sh: e.tensor_tensor(dstb[:, :, sh:16], cur[:, :, sh:16], cur[:, :, 0:16 - sh], ALU.add), reads=[cres], writes=[dres])
                cur = dstb
                cres = dres
                sh *= 2
            win = 2 ** (m + 1)
            P.op("dve", lambda e, cur=cur, m=m, win=win: e.scalar_tensor_tensor(pooled[:, m, 0:R], cur[:, :, 15], 1.0 / win, zcs[:, m, :, 15], ALU.mult, ALU.subtract),
                 reads=[cres, "zcT"], writes=["X0"])

    def sample_ffn_up(l):
        R = NS
        P.op("sp", lambda e: e.dma_start(out=nconv_s[l, :, 0, :], in_=state_conv[l, :, 1, :]), dma="st_misc")
        for jj in range(0, 22, 2):
            srcg, _ = wview(f_up[l], jj * 128, 256)
            srcv, _ = wview(f_up[l], DFF + jj * 128, 256)
            i = state["w"] % NW
            state["w"] += 1
            wbuf = wr[i]
            wres = f"wr{i}"
            dg = wbuf[:, 0:2048].rearrange("p (k n) -> p k n", k=8)
            dv = wbuf[:, 2048:4096].rearrange("p (k n) -> p k n", k=8)
            P.op("sp", lambda e, dg=dg, srcg=srcg: e.dma_start(out=dg, in_=srcg[0]), reads=[srcg[1]], writes=[wres], dma=wres)
            P.op("sp", lambda e, dv=dv, srcv=srcv: e.dma_start(out=dv, in_=srcv[0]), reads=[srcv[1]], writes=[wres], dma=wres)
            for part, wt in ((0, dg), (1, dv)):
                c0 = part * DFF + jj * 128
                P.op("sp", lambda e, part=part, c0=c0: e.dma_start(out=stg[0:2 * R, part, :], in_=state_conv[l, :, :, c0:c0 + 256].rearrange("i r f -> (i r) f")), writes=["F0"], dma="ld_stg")
                pt, pres = nextpm()
                mm_group(pt[0:R, 0:256], [(hT[:, k, 0:R], wt[:, k, :]) for k in range(8)], [wres, "hT"], pres)
                P.op("act", lambda e, pt=pt, part=part: e.activation(upst[0:R, part, :], pt[0:R, 0:256], AF.Copy), reads=[pres], writes=["F2"])
                P.op("sp", lambda e, part=part, c0=c0: e.dma_start(out=nconv_s[l, :, 1, c0:c0 + 256], in_=upst[0:R, part, :]), reads=["F2"], dma="st_up")
            for jo in range(2):
                j = jj + jo
                cfin = []
                for part, wt in ((0, dg), (1, dv)):
                    ch = j + 22 * part
                    pt, pres = nextpm()
                    mm_group(pt[:, 0:R], [(wt[:, k, jo * 128:(jo + 1) * 128], hT[:, k, 0:R]) for k in range(8)], [wres, "hT"], pres)
                    p2, p2res = nextpm()
                    P.op("pe", lambda e, p2=p2, part=part, jo=jo: e.transpose(p2[:, 0:2 * R], stg[0:2 * R, part, jo * 128:(jo + 1) * 128], cst[0:2 * R, 0, 0:2 * R]),
                         reads=["F0", "cst"], writes=[p2res])
                    cb = Fb[part * 2 + 1]
                    cres = f"F{part * 2 + 1}"
                    st2 = p2[:, 0:2 * R].rearrange("p (i r) -> p i r", r=2)
                    P.op("act", lambda e, cb=cb, pt=pt, ch=ch: e.activation(cb[:, 0:R], pt[:, 0:R], AF.Identity, bias=cvw[:, 3, ch:ch + 1], scale=cvw[:, 2, ch:ch + 1]),
                         reads=[pres, "cvw"], writes=[cres])
                    P.op("dve", lambda e, cb=cb, st2=st2, ch=ch: e.scalar_tensor_tensor(cb[:, 0:R], st2[:, :, 1], cvw[:, 1, ch:ch + 1], cb[:, 0:R], ALU.mult, ALU.add),
                         reads=[p2res, cres, "cvw"], writes=[cres])
                    P.op("dve", lambda e, cb=cb, st2=st2, ch=ch: e.scalar_tensor_tensor(cb[:, 0:R], st2[:, :, 0], cvw[:, 0, ch:ch + 1], cb[:, 0:R], ALU.mult, ALU.add),
                         reads=[p2res, cres, "cvw"], writes=[cres])
                    cfin.append((cb, cres))
                P.op("act", lambda e, c0_=cfin[0][0]: e.activation(Fb[4][:, 0:R], c0_[:, 0:R], AF.Silu), reads=[cfin[0][1]], writes=["F4"])
                P.op("dve", lambda e, c1=cfin[1][0], j=j: e.tensor_tensor(actT[:, j, 0:R], Fb[4][:, 0:R], c1[:, 0:R], ALU.mult), reads=["F4", cfin[1][1]], writes=["actT"])

    convert_layer(0)
    for l in range(DEPTH):
        li = layer_setup(l)
        for b in range(NB):
            for ti in range(NT):
                tile_layer(l, li, b, ti)
                if b == 0 and ti == 0 and l + 1 < DEPTH:
                    convert_layer(l + 1)
        if NS:
            tile_layer(l, li, 0, 0, sample=True)
    print("nrec", P.nrec, {e: len(P.ops[e]) for e in ENGS})
    P.emit()
    st.close()
    return nc


def make_consts():
    c = np.zeros((128, 4, 128), np.float32)
    c[:, 0, :] = np.eye(128, dtype=np.float32)
    s = np.arange(128)[:, None]
    t = np.arange(128)[None, :]
    c[:, 1, :] = (t <= s).astype(np.float32)
    c[:, 2, :] = (s <= t).astype(np.float32)
    for m in range(4):
        win = 2 ** (m + 1)
        for p in range(16):
            c[:, 3, m * 16 + p] = 1.0 / min(p + 1, win)
    c[:, 3, 64] = (np.arange(128) % 64).astype(np.float32)
    return c


_W_NAMES = ['norm1_g', 'w_in', 'a_vnorm_g', 'a_ws', 'a_bs', 'b_qnorm_g', 'b_knorm_g', 'b_lq1', 'b_lk1', 'b_lq2', 'b_lk2',
            'b_subln_g', 'c_w', 'c_scale', 'p_a', 'p_b', 'p_c', 'w_o', 'norm2_g', 'f_up', 'f_conv_w', 'f_conv_b', 'f_down']


def kernel(**inputs):
    f32 = np.float32
    x_prompt = np.ascontiguousarray(np.asarray(inputs['x_prompt'], dtype=f32))
    B, SEQ, _ = x_prompt.shape
    x_sample = np.asarray(inputs['x_sample'], dtype=f32)
    n_dec = x_sample.shape[0]
    page_table = np.asarray(inputs['page_table']).astype(np.int32)
    n_cores = B
    NS = n_dec // n_cores
    with_samples = not os.environ.get("KNOSAMPLE")
    cache_k = np.asarray(inputs['cache_k'], dtype=f32)
    cache_v = np.asarray(inputs['cache_v'], dtype=f32)
    n_pool = cache_k.shape[1]
    npg = page_table.shape[1]
    cfg = Cfg(NB=1, SEQ=SEQ, NS=(NS if with_samples else 0), NPOOL=n_pool, NPG=npg, PGRP=8)
    nc = build(cfg)
    consts = make_consts()
    weights = {k: np.ascontiguousarray(np.asarray(inputs[k], dtype=f32)) for k in _W_NAMES}
    state_pool = np.asarray(inputs['state_pool'], dtype=f32)
    state_conv = np.asarray(inputs['state_conv'], dtype=f32)
    kv_l = [np.concatenate([cache_k[l].reshape(n_pool * PAGE, 512), cache_v[l].reshape(n_pool * PAGE, 512)], axis=1).reshape(n_pool * 64, 2048)
            for l in range(DEPTH)] if with_samples else None
    in_maps = []
    for c in range(n_cores):
        m = dict(weights)
        m['x_prompt'] = x_prompt[c:c + 1]
        m['consts'] = consts
        if with_samples:
            sl = slice(c * NS, (c + 1) * NS)
            m['x_sample'] = np.ascontiguousarray(x_sample[sl, 0, :])
            m['page_table'] = np.ascontiguousarray(page_table[sl])
            m['state_pool'] = np.ascontiguousarray(state_pool[:, sl])
            m['state_conv'] = np.ascontiguousarray(state_conv[:, sl])
            for l in range(DEPTH):
                m[f'cache_kv{l}'] = kv_l[l]
        in_maps.append(m)
    res = run_bass_kernel_spmd(nc, in_maps, core_ids=list(range(n_cores))).results
    y_p = np.concatenate([r['y_prompt'].reshape(1, SEQ, D) for r in res], axis=0)
    nk = np.concatenate([r['nk_p'].reshape(DEPTH, 1, SEQ, NH, 128) for r in res], axis=1)
    nv = np.concatenate([r['nv_p'].reshape(DEPTH, 1, SEQ, NH, 128) for r in res], axis=1)
    npool = np.concatenate([r['npool_p'].reshape(DEPTH, 1, 15, WC) for r in res], axis=1)
    nconv = np.concatenate([r['nconv_p'].reshape(DEPTH, 1, 2, 2 * DFF) for r in res], axis=1)
    if with_samples:
        y_s = np.concatenate([r['y_sample'].reshape(NS, 1, D) for r in res], axis=0)
        nk_s = np.concatenate([r['nk_s'].reshape(DEPTH, NS, 1, NH, 128) for r in res], axis=1)
        nv_s = np.concatenate([r['nv_s'].reshape(DEPTH, NS, 1, NH, 128) for r in res], axis=1)
        ncv_s = np.concatenate([r['ncv_s'].reshape(DEPTH, NS, 1, WA) for r in res], axis=1)
        npool_s = np.concatenate([r['npool_s'].reshape(DEPTH, NS, 15, WC) for r in res], axis=1)
        nconv_s = np.concatenate([r['nconv_s'].reshape(DEPTH, NS, 2, 2 * DFF) for r in res], axis=1)
    else:
        y_s = np.zeros((n_dec, 1, D), f32)
        nk_s = np.zeros((DEPTH, n_dec, 1, NH, 128), f32)
        nv_s = np.zeros((DEPTH, n_dec, 1, NH, 128), f32)
        ncv_s = np.zeros((DEPTH, n_dec, 1, WA), f32)
        npool_s = np.zeros((DEPTH, n_dec, 15, WC), f32)
        nconv_s = np.zeros((DEPTH, n_dec, 2, 2 * DFF), f32)
    return (y_p.astype(f32), y_s.astype(f32), nk.astype(f32), nv.astype(f32), nk_s.astype(f32), nv_s.astype(f32), ncv_s.astype(f32),
            npool.astype(f32), npool_s.astype(f32), nconv.astype(f32), nconv_s.astype(f32))
```

```python
import math
from contextlib import ExitStack

import numpy as np
import concourse.bass as bass
import concourse.mybir as mybir
from concourse.bass_utils import run_bass_kernel_spmd

F32 = mybir.dt.float32
BF16 = mybir.dt.bfloat16
I32 = mybir.dt.int32
ALU = mybir.AluOpType
AF = mybir.ActivationFunctionType
AX = mybir.AxisListType

D = 1024
WA = 512
HD = 64
NH = 4
WB = 512
WC = 512
NIN = 6144
DFF = 2816
DEPTH = 2
EPS = 1e-6
PAGE = 128
C_U, C_VA, C_Q, C_K, C_V, C_C, C_G = 0, 512, 1024, 1536, 2048, 2560, 3072

ENGS = ("pe", "act", "dve", "pool", "sp")
import os
KSTOP = int(os.environ.get("KSTOP", "100000000"))


class Prog:
    def __init__(self, nc):
        self.nc = nc
        self.ops = {e: [] for e in ENGS}
        self.cnt = {e: 0 for e in ENGS}
        self.last_w = {}
        self.readers = {}
        self.seen = {e: {} for e in ENGS}
        self.dma_cnt = {}

    def op(self, eng, fn, reads=(), writes=(), dma=None, inc=True):
        self.nrec = getattr(self, "nrec", 0) + 1
        if self.nrec > KSTOP:
            return None
        waits = {}

        def need(tok):
            if tok is None:
                return
            k, v = tok
            if k == "pe" and eng == "pe" and dma is None:
                return
            if v > waits.get(k, 0):
                waits[k] = v

        for r in reads:
            need(self.last_w.get(r))
        for w in writes:
            need(self.last_w.get(w))
            for t in self.readers.get(w, ()):
                need(t)
        wl = []
        for k, v in waits.items():
            if self.seen[eng].get(k, 0) >= v:
                continue
            self.seen[eng][k] = v
            wl.append((k, v))
        if dma is not None:
            self.dma_cnt[dma] = self.dma_cnt.get(dma, 0) + 16
            tok = (dma, self.dma_cnt[dma])
            do_inc = True
        elif inc:
            self.cnt[eng] += 1
            tok = (eng, self.cnt[eng])
            do_inc = True
        else:
            tok = (eng, self.cnt[eng] + 1)
            do_inc = False
        self.ops[eng].append((wl, fn, tok, do_inc, dma is not None))
        for r in reads:
            self.readers.setdefault(r, []).append(tok)
        for w in writes:
            self.last_w[w] = tok
            self.readers[w] = []
        return tok

    def emit(self):
        nc = self.nc
        keys = list(ENGS) + sorted(self.dma_cnt.keys())
        with ExitStack() as st:
            sems = {k: st.enter_context(nc.semaphore("s_" + k)) for k in keys}
            block = st.enter_context(nc.Block())
            names = {"pe": "tensor", "act": "scalar", "dve": "vector", "pool": "gpsimd", "sp": "sync"}
            for eng in ENGS:
                def run(e, eng=eng):
                    for wl, fn, tok, do_inc, is_dma in self.ops[eng]:
                        for k, v in wl:
                            e.wait_ge(sems[k], v)
                        ins = fn(e)
                        if do_inc:
                            ins.then_inc(sems[tok[0]], 16 if is_dma else 1)
                    if eng == "sp":
                        for k in keys:
                            tot = self.cnt[k] if k in self.cnt else self.dma_cnt[k]
                            if tot > 0:
                                e.wait_ge(sems[k], tot)
                getattr(block, names[eng])(run)


class Cfg:
    def __init__(self, NB, SEQ, NS, NPOOL, NPG, PGRP):
        self.NB, self.SEQ, self.NS, self.NPOOL, self.NPG, self.PGRP = NB, SEQ, NS, NPOOL, NPG, PGRP


def build(cfg):
    NB, SEQ, NS, NPOOL, NPG, PGRP = cfg.NB, cfg.SEQ, cfg.NS, cfg.NPOOL, cfg.NPG, cfg.PGRP
    T = 512
    NT = SEQ // T
    NG = SEQ // 128
    nc = bass.Bass("TRN2", target_bir_lowering=False)
    P = Prog(nc)

    def din(name, shape, dt=F32):
        return nc.dram_tensor(name, list(shape), dt, kind="ExternalInput").ap()

    def dout(name, shape):
        return nc.dram_tensor(name, list(shape), F32, kind="ExternalOutput").ap()

    x_prompt = din("x_prompt", [NB, SEQ, D])
    norm1_g = din("norm1_g", [DEPTH, D])
    w_in = din("w_in", [DEPTH, D, NIN])
    a_vnorm_g = din("a_vnorm_g", [DEPTH, WA])
    a_ws = din("a_ws", [DEPTH, 4, 128, 128])
    a_bs = din("a_bs", [DEPTH, 4, 128])
    b_qnorm_g = din("b_qnorm_g", [DEPTH, HD])
    b_knorm_g = din("b_knorm_g", [DEPTH, HD])
    b_lq1 = din("b_lq1", [DEPTH, HD])
    b_lk1 = din("b_lk1", [DEPTH, HD])
    b_lq2 = din("b_lq2", [DEPTH, HD])
    b_lk2 = din("b_lk2", [DEPTH, HD])
    b_subln_g = din("b_subln_g", [DEPTH, 128])
    c_w = din("c_w", [DEPTH, 4, 128, 128])
    c_scale = din("c_scale", [DEPTH, WC])
    p_abc = [din("p_a", [DEPTH, 512, D]), din("p_b", [DEPTH, 512, D]), din("p_c", [DEPTH, 512, D])]
    w_o = din("w_o", [DEPTH, D, D])
    norm2_g = din("norm2_g", [DEPTH, D])
    f_up = din("f_up", [DEPTH, D, 2 * DFF])
    f_conv_w = din("f_conv_w", [DEPTH, 3, 2 * DFF])
    f_conv_b = din("f_conv_b", [DEPTH, 2 * DFF])
    f_down = din("f_down", [DEPTH, DFF, D])
    consts = din("consts", [128, 4, 128])
    if NS:
        x_sample = din("x_sample", [NS, D])
        cache_kv = [din(f"cache_kv{i}", [NPOOL * 64, 2048]) for i in range(DEPTH)]
        page_table = din("page_table", [NS, NPG], I32)
        state_pool = din("state_pool", [DEPTH, NS, 15, WC])
        state_conv = din("state_conv", [DEPTH, NS, 2, 2 * DFF])

    y_prompt = dout("y_prompt", [NB, SEQ, D])
    nk_p = dout("nk_p", [DEPTH, NB, SEQ, 512])
    nv_p = dout("nv_p", [DEPTH, NB, SEQ, 512])
    npool_p = dout("npool_p", [DEPTH, NB, 15, WC])
    nconv_p = dout("nconv_p", [DEPTH, NB, 2, 2 * DFF])
    if NS:
        y_sample = dout("y_sample", [NS, D])
        nk_s = dout("nk_s", [DEPTH, NS, 512])
        nv_s = dout("nv_s", [DEPTH, NS, 512])
        ncv_s = dout("ncv_s", [DEPTH, NS, WA])
        npool_s = dout("npool_s", [DEPTH, NS, 15, WC])
        nconv_s = dout("nconv_s", [DEPTH, NS, 2, 2 * DFF])
    xmid = nc.dram_tensor("xmid", [NB, SEQ, D], F32, kind="Internal").ap()
    DBG = bool(os.environ.get("KDBG"))
    if DBG:
        dbg = dout("dbg", [4, 128, 4 * 512])
    if NS:
        xmid_s = nc.dram_tensor("xmid_s", [NS, D], F32, kind="Internal").ap()
        qscr = nc.dram_tensor("qscr", [NS, 512], BF16, kind="Internal").ap()
        osc_t = nc.dram_tensor("osc", [NS, 8 * 512], F32, kind="Internal")
        osc = osc_t.ap()
        lsc = nc.dram_tensor("lsc", [NS, 8], F32, kind="Internal").ap()

    class W2D:
        def __init__(self, name, l, f32ap, bfap):
            self.name, self.l, self.f32, self.bf = name, l, f32ap, bfap
            self.res = f"cv_{name}{l}"

    class WT:
        def __init__(self, name, ap):
            self.name, self.ap = name, ap
            self.bf = nc.dram_tensor("bf_" + name, list(ap.shape), BF16, kind="Internal").ap()
        def __getitem__(self, l):
            return W2D(self.name, l, self.ap[l], self.bf[l])

    w_in = WT("w_in", w_in)
    p_abc = [WT(n, a) for n, a in zip(("p_a", "p_b", "p_c"), p_abc)]
    w_o = WT("w_o", w_o)
    f_up = WT("f_up", f_up)
    f_down = WT("f_down", f_down)

    def convert_layer(l):
        def cv(w, c0, c1):
            w2 = w[l]
            P.op("pool", lambda e: e.dma_start(out=w2.bf[:, c0:c1], in_=w2.f32[:, c0:c1]), writes=[w2.res], dma=w2.res)
        for c0, c1 in ((0, 1024), (1024, 2560), (2560, 4096), (4096, 6144)):
            cv(w_in, c0, c1)
        for w in p_abc:
            cv(w, 0, 1024)
        cv(w_o, 0, 1024)
        for c0, c1 in ((0, 2048), (2048, 4096), (4096, 5632)):
            cv(f_up, c0, c1)
        cv(f_down, 0, 1024)

    st = ExitStack()

    def sb(name, shape, dt=F32):
        return st.enter_context(nc.sbuf_tensor(name, list(shape), dt))

    def ps(name, shape, dt=F32):
        return st.enter_context(nc.psum_tensor(name, list(shape), dt))

    xres = sb("xres", [128, 4, D])
    hbf = sb("hbf", [128, D], BF16)
    junk = hbf
    hT = sb("hT", [128, 8, T], BF16)
    kT = sb("kT", [128, 4, SEQ], BF16)
    vE = sb("vE", [128, NG, 4, 130], BF16)
    NW = 3
    wrall = sb("wrall", [128, NW, 4096], BF16)
    wr = [wrall[:, i, :] for i in range(NW)]
    mergedT = sb("mergedT", [128, 8, T], BF16)
    X = [sb(f"X{i}", [128, 4, T], BF16) for i in range(3)]
    Fb = [sb(f"F{i}", [128, 528]) for i in range(8)]
    zcT = sb("zcT", [128, 4, 15 + T])
    NPT = 8
    PT = [sb(f"PT{i}", [128, T], BF16) for i in range(NPT)]
    actT = sb("actT", [128, 22, T], BF16)
    halo = sb("halo", [128, 44, 2])
    small = sb("small", [128, 64])
    epsb = sb("epsb", [128, 1])
    pref = sb("pref", [128, 60])
    smq = sb("smq", [128, 32])
    cst = sb("cst", [128, 4, 128])
    identb = sb("identb", [128, 128], BF16)
    maskb = sb("maskb", [128, 128], BF16)
    g1b = sb("g1T", [128, 8])
    g2b = sb("g2T", [128, 8])
    avgb = sb("avgb", [128, WA])
    gqk = sb("gqk", [128, 2, HD])
    lqk = sb("lqk", [128, 4, HD])
    gsub = sb("gsub", [128, 128])
    lam = sb("lam", [128, 4])
    wsraw = Fb[1][:, 0:512].rearrange("p (m s) -> p m s", m=4)
    wmT = sb("wmT", [128, 4, 128], BF16)
    bsT = sb("bsT", [128, 4, 128])
    cwb = sb("cwb", [128, 4, 128], BF16)
    csc = sb("csc", [128, 4])
    cvw = sb("cvw", [128, 4, 44])
    ob = sb("ob", [128, 128], BF16)
    ofin = sb("ofin", [128, 2, 128])
    if NS:
        NIDX = NS * NPG // 2
        idx = sb("idx", [128, NIDX], I32)
        assert NIDX <= 512 and NPG % 8 == 0
        idxf = Fb[0][:, 0:NIDX]
        qb2 = [PT[1], PT[2]]
        sm2 = sb("sm2", [128, 64])
        ones32 = sb("ones32", [128, 1])
        stg = Fb[0][:, 0:512].rearrange("p (a b) -> p a b", a=2)
        upst = Fb[2][:, 0:512].rearrange("p (a b) -> p a b", a=2)

    pm = [ps(f"pm{i}", [128, 512]) for i in range(7)]
    ptr = ps("ptr", [128, 1024], BF16)

    state = {"w": 0, "pm": 0, "gel": 0, "mrg": 0, "ring": [0, 1, 2, 3, 4, 5, 6]}

    def wload(src, kc, ncols):
        src_ap, cres = src
        i = state["w"] % NW
        state["w"] += 1
        buf = wr[i]
        dst = buf[:, 0:kc * ncols].rearrange("p (k n) -> p k n", k=kc)
        P.op("sp", lambda e: e.dma_start(out=dst, in_=src_ap), reads=[cres], writes=[f"wr{i}"], dma=f"wr{i}")
        return dst, f"wr{i}"

    def wview(w2d, c0, ncols, k0=0, kc=None):
        v = w2d.bf.rearrange("(k p) n -> p k n", p=128)
        if kc is None:
            kc = v.shape[1] - k0
        return (v[:, k0:k0 + kc, c0:c0 + ncols], w2d.res), kc

    def mm_group(out_ap, pairs, reads, wres):
        n = len(pairs)
        for i, (l, r) in enumerate(pairs):
            P.op("pe", lambda e, l=l, r=r, i=i: e.matmul(out_ap, l, r, start=(i == 0), stop=(i == n - 1)),
                 reads=reads, writes=[wres], inc=(i == n - 1))

    def norm_to_hT(l, gb, gname, Tn, R):
        G = (Tn + 127) // 128
        for g in range(G):
            P.op("act", lambda e, g=g: e.activation(junk[0:R, :], xres[0:R, g, :], AF.Square, accum_out=small[0:R, 0:1]),
                 reads=["xres"], writes=["hbfA", "hbfB", "small0"])
            P.op("act", lambda e: e.activation(small[0:R, 1:2], small[0:R, 0:1], AF.Sqrt, bias=epsb[0:R, 0:1], scale=1.0 / D),
                 reads=["small0"], writes=["small1"])
            P.op("dve", lambda e: e.reciprocal(small[0:R, 2:3], small[0:R, 1:2]),
                 reads=["small1"], writes=["small2"])
            P.op("dve", lambda e, g=g: e.tensor_scalar(hbf[0:R, :], xres[0:R, g, :], small[0:R, 2:3], None, ALU.mult),
                 reads=["xres", "small2"], writes=["hbfA", "hbfB"])
            for kc in range(8):
                P.op("pe", lambda e, kc=kc: e.transpose(ptr[:, kc * 128:kc * 128 + R], hbf[0:R, kc * 128:(kc + 1) * 128], identb[0:R, 0:R]),
                     reads=["hbfA", "hbfB", "identb"], writes=["ptrA", "ptrB"], inc=(kc == 7))
            P.op("dve", lambda e, g=g: e.tensor_tensor(hT[:, :, g * 128:g * 128 + R], ptr[:, :].rearrange("p (k t) -> p k t", k=8)[:, :, 0:R],
                                                   gb[:, :].unsqueeze(2).to_broadcast([128, 8, R]), ALU.mult),
                 reads=["ptrA", "ptrB", gname], writes=["hT"])

    def nextpm():
        ring = state["ring"]
        i = ring[state["pm"] % len(ring)]
        state["pm"] += 1
        return pm[i], f"pm{i}"

    def layer_setup(l):
        li = 0.8 - 0.6 * math.exp(-0.3 * l)
        q = "sp"
        def ld(dst, src, res):
            P.op(q, lambda e: e.dma_start(out=dst, in_=src, allow_slow_non_contiguous=True), writes=[res], dma="ld_" + res)
        ld(g1b[:, :], norm1_g[l].rearrange("(k p) -> p k", p=128), "g1b")
        ld(g2b[:, :], norm2_g[l].rearrange("(k p) -> p k", p=128), "g2b")
        ld(avgb[:, :], a_vnorm_g[l].partition_broadcast(128), "avgb")
        ld(gqk[:, 0, :], b_qnorm_g[l].partition_broadcast(128), "gqk")
        ld(gqk[:, 1, :], b_knorm_g[l].partition_broadcast(128), "gqk")
        ld(lqk[:, 0, :], b_lq1[l].partition_broadcast(128), "lqk")
        ld(lqk[:, 1, :], b_lk1[l].partition_broadcast(128), "lqk")
        ld(lqk[:, 2, :], b_lq2[l].partition_broadcast(128), "lqk")
        ld(lqk[:, 3, :], b_lk2[l].partition_broadcast(128), "lqk")
        ld(gsub[:, :], b_subln_g[l].partition_broadcast(128), "gsub")
        P.op("dve", lambda e: e.tensor_scalar(gsub[:, :], gsub[:, :], (1.0 - li), None, ALU.mult), reads=["gsub"], writes=["gsub"])
        ld(wsraw, a_ws[l].rearrange("m t s -> t m s"), "F1")
        ld(bsT[:, :, :].rearrange("p m t -> p (m t)"), a_bs[l].rearrange("m t -> (m t)").partition_broadcast(128), "bsT")
        ld(Fb[0][:, 0:512].rearrange("p (m d) -> p m d", m=4), c_w[l].rearrange("m c d -> c m d"), "F0")
        ld(csc[:, :], c_scale[l].rearrange("(m c) -> c m", c=128), "csc")
        if NS:
            ld(small[:, 16:20], a_ws[l][:, 0, 0].partition_broadcast(128), "small16")
            ld(small[:, 20:24], a_bs[l][:, 0].partition_broadcast(128), "small16")
        for j in range(3):
            ld(cvw[:, j, :], f_conv_w[l, j].rearrange("(k c) -> c k", c=128), "cvw")
        ld(cvw[:, 3, :], f_conv_b[l].rearrange("(k c) -> c k", c=128), "cvw")
        P.op("dve", lambda e: e.tensor_copy(cwb[:, :, :].rearrange("p m d -> p (m d)"), Fb[0][:, 0:512]), reads=["F0"], writes=["cwb"])
        P.op("dve", lambda e: e.tensor_tensor(lqk[:, 0, :], lqk[:, 0, :], lqk[:, 1, :], ALU.mult), reads=["lqk"], writes=["lqk"])
        P.op("dve", lambda e: e.tensor_tensor(lqk[:, 2, :], lqk[:, 2, :], lqk[:, 3, :], ALU.mult), reads=["lqk"], writes=["lqk"])
        P.op("dve", lambda e: e.tensor_reduce(small[:, 8:9], lqk[:, 0, :], AX.X, ALU.add), reads=["lqk"], writes=["small8"])
        P.op("dve", lambda e: e.tensor_reduce(small[:, 9:10], lqk[:, 2, :], AX.X, ALU.add), reads=["lqk"], writes=["small9"])
        P.op("act", lambda e: e.activation(small[:, 10:12], small[:, 8:10], AF.Exp), reads=["small8", "small9"], writes=["small10"])
        P.op("dve", lambda e: e.tensor_tensor(small[:, 12:13], small[:, 11:12], small[:, 10:11], ALU.subtract), reads=["small10"], writes=["small12"])
        P.op("dve", lambda e: e.tensor_scalar(lam[:, 0:1], small[:, 12:13], -li, None, ALU.add), reads=["small12"], writes=["lam"])
        for m in range(4):
            P.op("dve", lambda e, m=m: e.tensor_tensor(hbf[:, m * 128:(m + 1) * 128], wsraw[:, m, :], cst[:, 1, :], ALU.mult),
                 reads=["F1", "cst"], writes=["hbfA", "hbfB"])
        for m in range(4):
            P.op("pe", lambda e, m=m: e.transpose(ptr[:, m * 128:(m + 1) * 128], hbf[:, m * 128:(m + 1) * 128], identb[:, :]),
                 reads=["hbfA", "hbfB", "identb"], writes=["ptrA", "ptrB"], inc=(m == 3))
        P.op("act", lambda e: e.activation(wmT[:, :, :].rearrange("p m t -> p (m t)"), ptr[:, 0:512], AF.Copy), reads=["ptrA", "ptrB"], writes=["wmT"])
        return li

    P.op("sp", lambda e: e.dma_start(out=cst[:, :, :], in_=consts), writes=["cst"], dma="ld_cst")
    P.op("dve", lambda e: e.tensor_copy(identb[:, :], cst[:, 0, :]), reads=["cst"], writes=["identb"])
    P.op("dve", lambda e: e.tensor_copy(maskb[:, :], cst[:, 2, :]), reads=["cst"], writes=["maskb"])
    P.op("dve", lambda e: e.memset(vE[:, :, :, 128:130], 1.0), writes=["vE"])
    P.op("dve", lambda e: e.memset(epsb[:, :], EPS), writes=["epsb"])
    if NS:
        P.op("dve", lambda e: e.memset(ones32[:, :], 1.0), writes=["ones32"])
        ptv = page_table.rearrange("a (j two) -> two (a j)", two=2)
        P.op("sp", lambda e: e.dma_start(out=idx[0:64, :], in_=ptv[0].partition_broadcast(64), allow_slow_non_contiguous=True), writes=["idx"], dma="ld_idx")
        P.op("sp", lambda e: e.dma_start(out=idx[64:128, :], in_=ptv[1].partition_broadcast(64), allow_slow_non_contiguous=True), writes=["idx"], dma="ld_idx")
        P.op("dve", lambda e: e.tensor_copy(idxf[:, :], idx[:, :]), reads=["idx"], writes=["F0"])
        P.op("dve", lambda e: e.tensor_scalar(idxf[:, :], idxf[:, :], 64.0, cst[:, 3, 64:65], ALU.mult, ALU.add), reads=["F0", "cst"], writes=["F0"])
        P.op("dve", lambda e: e.tensor_copy(idx[:, :], idxf[:, :]), reads=["F0"], writes=["idx"])

    def tile_layer(l, li, b, ti, sample=False):
        last_layer = (l == DEPTH - 1)
        if sample:
            Tn, R, G = NS, NS, 1
        else:
            Tn, R, G = T, 128, 4
        t0 = ti * T
        if sample:
            src = (x_sample if l == 0 else xmid_s)
            P.op("sp", lambda e, src=src: e.dma_start(out=xres[0:R, 0, :], in_=src), reads=["xmid_s"], writes=["xres"], dma="ld_x")
        else:
            src = (x_prompt if l == 0 else xmid)[b, t0:t0 + T, :].rearrange("(g p) d -> p g d", p=128)
            P.op("sp", lambda e, src=src: e.dma_start(out=xres[:, :, :], in_=src), reads=[f"xmid{b}_{ti}"], writes=["xres"], dma="ld_x")
        norm_to_hT(l, g1b, "g1b", Tn, R)

        def fm_mm(wsrc2d, c0, ncols, kin_res, rhs_of, post):
            src, kc = wview(wsrc2d, c0, ncols)
            wt, wres = wload(src, kc, ncols)
            for m in range(ncols // 128):
                pt, pres = nextpm()
                mm_group(pt[:, 0:Tn], [(wt[:, k, m * 128:(m + 1) * 128], rhs_of(k)) for k in range(kc)], [wres] + kin_res, pres)
                post(m, pt, pres)

        def tm_mm(wsrc2d, c0, ncols, lhs_of, kin_res, post, k0=0, kcn=None):
            src, kc = wview(wsrc2d, c0, ncols, k0, kcn)
            wt, wres = wload(src, kc, ncols)
            outs = []
            for g in range(G):
                pt, pres = nextpm()
                mm_group(pt[0:R, 0:ncols], [(lhs_of(k, g), wt[:, k, :]) for k in range(kc)], [wres] + kin_res, pres)
                outs.append((pt, pres))
            for g, (pt, pres) in enumerate(outs):
                post(g, pt, pres)

        hT_rhs = lambda k: hT[:, k, 0:Tn]
        hT_lhs = lambda k, g: hT[:, k, g * 128:g * 128 + R]
        uT, va, outT = X[0], X[1], X[2]

        def gelu_post(dst_of):
            def post(m, pt, pres):
                rows = pt.shape[0] if False else None
                ia, ib = ((2, 3), (6, 7))[state["gel"] % 2]
                state["gel"] += 1
                a = Fb[ia]; bq = Fb[ib]
                ra, rb = f"F{ia}", f"F{ib}"
                n = Tn if dst_of[0] == "fm" else 512
                rr = 128 if dst_of[0] == "fm" else R
                P.op("act", lambda e: e.activation(a[0:rr, 0:n], pt[0:rr, 0:n], AF.Square), reads=[pres], writes=[ra])
                P.op("pool", lambda e: e.tensor_scalar(a[0:rr, 0:n], a[0:rr, 0:n], 0.044715, 1.0, ALU.mult, ALU.add), reads=[ra], writes=[ra])
                P.op("dve", lambda e: e.tensor_tensor(a[0:rr, 0:n], a[0:rr, 0:n], pt[0:rr, 0:n], ALU.mult), reads=[ra, pres], writes=[ra])
                P.op("act", lambda e: e.activation(bq[0:rr, 0:n], a[0:rr, 0:n], AF.Sigmoid, scale=1.5957691216), reads=[ra], writes=[rb])
                dst, dres = dst_of[1](m)
                P.op("dve", lambda e: e.tensor_tensor(dst, bq[0:rr, 0:n], pt[0:rr, 0:n], ALU.mult), reads=[rb, pres], writes=[dres])
            return post
        fm_mm(w_in[l], C_U, 512, ["hT"], hT_rhs, gelu_post(("fm", lambda m: (uT[:, m, 0:Tn], "X0"))))

        def va_post(g, pt, pres):
            gelu_post(("tm", lambda m: (Fb[0][0:R, 0:512], "F0")))(g, pt, pres)
            P.op("act", lambda e: e.activation(junk[0:R, 0:512], Fb[0][0:R, 0:512], AF.Square, accum_out=small[0:R, 0:1]),
                 reads=["F0"], writes=["hbfA", "hbfB", "small0"])
            P.op("act", lambda e: e.activation(small[0:R, 1:2], small[0:R, 0:1], AF.Sqrt, bias=epsb[0:R, 0:1], scale=1.0 / WA), reads=["small0"], writes=["small1"])
            P.op("dve", lambda e: e.reciprocal(small[0:R, 2:3], small[0:R, 1:2]), reads=["small1"], writes=["small2"])
            if sample:
                P.op("dve", lambda e: e.scalar_tensor_tensor(Fb[1][0:R, 0:512], Fb[0][0:R, 0:512], small[0:R, 2:3], avgb[0:R, :], ALU.mult, ALU.mult),
                     reads=["F0", "small2", "avgb"], writes=["F1"])
                P.op("sp", lambda e: e.dma_start(out=ncv_s[l], in_=Fb[1][0:R, 0:512]), reads=["F1"], dma="st_ncv")
                P.op("dve", lambda e: e.tensor_copy(va[0:R, 0, :], Fb[1][0:R, 0:512]), reads=["F1"], writes=["X1"])
            else:
                P.op("dve", lambda e: e.scalar_tensor_tensor(va[0:R, g, :], Fb[0][0:R, 0:512], small[0:R, 2:3], avgb[0:R, :], ALU.mult, ALU.mult),
                     reads=["F0", "small2", "avgb"], writes=["X1"])
        tm_mm(w_in[l], C_VA, 512, hT_lhs, ["hT"], va_post)

        for g in range(G):
            pt, pres = nextpm()
            for m in range(4):
                if sample:
                    P.op("pe", lambda e, m=m, pt=pt: e.matmul(pt[:, m * 128:m * 128 + R], va[0:R, 0, m * 128:(m + 1) * 128], identb[0:R, 0:R], start=True, stop=True),
                         reads=["X1", "identb"], writes=[pres], inc=(m == 3))
                else:
                    P.op("pe", lambda e, m=m, g=g, pt=pt: e.matmul(pt[:, m * 128:(m + 1) * 128], va[:, g, m * 128:(m + 1) * 128], wmT[:, m, :], start=True, stop=True),
                         reads=["X1", "wmT"], writes=[pres], inc=(m == 3))
            if sample:
                for m in range(4):
                    P.op("dve", lambda e, m=m, pt=pt: e.tensor_scalar(Fb[1][:, m * 128:m * 128 + R], pt[:, m * 128:m * 128 + R], small[:, 16 + m:17 + m], small[:, 20 + m:21 + m], ALU.mult, ALU.add),
                         reads=[pres, "small16"], writes=["F1"])
                    P.op("dve", lambda e, m=m: e.tensor_tensor(outT[:, m, 0:R], Fb[1][:, m * 128:m * 128 + R], uT[:, m, 0:R], ALU.mult),
                         reads=["F1", "X0"], writes=["X2"])
            else:
                P.op("dve", lambda e, pt=pt: e.tensor_tensor(Fb[1][:, 0:512], pt[:, 0:512], bsT[:, :, :].rearrange("p m t -> p (m t)"), ALU.add),
                     reads=[pres, "bsT"], writes=["F1"])
                P.op("dve", lambda e, g=g: e.tensor_tensor(outT[:, :, g * 128:(g + 1) * 128], Fb[1][:, 0:512].rearrange("p (m t) -> p m t", m=4), uT[:, :, g * 128:(g + 1) * 128], ALU.mult),
                     reads=["F1", "X0"], writes=["X2"])

        def merge_branch(i, first):
            for half in range(2):
                srcp, kcp = wview(p_abc[i][l], half * 512, 512)
                wp, wpres = wload(srcp, kcp, 512)
                srcg, kcg = wview(w_in[l], C_G + i * 1024 + half * 512, 512)
                wg, wgres = wload(srcg, kcg, 512)
                for mm in range(4):
                    m = half * 4 + mm
                    pp, ppres = nextpm()
                    mm_group(pp[:, 0:Tn], [(wp[:, k, mm * 128:(mm + 1) * 128], outT[:, k, 0:Tn]) for k in range(4)], [wpres, "X2"], ppres)
                    pg, pgres = nextpm()
                    mm_group(pg[:, 0:Tn], [(wg[:, k, mm * 128:(mm + 1) * 128], hT[:, k, 0:Tn]) for k in range(8)], [wgres, "hT"], pgres)
                    isg, itm = ((5, 4), (7, 6))[state["mrg"] % 2]
                    state["mrg"] += 1
                    sg_, tm_ = Fb[isg], Fb[itm]
                    rsg, rtm = f"F{isg}", f"F{itm}"
                    P.op("act", lambda e, pg=pg, sg_=sg_: e.activation(sg_[:, 0:Tn], pg[:, 0:Tn], AF.Sigmoid), reads=[pgres], writes=[rsg])
                    if first:
                        P.op("dve", lambda e, pp=pp, m=m, sg_=sg_: e.tensor_tensor(mergedT[:, m, 0:Tn], sg_[:, 0:Tn], pp[:, 0:Tn], ALU.mult),
                             reads=[rsg, ppres], writes=["mergedT"])
                    else:
                        P.op("dve", lambda e, pp=pp, sg_=sg_, tm_=tm_: e.tensor_tensor(tm_[:, 0:Tn], sg_[:, 0:Tn], pp[:, 0:Tn], ALU.mult),
                             reads=[rsg, ppres], writes=[rtm])
                        P.op("pool", lambda e, m=m, tm_=tm_: e.tensor_tensor(mergedT[:, m, 0:Tn], mergedT[:, m, 0:Tn], tm_[:, 0:Tn], ALU.add),
                             reads=[rtm, "mergedT"], writes=["mergedT"])
        def dump(i, srcT):
            if DBG and l == 0 and ti == 0 and sample == bool(os.environ.get("KDBGS")):
                for m in range(4):
                    P.op("dve", lambda e, m=m: e.tensor_copy(Fb[4][:, 0:512], srcT[:, m, :]), reads=["X2", "mergedT"], writes=["F4"])
                    P.op("sp", lambda e, m=m: e.dma_start(out=dbg[i, :, m * 512:(m + 1) * 512], in_=Fb[4][:, 0:512]), reads=["F4"], dma="st_dbg")
        dump(0, outT)
        merge_branch(0, True)

        qT = X[0]
        gq = gqk[:, 0, :]
        gk = gqk[:, 1, :]

        def qk_norm(srcp, pres, g, gain, dst, dres):
            r = g % 2
            sq = Fb[2] if r == 0 else Fb[6]
            sqres = "F2" if r == 0 else "F6"
            c0 = r * 16
            sa = smq[0:R, c0:c0 + 8]
            sb_ = smq[0:R, c0 + 8:c0 + 16]
            ra, rb = f"smq{r}a", f"smq{r}b"
            P.op("act", lambda e: e.activation(sq[0:R, 0:512], srcp[0:R, 0:512], AF.Square), reads=[pres], writes=[sqres])
            P.op("dve", lambda e: e.tensor_reduce(sa, sq[0:R, 0:512].rearrange("p (a d) -> p a d", d=HD), AX.X, ALU.add), reads=[sqres], writes=[ra])
            P.op("act", lambda e: e.activation(sb_, sa, AF.Sqrt, bias=epsb[0:R, 0:1], scale=1.0 / HD), reads=[ra], writes=[rb])
            P.op("dve", lambda e: e.reciprocal(sa, sb_), reads=[rb], writes=[ra])
            P.op("dve", lambda e: e.tensor_tensor(dst[0:R, 0:512].rearrange("p (a d) -> p a d", d=HD), srcp[0:R, 0:512].rearrange("p (a d) -> p a d", d=HD),
                                                   sa.unsqueeze(2).to_broadcast([R, 8, HD]), ALU.mult), reads=[pres, ra], writes=[dres])
            P.op("pool", lambda e: e.tensor_tensor(dst[0:R, 0:512].rearrange("p (a d) -> p a d", d=HD), dst[0:R, 0:512].rearrange("p (a d) -> p a d", d=HD),
                                                    gain[0:R, :].unsqueeze(1).to_broadcast([R, 8, HD]), ALU.mult), reads=[dres, "gqk"], writes=[dres])

        def to_bf_T(src, sres, dst_of, dres, g=0, nblk=4):
            r = g % 2
            hres = "hbfA" if r == 0 else "hbfB"
            pres_ = "ptrA" if r == 0 else "ptrB"
            hb = hbf[0:R, r * 512:(r + 1) * 512]
            pp = ptr[:, r * 512:(r + 1) * 512]
            P.op("act", lambda e: e.activation(hb, src[0:R, 0:512], AF.Copy), reads=[sres], writes=[hres])
            for j in range(nblk):
                P.op("pe", lambda e, j=j: e.transpose(pp[:, j * 128:j * 128 + R], hb[:, j * 128:(j + 1) * 128], identb[0:R, 0:R]),
                     reads=[hres, "identb"], writes=[pres_], inc=(j == nblk - 1))
            P.op("act", lambda e: e.activation(dst_of, pp.rearrange("p (j t) -> p j t", j=4)[:, :, 0:R], AF.Copy), reads=[pres_], writes=[dres])

        kst = Fb[0]
        vst = Fb[1]
        if sample:
            ksT = X[1]
        def q_post(g, pt, pres):
            qd, qr = (Fb[3], "F3") if g % 2 == 0 else (Fb[7], "F7")
            qk_norm(pt, pres, g, gq, qd, qr)
            to_bf_T(qd, qr, qT[:, :, g * 128:g * 128 + R], "X0", g)
        tm_mm(w_in[l], C_Q, 512, hT_lhs, ["hT"], q_post)

        def k_post(g, pt, pres):
            kd, kr = (Fb[0], "F0") if g % 2 == 0 else (Fb[4], "F4")
            qk_norm(pt, pres, g, gk, kd, kr)
            if sample:
                P.op("sp", lambda e: e.dma_start(out=nk_s[l], in_=kd[0:R, 0:512]), reads=[kr], dma="st_k")
            else:
                P.op("sp", lambda e: e.dma_start(out=nk_p[l, b, t0 + g * 128:t0 + (g + 1) * 128, :], in_=kd[:, 0:512]), reads=[kr], dma="st_k")
                to_bf_T(kd, kr, kT[:, :, t0 + g * 128:t0 + (g + 1) * 128], "kT", g)
        tm_mm(w_in[l], C_K, 512, hT_lhs, ["hT"], k_post)

        def v_post(g, pt, pres):
            vd, vr = (Fb[1], "F1") if g % 2 == 0 else (Fb[5], "F5")
            P.op("act", lambda e: e.activation(vd[0:R, 0:512], pt[0:R, 0:512], AF.Copy), reads=[pres], writes=[vr])
            if sample:
                P.op("sp", lambda e: e.dma_start(out=nv_s[l], in_=vd[0:R, 0:512]), reads=[vr], dma="st_v")
            else:
                P.op("sp", lambda e: e.dma_start(out=nv_p[l, b, t0 + g * 128:t0 + (g + 1) * 128, :], in_=vd[:, 0:512]), reads=[vr], dma="st_v")
                P.op("pool", lambda e: e.tensor_copy(vE[:, ti * 4 + g, :, 0:128], vd[:, 0:512].rearrange("p (h e) -> p h e", h=4)), reads=[vr], writes=["vE"])
        tm_mm(w_in[l], C_V, 512, hT_lhs, ["hT"], v_post)

        scale = HD ** -0.5
        if not sample:
            b0 = ti * 4
            ptc = [0]
            scnt = [0]
            nkb = b0 + 4
            accs = [pm[4], pm[5], pm[6]]

            def acc_of(g, c):
                idx = g * 2 + c
                return accs[idx // 3][:, (idx % 3) * 129:(idx % 3) * 129 + 129], f"pm{4 + idx // 3}"

            def qk_exp(h, j):
                g_lo = max(0, j - b0)
                q0 = g_lo * 128
                pts = []
                for c in range(2):
                    bi = scnt[0] % 4
                    scnt[0] += 1
                    sp_, spres = pm[bi], f"pm{bi}"
                    P.op("pe", lambda e, sp_=sp_, c=c, j=j, q0=q0, h=h: e.matmul(sp_[:, q0:T], kT[c * 64:(c + 1) * 64, h, j * 128:(j + 1) * 128],
                                                                              qT[c * 64:(c + 1) * 64, h, q0:T], start=True, stop=True),
                         reads=["kT", "X0"], writes=[spres])
                    pi = ptc[0] % NPT
                    ptc[0] += 1
                    P.op("act", lambda e, sp_=sp_, pi=pi, q0=q0: e.activation(PT[pi][:, q0:T], sp_[:, q0:T], AF.Exp, scale=scale),
                         reads=[spres], writes=[f"PT{pi}"])
                    if j >= b0:
                        P.op("pool", lambda e, pi=pi, q0=q0: e.tensor_tensor(PT[pi][:, q0:q0 + 128], PT[pi][:, q0:q0 + 128], maskb[:, :], ALU.mult),
                             reads=[f"PT{pi}", "maskb"], writes=[f"PT{pi}"])
                    pts.append(pi)
                return (h, j, g_lo, pts)

            def pv(h, j, g_lo, pts):
                for g in range(g_lo, 4):
                    for c in range(2):
                        a_ap, ares = acc_of(g, c)
                        pi = pts[c]
                        last = (j == b0 + g)
                        P.op("pe", lambda e, a_ap=a_ap, pi=pi, g=g, j=j, h=h, last=last, c=c: e.matmul(a_ap, PT[pi][:, g * 128:(g + 1) * 128], vE[:, j, h, 0:129],
                                                                                          start=(j == 0 and (g * 2 + c) % 3 == 0), stop=last, skip_group_check=True),
                             reads=[f"PT{pi}", "vE"], writes=[ares], inc=(c == 1))

            def finalize(h):
                of = Fb[4]
                sq = Fb[5]
                of3 = of[:, 0:512].rearrange("p (g e) -> p g e", g=4)
                sq3 = sq[:, 0:512].rearrange("p (g e) -> p g e", g=4)
                ob4 = hbf[:, 0:512].rearrange("p (g e) -> p g e", g=4)
                for idx_ in range(8):
                    a_, r_ = acc_of(idx_ // 2, idx_ % 2)
                    P.op("dve", lambda e, a_=a_, idx_=idx_: e.reciprocal(small[:, 32 + idx_:33 + idx_], a_[:, 128:129]), reads=[r_], writes=["small32"])
                rd = small[:, 32:40].rearrange("p (g c) -> p g c", c=2)
                P.op("dve", lambda e: e.tensor_scalar(rd[:, :, 1], rd[:, :, 1], lam[:, 0:1], None, ALU.mult), reads=["small32", "lam"], writes=["small32"])
                for g in range(4):
                    a0, r0 = acc_of(g, 0)
                    a1, r1 = acc_of(g, 1)
                    P.op("dve", lambda e, a0=a0, g=g: e.tensor_scalar(of3[:, g, :], a0[:, 0:128], small[:, 32 + 2 * g:33 + 2 * g], None, ALU.mult), reads=[r0, "small32"], writes=["F4"])
                    P.op("dve", lambda e, a1=a1, g=g: e.scalar_tensor_tensor(of3[:, g, :], a1[:, 0:128], small[:, 33 + 2 * g:34 + 2 * g], of3[:, g, :], ALU.mult, ALU.add),
                         reads=[r1, "small32", "F4"], writes=["F4"])
                P.op("pool", lambda e: e.tensor_tensor(sq[:, 0:512], of[:, 0:512], of[:, 0:512], ALU.mult), reads=["F4"], writes=["F5"])
                P.op("dve", lambda e: e.tensor_reduce(small[:, 40:44], sq3, AX.X, ALU.add), reads=["F5"], writes=["small40"])
                P.op("act", lambda e: e.activation(small[:, 44:48], small[:, 40:44], AF.Sqrt, bias=epsb[:, 0:1], scale=1.0 / 128), reads=["small40"], writes=["small44"])
                P.op("dve", lambda e: e.reciprocal(small[:, 48:52], small[:, 44:48]), reads=["small44"], writes=["small48"])
                P.op("dve", lambda e: e.tensor_tensor(sq3, of3, small[:, 48:52].unsqueeze(2).to_broadcast([128, 4, 128]), ALU.mult), reads=["F4", "small48"], writes=["F5"])
                P.op("dve", lambda e: e.tensor_tensor(ob4, sq3, gsub[:, :].unsqueeze(1).to_broadcast([128, 4, 128]), ALU.mult), reads=["F5", "gsub"], writes=["hbfA", "hbfB"])
                for g in range(4):
                    P.op("pe", lambda e, g=g: e.transpose(ptr[:, g * 128:(g + 1) * 128], hbf[:, g * 128:(g + 1) * 128], identb[:, :]), reads=["hbfA", "hbfB", "identb"], writes=["ptrA", "ptrB"], inc=(g == 3))
                P.op("act", lambda e, h=h: e.activation(outT[:, h, 0:512], ptr[:, 0:512], AF.Copy), reads=["ptrA", "ptrB"], writes=["X2"])

            pend = None
            for h in range(NH):
                for j in range(nkb):
                    cur = qk_exp(h, j)
                    if os.environ.get("KNOSKEW"):
                        pv(*cur)
                        if j == nkb - 1:
                            finalize(h)
                        continue
                    if pend is not None:
                        pv(*pend)
                        if pend[1] == nkb - 1:
                            finalize(pend[0])
                    pend = cur
            if pend is not None:
                pv(*pend)
                finalize(pend[0])
        else:
            sample_attention(l, li, qT, kst, vst, outT)
        dump(1, outT)
        merge_branch(1, False)

        pooled = X[0]
        if not sample:
            if ti == 0:
                P.op("dve", lambda e: e.memset(zcT[:, :, 0:15], 0.0), writes=["zcT"])
            else:
                P.op("dve", lambda e: e.tensor_copy(pref[:, :].rearrange("p (m t) -> p m t", m=4), zcT[:, :, T:T + 15]), reads=["zcT"], writes=["pref"])
                P.op("dve", lambda e: e.tensor_copy(zcT[:, :, 0:15], pref[:, :].rearrange("p (m t) -> p m t", m=4)), reads=["pref"], writes=["zcT"])
            def c_post(m, pt, pres):
                P.op("act", lambda e: e.activation(zcT[:, m, 15:15 + T], pt[:, 0:T], AF.Copy), reads=[pres], writes=["zcT"])
            fm_mm(w_in[l], C_C, 512, ["hT"], hT_rhs, c_post)
            if ti == NT - 1:
                for m in range(4):
                    P.op("sp", lambda e, m=m: e.dma_start(out=npool_p[l, b][:, m * 128:(m + 1) * 128].rearrange("t c -> c t"), in_=zcT[:, m, T:T + 15], allow_slow_non_contiguous=True), reads=["zcT"], dma="st_misc")
            L = 15 + T
            for m in range(4):
                cur = zcT[:, m, :]
                cres = "zcT"
                sh = 1
                for step in range(m + 1):
                    dstb = Fb[step % 2]
                    dres = f"F{step % 2}"
                    P.op("dve" if step % 2 == 0 else "pool", lambda e, dstb=dstb, cur=cur, sh=sh: e.tensor_tensor(dstb[:, sh:L], cur[:, sh:L], cur[:, 0:L - sh], ALU.add),
                         reads=[cres], writes=[dres])
                    cur = dstb
                    cres = dres
                    sh *= 2
                win = 2 ** (m + 1)
                P.op("dve", lambda e, cur=cur, m=m, win=win: e.scalar_tensor_tensor(pooled[:, m, 0:T], cur[:, 15:L], 1.0 / win, zcT[:, m, 15:L], ALU.mult, ALU.subtract),
                     reads=[cres, "zcT"], writes=["X0"])
                if ti == 0:
                    P.op("dve", lambda e, cur=cur, m=m: e.tensor_tensor(Fb[2][:, 0:16], cur[:, 15:31], cst[:, 3, m * 16:(m + 1) * 16], ALU.mult), reads=[cres, "cst"], writes=["F2"])
                    P.op("dve", lambda e, m=m: e.tensor_tensor(pooled[:, m, 0:16], Fb[2][:, 0:16], zcT[:, m, 15:31], ALU.subtract), reads=["F2", "zcT"], writes=["X0"])
        else:
            sample_pool(l, pooled)
        for m in range(4):
            pt, pres = nextpm()
            P.op("pe", lambda e, pt=pt, m=m: e.matmul(pt[:, 0:Tn], cwb[:, m, :], pooled[:, m, 0:Tn], start=True, stop=True), reads=["cwb", "X0"], writes=[pres])
            P.op("act", lambda e, pt=pt, m=m: e.activation(outT[:, m, 0:Tn], pt[:, 0:Tn], AF.Copy, scale=csc[:, m:m + 1]), reads=[pres, "csc"], writes=["X2"])
        dump(2, outT)
        merge_branch(2, False)
        dump(3, mergedT)

        def wo_post_of(half):
            def post(g, pt, pres):
                P.op("dve", lambda e: e.tensor_tensor(xres[0:R, g, half * 512:(half + 1) * 512], xres[0:R, g, half * 512:(half + 1) * 512], pt[0:R, 0:512], ALU.add),
                     reads=[pres, "xres"], writes=["xres"])
            return post
        for half in range(2):
            tm_mm(w_o[l], half * 512, 512, lambda k, g: mergedT[:, k, g * 128:g * 128 + R], ["mergedT"], wo_post_of(half))

        norm_to_hT(l, g2b, "g2b", Tn, R)
        if sample:
            sample_ffn_up(l)
        else:
            if ti == 0:
                P.op("dve", lambda e: e.memset(halo[:, :, :], 0.0), writes=["halo"])
            ffn_pend = [None]

            def ffn_finish(j, cfin):
                P.op("act", lambda e, c0=cfin[0][0]: e.activation(c0[:, 0:T], c0[:, 0:T], AF.Silu), reads=[cfin[0][1]], writes=[cfin[0][1]])
                P.op("pool", lambda e, c0=cfin[0][0], c1=cfin[1][0], j=j: e.tensor_tensor(actT[:, j, 0:T], c0[:, 0:T], c1[:, 0:T], ALU.mult), reads=[cfin[0][1], cfin[1][1]], writes=["actT"])

            for jj in range(0, 22, 2):
                srcg, _ = wview(f_up[l], jj * 128, 256)
                srcv, _ = wview(f_up[l], DFF + jj * 128, 256)
                i = state["w"] % NW
                state["w"] += 1
                wbuf = wr[i]
                wres = f"wr{i}"
                dg = wbuf[:, 0:2048].rearrange("p (k n) -> p k n", k=8)
                dv = wbuf[:, 2048:4096].rearrange("p (k n) -> p k n", k=8)
                P.op("sp", lambda e, dg=dg, srcg=srcg: e.dma_start(out=dg, in_=srcg[0]), reads=[srcg[1]], writes=[wres], dma=wres)
                P.op("sp", lambda e, dv=dv, srcv=srcv: e.dma_start(out=dv, in_=srcv[0]), reads=[srcv[1]], writes=[wres], dma=wres)
                for jo in range(2):
                    j = jj + jo
                    cfin = []
                    for part, wt in ((0, dg), (1, dv)):
                        ch = j + 22 * part
                        pt, pres = nextpm()
                        mm_group(pt[:, 0:T], [(wt[:, k, jo * 128:(jo + 1) * 128], hT[:, k, 0:T]) for k in range(8)], [wres, "hT"], pres)
                        fbase = 4 * (j % 2)
                        ub = Fb[fbase + part * 2]
                        ures = f"F{fbase + part * 2}"
                        cb = Fb[fbase + part * 2 + 1]
                        cres = f"F{fbase + part * 2 + 1}"
                        P.op("pool", lambda e, ub=ub, ch=ch: e.tensor_copy(ub[:, 0:2], halo[:, ch, :]), reads=["halo"], writes=[ures])
                        P.op("act", lambda e, ub=ub, pt=pt: e.activation(ub[:, 2:2 + T], pt[:, 0:T], AF.Copy), reads=[pres], writes=[ures])
                        P.op("pool", lambda e, ub=ub, ch=ch: e.tensor_copy(halo[:, ch, :], ub[:, T:T + 2]), reads=[ures], writes=["halo"])
                        P.op("act", lambda e, ub=ub, cb=cb, ch=ch: e.activation(cb[:, 0:T], ub[:, 2:2 + T], AF.Identity, bias=cvw[:, 3, ch:ch + 1], scale=cvw[:, 2, ch:ch + 1]),
                             reads=[ures, "cvw"], writes=[cres])
                        P.op("dve", lambda e, ub=ub, cb=cb, ch=ch: e.scalar_tensor_tensor(cb[:, 0:T], ub[:, 1:1 + T], cvw[:, 1, ch:ch + 1], cb[:, 0:T], ALU.mult, ALU.add),
                             reads=[ures, cres, "cvw"], writes=[cres])
                        P.op("dve", lambda e, ub=ub, cb=cb, ch=ch: e.scalar_tensor_tensor(cb[:, 0:T], ub[:, 0:T], cvw[:, 0, ch:ch + 1], cb[:, 0:T], ALU.mult, ALU.add),
                             reads=[ures, cres, "cvw"], writes=[cres])
                        cfin.append((cb, cres))
                    if ffn_pend[0] is not None:
                        ffn_finish(*ffn_pend[0])
                    ffn_pend[0] = (j, cfin)
            ffn_finish(*ffn_pend[0])
            if ti == NT - 1:
                for r in range(2):
                    P.op("sp", lambda e, r=r: e.dma_start(out=nconv_p[l, b, r].rearrange("(k c) -> c k", c=128), in_=halo[:, :, r], allow_slow_non_contiguous=True), reads=["halo"], dma="st_misc")
        for half in range(2):
            fbanks = ([0, 1, 2, 3] if half == 0 else [4, 5, 6, 0])[:G]
            accs = [pm[i_] for i_ in fbanks]
            kparts = [(0, 8), (8, 8), (16, 6)]
            for pi_, (k0, kcn) in enumerate(kparts):
                src, kc = wview(f_down[l], half * 512, 512, k0, kcn)
                wt, wres = wload(src, kc, 512)
                for g in range(G):
                    for k in range(kc):
                        first = (pi_ == 0 and k == 0)
                        lastk = (pi_ == 2 and k == kc - 1)
                        P.op("pe", lambda e, g=g, k=k, k0=k0, wt=wt, first=first, lastk=lastk: e.matmul(accs[g][0:R, 0:512], actT[:, k0 + k, g * 128:g * 128 + R], wt[:, k, :], start=first, stop=lastk),
                             reads=[wres, "actT"], writes=[f"pm{fbanks[g]}"], inc=(k == kc - 1))
            for g in range(G):
                P.op("dve", lambda e, g=g, half=half: e.tensor_tensor(xres[0:R, g, half * 512:(half + 1) * 512], xres[0:R, g, half * 512:(half + 1) * 512], accs[g][0:R, 0:512], ALU.add),
                     reads=[f"pm{fbanks[g]}", "xres"], writes=["xres"])
        if sample:
            dst = y_sample if last_layer else xmid_s
            P.op("sp", lambda e: e.dma_start(out=dst, in_=xres[0:R, 0, :]), reads=["xres"], writes=["xmid_s"], dma="st_x")
        else:
            dst = (y_prompt if last_layer else xmid)[b, t0:t0 + T, :].rearrange("(g p) d -> p g d", p=128)
            P.op("sp", lambda e: e.dma_start(out=dst, in_=xres[:, :, :]), reads=["xres"], writes=[f"xmid{b}_{ti}"], dma="st_x")

    def subln(src, sres, R, li):
        P.op("act", lambda e: e.activation(junk[0:R, 0:128], src, AF.Square, accum_out=small[0:R, 34:35]), reads=[sres], writes=["hbfA", "hbfB", "small34"])
        P.op("act", lambda e: e.activation(small[0:R, 35:36], small[0:R, 34:35], AF.Sqrt, bias=epsb[0:R, 0:1], scale=1.0 / 128), reads=["small34"], writes=["small35"])
        P.op("dve", lambda e: e.reciprocal(small[0:R, 36:37], small[0:R, 35:36]), reads=["small35"], writes=["small36"])
        P.op("dve", lambda e: e.scalar_tensor_tensor(ob[0:R, :], src, small[0:R, 36:37], gsub[0:R, :], ALU.mult, ALU.mult), reads=[sres, "small36", "gsub"], writes=["ob"])

    def sample_attention(l, li, qT, kst, vst, outT):
        R = NS
        scale = HD ** -0.5
        qf = Fb[3]
        P.op("sp", lambda e: e.dma_start(out=qscr, in_=hbf[0:R, 0:512]), reads=["hbfA", "hbfB"], writes=["qscr"], dma="st_q")
        akv = actT[:, :, :].rearrange("p a b -> p (a b)")
        KV = [akv[:, 0:8192].rearrange("p (d s n) -> p d s n", d=4, s=2),
              wrall[:, 0:2, :].rearrange("p a b -> p (a b)").rearrange("p (d s n) -> p d s n", d=4, s=2)]
        KVres = [["KVa"], ["wr0", "wr1"]]
        KVsem = ["KVa", "KVb"]
        S = Fb[5]
        Pb = PT[0]
        ngrp = NPG // 8
        gcount = [0]
        first = [True]
        state["ring"] = [0, 1, 2]
        for i in range(NS):
            qb = qb2[i % 2]
            qres = f"PT{1 + i % 2}"
            P.op("sp", lambda e, qb=qb, i=i: e.dma_start(out=qb[:, :], in_=qscr[i].partition_broadcast(128)), reads=["qscr"], writes=[qres], dma="ld_" + qres)
            for gi in range(ngrp):
                bi = gcount[0] % 2
                gcount[0] += 1
                for dd in range(4):
                    col = i * (NPG // 2) + gi * 4 + dd
                    wk = KVres[bi] + (["actT"] if first[0] else [])
                    first[0] = False
                    P.op("pool", lambda e, bi=bi, dd=dd, col=col: e.indirect_dma_start(out=KV[bi][:, dd, :, :].rearrange("p s n -> p (s n)"), out_offset=None, in_=cache_kv[l],
                                                                                 in_offset=bass.IndirectOffsetOnAxis(ap=idx[:, col:col + 1], axis=0)),
                         reads=["idx"], writes=wk, dma=KVsem[bi])
                for hf in range(2):
                    tmp = X[hf]
                    tres = f"X{hf}"
                    P.op("dve", lambda e, tmp=tmp, bi=bi, hf=hf, qb=qb: e.tensor_tensor(tmp[:, :, :].rearrange("p (d s) n -> p d s n", s=2), KV[bi][:, hf * 2:(hf + 1) * 2, :, 0:512],
                                                                                   qb[:, :].unsqueeze(1).unsqueeze(1).to_broadcast([128, 2, 2, 512]), ALU.mult),
                         reads=KVres[bi] + [qres], writes=[tres])
                    pg0 = gi * 8 + hf * 4
                    P.op("dve", lambda e, tmp=tmp, pg0=pg0: e.tensor_reduce(S[:, pg0 * 8:(pg0 + 4) * 8], tmp[:, :, :].rearrange("p j (a d) -> p (j a) d", d=HD), AX.X, ALU.add),
                         reads=[tres], writes=["F5"])
                P.op("act", lambda e, gi=gi: e.activation(Pb[:, gi * 64:(gi + 1) * 64], S[:, gi * 64:(gi + 1) * 64], AF.Exp, scale=scale), reads=["F5"], writes=["PT0"])
                for jj in range(8):
                    j = gi * 8 + jj
                    P.op("pe", lambda e, bi=bi, jj=jj, j=j: e.matmul(pm[3][0:8, 0:512], Pb[:, j * 8:(j + 1) * 8], KV[bi][:, jj // 2, jj % 2, 512:1024], start=(j == 0), stop=(j == NPG - 1)),
                         reads=["PT0"] + KVres[bi], writes=["pm3"], inc=(jj == 7))
            P.op("dve", lambda e: e.tensor_reduce(sm2[:, 0:8], Pb[:, 0:NPG * 8].rearrange("p (j a) -> p a j", a=8), AX.X, ALU.add), reads=["PT0"], writes=["sm2a"])
            lp, lres = nextpm()
            P.op("pe", lambda e, lp=lp: e.matmul(lp[0:8, 0:1], sm2[:, 0:8], ones32[:, 0:1], start=True, stop=True), reads=["sm2a", "ones32"], writes=[lres])
            P.op("act", lambda e, lp=lp: e.activation(sm2[0:8, 8:9], lp[0:8, 0:1], AF.Copy), reads=[lres], writes=["sm2b"])
            P.op("sp", lambda e, i=i: e.dma_start(out=lsc[i].rearrange("(a b) -> a b", b=1), in_=sm2[0:8, 8:9], allow_slow_non_contiguous=True), reads=["sm2b"], writes=["lsc"], dma="st_l")
            ob_ = Fb[2] if i % 2 == 0 else Fb[4]
            ores = "F2" if i % 2 == 0 else "F4"
            P.op("act", lambda e, ob_=ob_: e.activation(ob_[0:8, 0:512], pm[3][0:8, 0:512], AF.Copy), reads=["pm3"], writes=[ores])
            P.op("sp", lambda e, ob_=ob_, i=i: e.dma_start(out=osc[i].rearrange("(a b) -> a b", b=512), in_=ob_[0:8, 0:512]), reads=[ores], writes=["osc"], dma="st_o")
        Od = [Fb[4], Fb[5]]
        for c in range(2):
            src = bass.AP(osc_t, c * 512, [[4096, NS], [1152, 4], [1, 128]])
            P.op("sp", lambda e, c=c, src=src: e.dma_start(out=Od[c][0:R, 0:512].rearrange("p (h e) -> p h e", h=4), in_=src), reads=["osc"], writes=[f"F{4 + c}"], dma=f"ld_od{c}")
        P.op("sp", lambda e: e.dma_start(out=sm2[0:R, 16:24], in_=lsc), reads=["lsc"], writes=["sm2c"], dma="ld_l")
        P.op("dve", lambda e: e.tensor_tensor(Fb[2][0:R, 0:512], qf[0:R, 0:512], kst[0:R, 0:512], ALU.mult), reads=["F3", "F0"], writes=["F2"])
        P.op("dve", lambda e: e.tensor_reduce(sm2[0:R, 24:32], Fb[2][0:R, 0:512].rearrange("p (a d) -> p a d", d=HD), AX.X, ALU.add), reads=["F2"], writes=["sm2d"])
        P.op("act", lambda e: e.activation(sm2[0:R, 32:40], sm2[0:R, 24:32], AF.Exp, scale=scale), reads=["sm2d"], writes=["sm2e"])
        P.op("dve", lambda e: e.tensor_tensor(sm2[0:R, 40:48], sm2[0:R, 16:24], sm2[0:R, 32:40], ALU.add), reads=["sm2c", "sm2e"], writes=["sm2f"])
        P.op("dve", lambda e: e.reciprocal(sm2[0:R, 48:56], sm2[0:R, 40:48]), reads=["sm2f"], writes=["sm2g"])
        rd = sm2[0:R, 48:56].rearrange("p (h c) -> p h c", c=2)
        ps_ = sm2[0:R, 32:40].rearrange("p (h c) -> p h c", c=2)
        P.op("dve", lambda e: e.tensor_scalar(rd[:, :, 1], rd[:, :, 1], lam[0:R, 0:1], None, ALU.mult), reads=["sm2g", "lam"], writes=["sm2g"])
        v3 = vst[0:R, 0:512].rearrange("p (h e) -> p h e", h=4)
        t3 = Fb[2][0:R, 0:512].rearrange("p (h e) -> p h e", h=4)
        for c in range(2):
            o3 = Od[c][0:R, 0:512].rearrange("p (h e) -> p h e", h=4)
            P.op("dve", lambda e, c=c: e.tensor_tensor(t3, v3, ps_[:, :, c].unsqueeze(2).to_broadcast([R, 4, 128]), ALU.mult), reads=["F1", "sm2e"], writes=["F2"])
            P.op("dve", lambda e, o3=o3: e.tensor_tensor(o3, o3, t3, ALU.add), reads=["F2", f"F{4 + c}"], writes=[f"F{4 + c}"])
            P.op("dve", lambda e, o3=o3, c=c: e.tensor_tensor(o3, o3, rd[:, :, c].unsqueeze(2).to_broadcast([R, 4, 128]), ALU.mult), reads=["sm2g", f"F{4 + c}"], writes=[f"F{4 + c}"])
        P.op("dve", lambda e: e.tensor_tensor(Fb[4][0:R, 0:512], Fb[4][0:R, 0:512], Fb[5][0:R, 0:512], ALU.add), reads=["F4", "F5"], writes=["F4"])
        for h in range(NH):
            subln(Fb[4][0:R, h * 128:(h + 1) * 128], "F4", R, li)
            P.op("pe", lambda e: e.transpose(ptr[:, 0:R], ob[0:R, :], identb[0:R, 0:R]), reads=["ob", "identb"], writes=["ptrA", "ptrB"])
            P.op("act", lambda e, h=h: e.activation(outT[:, h, 0:R], ptr[:, 0:R], AF.Copy), reads=["ptrA", "ptrB"], writes=["X2"])
        state["ring"] = [0, 1, 2, 3, 4, 5, 6]

    def sample_pool(l, pooled):
        R = NS
        zcs = zcT[:, :, 0:NS * 16].rearrange("p m (i t) -> p m i t", t=16)
        src, kc = wview(w_in[l], C_C, 512)
        wt, wres = wload(src, kc, 512)
        pt, pres = nextpm()
        mm_group(pt[0:R, 0:512], [(hT[:, k, 0:R], wt[:, k, :]) for k in range(8)], [wres, "hT"], pres)
        P.op("act", lambda e, pt=pt: e.activation(Fb[2][0:R, 0:512], pt[0:R, 0:512], AF.Copy), reads=[pres], writes=["F2"])
        P.op("sp", lambda e: e.dma_start(out=npool_s[l, :, 14, :], in_=Fb[2][0:R, 0:512]), reads=["F2"], dma="st_misc")
        P.op("sp", lambda e: e.dma_start(out=npool_s[l, :, 0:14, :], in_=state_pool[l, :, 1:15, :]), dma="st_misc")
        for m in range(4):
            pt, pres = nextpm()
            mm_group(pt[:, 0:R], [(wt[:, k, m * 128:(m + 1) * 128], hT[:, k, 0:R]) for k in range(8)], [wres, "hT"], pres)
            P.op("act", lambda e, pt=pt, m=m: e.activation(zcs[:, m, :, 15], pt[:, 0:R], AF.Copy), reads=[pres], writes=["zcT"])
        for i0 in range(0, NS, 8):
            n = min(8, NS - i0)
            rows = n * 15
            P.op("sp", lambda e, i0=i0, n=n, rows=rows: e.dma_start(out=Fb[0][0:rows, 0:512], in_=state_pool[l, i0:i0 + n].rearrange("i r f -> (i r) f")), writes=["F0"], dma="ld_sp")
            pt, pres = nextpm()
            for m in range(4):
                P.op("pe", lambda e, pt=pt, m=m, rows=rows: e.transpose(pt[:, m * 120:m * 120 + rows], Fb[0][0:rows, m * 128:(m + 1) * 128], cst[0:rows, 0, 0:rows]),
                     reads=["F0", "cst"], writes=[pres], inc=(m == 3))
            for m in range(4):
                P.op("act", lambda e, pt=pt, m=m, i0=i0, n=n, rows=rows: e.activation(zcs[:, m, i0:i0 + n, 0:15], pt[:, m * 120:m * 120 + rows].rearrange("p (i t) -> p i t", t=15), AF.Copy),
                     reads=[pres], writes=["zcT"])
        for m in range(4):
            cur = zcs[:, m, :, :]
            cres = "zcT"
            sh = 1
            for step in range(m + 1):
                dstb = Fb[step % 2][:, 0:NS * 16].rearrange("p (i t) -> p i t", t=16)
                dres = f"F{step % 2}"
                P.op("dve", lambda e, dstb=dstb, cur=cur, sh=sh: e.tensor_tensor(dstb[:, :, sh:16], cur[:, :, sh:16], cur[:, :, 0:16 - sh], ALU.add), reads=[cres], writes=[dres])
                cur = dstb
                cres = dres
                sh *= 2
            win = 2 ** (m + 1)
            P.op("dve", lambda e, cur=cur, m=m, win=win: e.scalar_tensor_tensor(pooled[:, m, 0:R], cur[:, :, 15], 1.0 / win, zcs[:, m, :, 15], ALU.mult, ALU.subtract),
                 reads=[cres, "zcT"], writes=["X0"])

    def sample_ffn_up(l):
        R = NS
        P.op("sp", lambda e: e.dma_start(out=nconv_s[l, :, 0, :], in_=state_conv[l, :, 1, :]), dma="st_misc")
        for jj in range(0, 22, 2):
            srcg, _ = wview(f_up[l], jj * 128, 256)
            srcv, _ = wview(f_up[l], DFF + jj * 128, 256)
            i = state["w"] % NW
            state["w"] += 1
            wbuf = wr[i]
            wres = f"wr{i}"
            dg = wbuf[:, 0:2048].rearrange("p (k n) -> p k n", k=8)
            dv = wbuf[:, 2048:4096].rearrange("p (k n) -> p k n", k=8)
            P.op("sp", lambda e, dg=dg, srcg=srcg: e.dma_start(out=dg, in_=srcg[0]), reads=[srcg[1]], writes=[wres], dma=wres)
            P.op("sp", lambda e, dv=dv, srcv=srcv: e.dma_start(out=dv, in_=srcv[0]), reads=[srcv[1]], writes=[wres], dma=wres)
            for part, wt in ((0, dg), (1, dv)):
                c0 = part * DFF + jj * 128
                P.op("sp", lambda e, part=part, c0=c0: e.dma_start(out=stg[0:2 * R, part, :], in_=state_conv[l, :, :, c0:c0 + 256].rearrange("i r f -> (i r) f")), writes=["F0"], dma="ld_stg")
                pt, pres = nextpm()
                mm_group(pt[0:R, 0:256], [(hT[:, k, 0:R], wt[:, k, :]) for k in range(8)], [wres, "hT"], pres)
                P.op("act", lambda e, pt=pt, part=part: e.activation(upst[0:R, part, :], pt[0:R, 0:256], AF.Copy), reads=[pres], writes=["F2"])
                P.op("sp", lambda e, part=part, c0=c0: e.dma_start(out=nconv_s[l, :, 1, c0:c0 + 256], in_=upst[0:R, part, :]), reads=["F2"], dma="st_up")
            for jo in range(2):
                j = jj + jo
                cfin = []
                for part, wt in ((0, dg), (1, dv)):
                    ch = j + 22 * part
                    pt, pres = nextpm()
                    mm_group(pt[:, 0:R], [(wt[:, k, jo * 128:(jo + 1) * 128], hT[:, k, 0:R]) for k in range(8)], [wres, "hT"], pres)
                    p2, p2res = nextpm()
                    P.op("pe", lambda e, p2=p2, part=part, jo=jo: e.transpose(p2[:, 0:2 * R], stg[0:2 * R, part, jo * 128:(jo + 1) * 128], cst[0:2 * R, 0, 0:2 * R]),
                         reads=["F0", "cst"], writes=[p2res])
                    cb = Fb[part * 2 + 1]
                    cres = f"F{part * 2 + 1}"
                    st2 = p2[:, 0:2 * R].rearrange("p (i r) -> p i r", r=2)
                    P.op("act", lambda e, cb=cb, pt=pt, ch=ch: e.activation(cb[:, 0:R], pt[:, 0:R], AF.Identity, bias=cvw[:, 3, ch:ch + 1], scale=cvw[:, 2, ch:ch + 1]),
                         reads=[pres, "cvw"], writes=[cres])
                    P.op("dve", lambda e, cb=cb, st2=st2, ch=ch: e.scalar_tensor_tensor(cb[:, 0:R], st2[:, :, 1], cvw[:, 1, ch:ch + 1], cb[:, 0:R], ALU.mult, ALU.add),
                         reads=[p2res, cres, "cvw"], writes=[cres])
                    P.op("dve", lambda e, cb=cb, st2=st2, ch=ch: e.scalar_tensor_tensor(cb[:, 0:R], st2[:, :, 0], cvw[:, 0, ch:ch + 1], cb[:, 0:R], ALU.mult, ALU.add),
                         reads=[p2res, cres, "cvw"], writes=[cres])
                    cfin.append((cb, cres))
                P.op("act", lambda e, c0_=cfin[0][0]: e.activation(Fb[4][:, 0:R], c0_[:, 0:R], AF.Silu), reads=[cfin[0][1]], writes=["F4"])
                P.op("dve", lambda e, c1=cfin[1][0], j=j: e.tensor_tensor(actT[:, j, 0:R], Fb[4][:, 0:R], c1[:, 0:R], ALU.mult), reads=["F4", cfin[1][1]], writes=["actT"])

    convert_layer(0)
    for l in range(DEPTH):
        li = layer_setup(l)
        for b in range(NB):
            for ti in range(NT):
                tile_layer(l, li, b, ti)
                if b == 0 and ti == 0 and l + 1 < DEPTH:
                    convert_layer(l + 1)
        if NS:
            tile_layer(l, li, 0, 0, sample=True)
    print("nrec", P.nrec, {e: len(P.ops[e]) for e in ENGS})
    P.emit()
    st.close()
    return nc


def make_consts():
    c = np.zeros((128, 4, 128), np.float32)
    c[:, 0, :] = np.eye(128, dtype=np.float32)
    s = np.arange(128)[:, None]
    t = np.arange(128)[None, :]
    c[:, 1, :] = (t <= s).astype(np.float32)
    c[:, 2, :] = (s <= t).astype(np.float32)
    for m in range(4):
        win = 2 ** (m + 1)
        for p in range(16):
            c[:, 3, m * 16 + p] = 1.0 / min(p + 1, win)
    c[:, 3, 64] = (np.arange(128) % 64).astype(np.float32)
    return c


_W_NAMES = ['norm1_g', 'w_in', 'a_vnorm_g', 'a_ws', 'a_bs', 'b_qnorm_g', 'b_knorm_g', 'b_lq1', 'b_lk1', 'b_lq2', 'b_lk2',
            'b_subln_g', 'c_w', 'c_scale', 'p_a', 'p_b', 'p_c', 'w_o', 'norm2_g', 'f_up', 'f_conv_w', 'f_conv_b', 'f_down']


def kernel(**inputs):
    f32 = np.float32
    x_prompt = np.ascontiguousarray(np.asarray(inputs['x_prompt'], dtype=f32))
    B, SEQ, _ = x_prompt.shape
    x_sample = np.asarray(inputs['x_sample'], dtype=f32)
    n_dec = x_sample.shape[0]
    page_table = np.asarray(inputs['page_table']).astype(np.int32)
    n_cores = B
    NS = n_dec // n_cores
    with_samples = not os.environ.get("KNOSAMPLE")
    cache_k = np.asarray(inputs['cache_k'], dtype=f32)
    cache_v = np.asarray(inputs['cache_v'], dtype=f32)
    n_pool = cache_k.shape[1]
    npg = page_table.shape[1]
    cfg = Cfg(NB=1, SEQ=SEQ, NS=(NS if with_samples else 0), NPOOL=n_pool, NPG=npg, PGRP=8)
    nc = build(cfg)
    consts = make_consts()
    weights = {k: np.ascontiguousarray(np.asarray(inputs[k], dtype=f32)) for k in _W_NAMES}
    state_pool = np.asarray(inputs['state_pool'], dtype=f32)
    state_conv = np.asarray(inputs['state_conv'], dtype=f32)
    kv_l = [np.concatenate([cache_k[l].reshape(n_pool * PAGE, 512), cache_v[l].reshape(n_pool * PAGE, 512)], axis=1).reshape(n_pool * 64, 2048)
            for l in range(DEPTH)] if with_samples else None
    in_maps = []
    for c in range(n_cores):
        m = dict(weights)
        m['x_prompt'] = x_prompt[c:c + 1]
        m['consts'] = consts
        if with_samples:
            sl = slice(c * NS, (c + 1) * NS)
            m['x_sample'] = np.ascontiguousarray(x_sample[sl, 0, :])
            m['page_table'] = np.ascontiguousarray(page_table[sl])
            m['state_pool'] = np.ascontiguousarray(state_pool[:, sl])
            m['state_conv'] = np.ascontiguousarray(state_conv[:, sl])
            for l in range(DEPTH):
                m[f'cache_kv{l}'] = kv_l[l]
        in_maps.append(m)
    res = run_bass_kernel_spmd(nc, in_maps, core_ids=list(range(n_cores))).results
    y_p = np.concatenate([r['y_prompt'].reshape(1, SEQ, D) for r in res], axis=0)
    nk = np.concatenate([r['nk_p'].reshape(DEPTH, 1, SEQ, NH, 128) for r in res], axis=1)
    nv = np.concatenate([r['nv_p'].reshape(DEPTH, 1, SEQ, NH, 128) for r in res], axis=1)
    npool = np.concatenate([r['npool_p'].reshape(DEPTH, 1, 15, WC) for r in res], axis=1)
    nconv = np.concatenate([r['nconv_p'].reshape(DEPTH, 1, 2, 2 * DFF) for r in res], axis=1)
    if with_samples:
        y_s = np.concatenate([r['y_sample'].reshape(NS, 1, D) for r in res], axis=0)
        nk_s = np.concatenate([r['nk_s'].reshape(DEPTH, NS, 1, NH, 128) for r in res], axis=1)
        nv_s = np.concatenate([r['nv_s'].reshape(DEPTH, NS, 1, NH, 128) for r in res], axis=1)
        ncv_s = np.concatenate([r['ncv_s'].reshape(DEPTH, NS, 1, WA) for r in res], axis=1)
        npool_s = np.concatenate([r['npool_s'].reshape(DEPTH, NS, 15, WC) for r in res], axis=1)
        nconv_s = np.concatenate([r['nconv_s'].reshape(DEPTH, NS, 2, 2 * DFF) for r in res], axis=1)
    else:
        y_s = np.zeros((n_dec, 1, D), f32)
        nk_s = np.zeros((DEPTH, n_dec, 1, NH, 128), f32)
        nv_s = np.zeros((DEPTH, n_dec, 1, NH, 128), f32)
        ncv_s = np.zeros((DEPTH, n_dec, 1, WA), f32)
        npool_s = np.zeros((DEPTH, n_dec, 15, WC), f32)
        nconv_s = np.zeros((DEPTH, n_dec, 2, 2 * DFF), f32)
    return (y_p.astype(f32), y_s.astype(f32), nk.astype(f32), nv.astype(f32), nk_s.astype(f32), nv_s.astype(f32), ncv_s.astype(f32),
            npool.astype(f32), npool_s.astype(f32), nconv.astype(f32), nconv_s.astype(f32))
```
